# Optimizing a Trainium2 kernel written in Bass

```python
import math
import jax, jax.numpy as jnp
from jax import lax
import numpy as np

D_MODEL = 1024
BATCH = 4
SEQ = 8192
DEPTH = 2

N_EVEN = (DEPTH + 1) // 2
N_ODD = DEPTH // 2
EPS = 1e-6

MLA_HEADS = 16
QK_NOPE = 64
QK_ROPE = 32
V_DIM = 64
Q_LORA = 384
KV_LORA = 256
MLA_W = MLA_HEADS * V_DIM
ROPE_THETA = 10000.0
Q_BLOCK = 128

CONV_W = 1024
CONV_K = 31

SSD_HEAD_DIM = 64
SSD_HEADS = 24
SSD_W = SSD_HEADS * SSD_HEAD_DIM
SSD_GROUPS = 4
SSD_HPG = SSD_HEADS // SSD_GROUPS
SSD_STATE = 128
SSD_CONV_K = 5
SSD_CHUNK = 128
XBC_W = SSD_W + 2 * SSD_GROUPS * SSD_STATE

SC_W = 512
SC_K = 3

E_IN = Q_LORA + KV_LORA + QK_ROPE + MLA_W + 3 * CONV_W
O_IN = SSD_W + XBC_W + 2 * SSD_HEADS + 4 * SC_W

kernel_name = "hybrid_mla_conformer_ssd_shortconv_encoder"


def _offsets(*widths):
    out, acc = [], 0
    for w in widths[:-1]:
        acc += w
        out.append(acc)
    return out


def rms_norm(x, g):
    xf = x.astype(jnp.float32)
    y = xf * lax.rsqrt(jnp.mean(xf * xf, axis=-1, keepdims=True) + EPS)
    return (y * g.astype(jnp.float32)).astype(x.dtype)


def layer_norm(x, g, b):
    xf = x.astype(jnp.float32)
    xc = xf - jnp.mean(xf, axis=-1, keepdims=True)
    var = jnp.mean(xc * xc, axis=-1, keepdims=True)
    return (xc * lax.rsqrt(var + EPS) * g.astype(jnp.float32) + b.astype(jnp.float32)).astype(x.dtype)


def dw_conv(u, w):
    k = w.shape[0]
    return lax.conv_general_dilated(
        u, w[:, None, :].astype(u.dtype), window_strides=(1,), padding=[(k // 2, k // 2)],
        dimension_numbers=("NWC", "WIO", "NWC"), feature_group_count=u.shape[-1])


def rope_tables(seq):
    half = QK_ROPE // 2
    inv = ROPE_THETA ** (-jnp.arange(half, dtype=jnp.float32) / half)
    ang = jnp.arange(seq, dtype=jnp.float32)[:, None] * inv[None, :]
    return jnp.cos(ang), jnp.sin(ang)


def apply_rope(x, cos, sin):
    x1, x2 = jnp.split(x, 2, axis=-1)
    cos = cos.astype(x.dtype)
    sin = sin.astype(x.dtype)
    return jnp.concatenate([x1 * cos - x2 * sin, x2 * cos + x1 * sin], axis=-1)


def mla(q_c, kv_c, k_r, w_uq, q_g, w_ukv, kv_g, cos, sin):
    b, s, _ = q_c.shape
    q = (rms_norm(q_c, q_g) @ w_uq).reshape(b, s, MLA_HEADS, QK_NOPE + QK_ROPE)
    q_nope = q[..., :QK_NOPE]
    q_rope = apply_rope(q[..., QK_NOPE:], cos[None, :, None, :], sin[None, :, None, :])
    kv = (rms_norm(kv_c, kv_g) @ w_ukv).reshape(b, s, MLA_HEADS, QK_NOPE + V_DIM)
    k_nope, v = kv[..., :QK_NOPE], kv[..., QK_NOPE:]
    k_rope = apply_rope(k_r, cos[None], sin[None])
    scale = (QK_NOPE + QK_ROPE) ** -0.5
    nb = s // Q_BLOCK

    def to_blocks(t):
        return jnp.moveaxis(t.reshape(b, nb, Q_BLOCK, *t.shape[2:]), 1, 0)

    def attend(blk):
        qn, qr = blk
        sc = jnp.einsum("bqhd,bkhd->bhqk", qn, k_nope) + jnp.einsum("bqhr,bkr->bhqk", qr, k_rope)
        p = jax.nn.softmax(sc.astype(jnp.float32) * scale, axis=-1).astype(v.dtype)
        return jnp.einsum("bhqk,bkhv->bqhv", p, v)

    o = lax.map(attend, (to_blocks(q_nope), to_blocks(q_rope)))
    return jnp.moveaxis(o, 0, 1).reshape(b, s, MLA_W)


def conformer_conv(a, g, w, bias, ln_g, ln_b):
    u = a * jax.nn.sigmoid(g)
    u = dw_conv(u, w) + bias.astype(u.dtype)
    return jax.nn.silu(layer_norm(u, ln_g, ln_b))


def ssd_scan(xs, dt, a, bm, cm):
    b, s, g, r, p = xs.shape
    l = SSD_CHUNK
    c = s // l
    X = (xs * dt[..., None].astype(xs.dtype)).reshape(b, c, l, g, r, p)
    Bc = bm.reshape(b, c, l, g, -1)
    Cc = cm.reshape(b, c, l, g, -1)
    dA = (dt.astype(jnp.float32) * a.astype(jnp.float32)).reshape(b, c, l, g, r)
    a_cs = jnp.cumsum(jnp.moveaxis(dA, 2, -1), axis=-1)
    tri = jnp.tril(jnp.ones((l, l), dtype=bool))
    L = jnp.exp(jnp.where(tri, a_cs[..., :, None] - a_cs[..., None, :], -jnp.inf))
    cb = jnp.einsum("bclgn,bcsgn->bcgls", Cc, Bc)
    M = (cb[:, :, :, None].astype(jnp.float32) * L).astype(X.dtype)
    y_diag = jnp.einsum("bcgrls,bcsgrp->bclgrp", M, X)
    decay_states = jnp.exp(a_cs[..., -1:] - a_cs).astype(X.dtype)
    states = jnp.einsum("bclgn,bcgrl,bclgrp->bcgrpn", Bc, decay_states, X)
    chunk_cs = jnp.cumsum(jnp.pad(a_cs[..., -1], ((0, 0), (1, 0), (0, 0), (0, 0))), axis=1)
    ctri = jnp.tril(jnp.ones((c + 1, c + 1), dtype=bool))[None, :, :, None, None]
    decay_chunk = jnp.exp(jnp.where(ctri, chunk_cs[:, :, None] - chunk_cs[:, None, :], -jnp.inf)).astype(X.dtype)
    states = jnp.pad(states, ((0, 0), (1, 0), (0, 0), (0, 0), (0, 0), (0, 0)))
    entering = jnp.einsum("bzcgr,bcgrpn->bzgrpn", decay_chunk, states)[:, :-1]
    y_off = jnp.einsum("bclgn,bcgrpn,bcgrl->bclgrp", Cc, entering, jnp.exp(a_cs).astype(X.dtype))
    return (y_diag + y_off).reshape(b, s, g, r, p)


def ssd_mixer(z, xbc, dt_f_raw, dt_b_raw, conv_w, conv_b, dt_bias_f, dt_bias_b,
              a_log_f, a_log_b, d_skip, norm_g):
    b, s, _ = xbc.shape
    gn = SSD_GROUPS * SSD_STATE
    xbc = jax.nn.silu(dw_conv(xbc, conv_w) + conv_b.astype(xbc.dtype))
    xs = xbc[..., :SSD_W].reshape(b, s, SSD_GROUPS, SSD_HPG, SSD_HEAD_DIM)
    bm = xbc[..., SSD_W:SSD_W + gn].reshape(b, s, SSD_GROUPS, SSD_STATE)
    cm = xbc[..., SSD_W + gn:].reshape(b, s, SSD_GROUPS, SSD_STATE)
    dt_f = jax.nn.softplus(dt_f_raw + dt_bias_f).reshape(b, s, SSD_GROUPS, SSD_HPG)
    dt_b = jax.nn.softplus(dt_b_raw + dt_bias_b).reshape(b, s, SSD_GROUPS, SSD_HPG)
    a_f = -jnp.exp(a_log_f).reshape(SSD_GROUPS, SSD_HPG)
    a_b = -jnp.exp(a_log_b).reshape(SSD_GROUPS, SSD_HPG)
    flip = lambda t: jnp.flip(t, axis=1)
    y_fwd = ssd_scan(xs, dt_f, a_f, bm, cm)
    y_bwd = flip(ssd_scan(flip(xs), flip(dt_b), a_b, flip(bm), flip(cm)))
    y = y_fwd + y_bwd + d_skip.reshape(SSD_GROUPS, SSD_HPG, 1).astype(xs.dtype) * xs
    y = y.reshape(b, s, SSD_W) * jax.nn.silu(z)
    gw = SSD_W // SSD_GROUPS
    y = rms_norm(y.reshape(b, s, SSD_GROUPS, gw), norm_g.reshape(SSD_GROUPS, gw))
    return y.reshape(b, s, SSD_W)


def short_conv(g_b, g_c, h, w):
    return g_b * dw_conv(g_c * h, w)


def even_layer(x, norm_g, w_in, w_uq, q_g, w_ukv, kv_g, conv_w, conv_b, ln_g, ln_b, w_out, cos, sin):
    u = rms_norm(x, norm_g) @ w_in
    q_c, kv_c, k_r, z_b, glu_a, glu_g, z_a = jnp.split(
        u, _offsets(Q_LORA, KV_LORA, QK_ROPE, MLA_W, CONV_W, CONV_W, CONV_W), axis=-1)
    o_b = mla(q_c, kv_c, k_r, w_uq, q_g, w_ukv, kv_g, cos, sin) * jax.nn.silu(z_b)
    o_a = conformer_conv(glu_a, glu_g, conv_w, conv_b, ln_g, ln_b) * jax.nn.silu(z_a)
    return x + (jnp.concatenate([o_b, o_a], axis=-1) @ w_out).astype(x.dtype)


def odd_layer(x, norm_g, w_in, conv_c_w, conv_c_b, dt_bias_f, dt_bias_b, a_log_f, a_log_b,
              d_skip, ssd_g, conv_d_w, w_out):
    u = rms_norm(x, norm_g) @ w_in
    z_c, xbc, dt_f, dt_b, g_b, g_c, h_d, z_d = jnp.split(
        u, _offsets(SSD_W, XBC_W, SSD_HEADS, SSD_HEADS, SC_W, SC_W, SC_W, SC_W), axis=-1)
    o_c = ssd_mixer(z_c, xbc, dt_f, dt_b, conv_c_w, conv_c_b, dt_bias_f, dt_bias_b,
                    a_log_f, a_log_b, d_skip, ssd_g)
    o_d = short_conv(g_b, g_c, h_d, conv_d_w) * jax.nn.silu(z_d)
    return x + (jnp.concatenate([o_c, o_d], axis=-1) @ w_out).astype(x.dtype)


def setup_inputs(seed: int = 0) -> dict:
    key = jax.random.key(seed)
    ks = iter(jax.random.split(key, 32))
    nrm = lambda shape, scale: jax.random.normal(next(ks), shape, jnp.float32) * scale
    gain = lambda shape: 1.0 + 0.05 * jax.random.normal(next(ks), shape, jnp.float32)
    NE, NO = N_EVEN, N_ODD

    def dt_bias(shape):
        dt = jnp.exp(jax.random.uniform(next(ks), shape, jnp.float32,
                                        minval=math.log(1e-3), maxval=math.log(1e-1)))
        return dt + jnp.log(-jnp.expm1(-dt))

    def a_log(shape):
        return jnp.log(jax.random.uniform(next(ks), shape, jnp.float32, minval=1.0, maxval=16.0))

    return {
        "x": nrm((BATCH, SEQ, D_MODEL), 1.0),
        "norm_e": gain((NE, D_MODEL)),
        "w_in_e": nrm((NE, D_MODEL, E_IN), D_MODEL ** -0.5),
        "w_uq": nrm((NE, Q_LORA, MLA_HEADS * (QK_NOPE + QK_ROPE)), Q_LORA ** -0.5),
        "q_norm": gain((NE, Q_LORA)),
        "w_ukv": nrm((NE, KV_LORA, MLA_HEADS * (QK_NOPE + V_DIM)), KV_LORA ** -0.5),
        "kv_norm": gain((NE, KV_LORA)),
        "conv_a_w": nrm((NE, CONV_K, CONV_W), CONV_K ** -0.5),
        "conv_a_b": nrm((NE, CONV_W), 0.02),
        "ln_a_g": gain((NE, CONV_W)),
        "ln_a_b": nrm((NE, CONV_W), 0.02),
        "w_out_e": nrm((NE, MLA_W + CONV_W, D_MODEL), (MLA_W + CONV_W) ** -0.5),
        "norm_o": gain((NO, D_MODEL)),
        "w_in_o": nrm((NO, D_MODEL, O_IN), D_MODEL ** -0.5),
        "conv_c_w": nrm((NO, SSD_CONV_K, XBC_W), SSD_CONV_K ** -0.5),
        "conv_c_b": nrm((NO, XBC_W), 0.02),
        "dt_bias_f": dt_bias((NO, SSD_HEADS)),
        "dt_bias_b": dt_bias((NO, SSD_HEADS)),
        "a_log_f": a_log((NO, SSD_HEADS)),
        "a_log_b": a_log((NO, SSD_HEADS)),
        "d_skip": gain((NO, SSD_HEADS)),
        "ssd_norm": gain((NO, SSD_W)),
        "conv_d_w": nrm((NO, SC_K, SC_W), SC_K ** -0.5),
        "w_out_o": nrm((NO, SSD_W + SC_W, D_MODEL), (SSD_W + SC_W) ** -0.5),
        "final_norm": gain((D_MODEL,)),
    }


def reference(x, norm_e, w_in_e, w_uq, q_norm, w_ukv, kv_norm, conv_a_w, conv_a_b, ln_a_g, ln_a_b,
              w_out_e, norm_o, w_in_o, conv_c_w, conv_c_b, dt_bias_f, dt_bias_b, a_log_f, a_log_b,
              d_skip, ssd_norm, conv_d_w, w_out_o, final_norm):
    cos, sin = rope_tables(x.shape[1])
    for i in range(DEPTH):
        j = i // 2
        if i % 2 == 0:
            x = even_layer(x, norm_e[j], w_in_e[j], w_uq[j], q_norm[j], w_ukv[j], kv_norm[j],
                           conv_a_w[j], conv_a_b[j], ln_a_g[j], ln_a_b[j], w_out_e[j], cos, sin)
        else:
            x = odd_layer(x, norm_o[j], w_in_o[j], conv_c_w[j], conv_c_b[j], dt_bias_f[j], dt_bias_b[j],
                          a_log_f[j], a_log_b[j], d_skip[j], ssd_norm[j], conv_d_w[j], w_out_o[j])
    return rms_norm(x, final_norm)
```

```python
import numpy as np
from contextlib import ExitStack
import concourse.bass as bass
import concourse.mybir as mybir
from concourse.bass_utils import run_bass_kernel_spmd

F32 = mybir.dt.float32
BF16 = mybir.dt.bfloat16
AF = mybir.ActivationFunctionType
ALU = mybir.AluOpType
AX = mybir.AxisListType

NDMASEM = 40


class Op:
    __slots__ = ("id", "eng", "fn", "deps", "dom", "sig", "sigval", "dma", "inc")


class Prog:
    ENGS = ["pe", "act", "dve", "pool", "sp"]

    def __init__(self, nc):
        self.nc = nc
        self.ops = []
        self.lastw = {}
        self.readers = {}
        self.ndma = 0
        self.ndma_eng = {}
        self.slot_last = {}
        self.last_on_eng = {}

    def add(self, eng, fn, r=(), w=(), dma=False, extra_deps=(), inc=None):
        op = Op()
        op.id = len(self.ops)
        op.eng = eng
        op.fn = fn
        op.dma = dma
        op.sig = False
        op.inc = inc if inc is not None else (16 if dma else 1)
        op.sigval = 0
        deps = set(extra_deps)
        for k in r:
            lw = self.lastw.get(k)
            if lw is not None:
                deps.add(lw)
        for k in w:
            lw = self.lastw.get(k)
            if lw is not None:
                deps.add(lw)
            for rd in self.readers.get(k, ()):
                deps.add(rd)
        if dma:
            if inc is not None and inc != 16:
                slot = ("cc", self.ndma)
                self.ndma += 1
            else:
                n = self.ndma_eng.get(eng, 0)
                self.ndma_eng[eng] = n + 1
                slot = (eng, n % (24 if eng == "sp" else 12))
            op.dom = ("dma", slot)
            prev = self.slot_last.get(slot)
            if prev is not None:
                deps.add(prev)
            self.slot_last[slot] = op.id
        else:
            op.dom = eng
        deps.discard(op.id)
        if eng == "pe" and not dma:
            deps = {d for d in deps if self.ops[d].dma or self.ops[d].eng != "pe"}
        op.deps = deps
        for k in r:
            self.readers.setdefault(k, []).append(op.id)
        for k in w:
            self.lastw[k] = op.id
            self.readers[k] = []
        self.ops.append(op)
        if fn is not None:
            self.last_on_eng[eng] = op.id
        return op.id

    def barrier(self):
        lasts = [v for v in self.last_on_eng.values()]
        lasts += list(self.slot_last.values())
        for e in self.ENGS:
            self.add(e, None, extra_deps=lasts)

    def emit(self):
        nc = self.nc
        for op in self.ops:
            for d in op.deps:
                self.ops[d].sig = True
        cnt = {}
        for op in self.ops:
            if op.sig:
                cnt[op.dom] = cnt.get(op.dom, 0) + op.inc
                op.sigval = cnt[op.dom]
        doms = sorted(cnt.keys(), key=str)
        with ExitStack() as st:
            sems = {}
            for i, d in enumerate(doms):
                sems[d] = st.enter_context(nc.semaphore("s%d" % i))
            block = st.enter_context(nc.Block())
            ops = self.ops

            def run(engname, e):
                known = {}
                for op in ops:
                    if op.eng != engname:
                        continue
                    need = {}
                    for d in op.deps:
                        o = ops[d]
                        if need.get(o.dom, 0) < o.sigval:
                            need[o.dom] = o.sigval
                    for dom, v in need.items():
                        if known.get(dom, 0) < v:
                            e.wait_ge(sems[dom], v)
                            known[dom] = v
                    if op.fn is not None:
                        ins = op.fn(e)
                        if op.sig:
                            ins.then_inc(sems[op.dom], op.inc)
                    elif op.sig:
                        raise RuntimeError("fn None op cannot signal")

            @block.sync
            def _(e):
                run("sp", e)

            @block.tensor
            def _(e):
                run("pe", e)

            @block.scalar
            def _(e):
                run("act", e)

            @block.vector
            def _(e):
                run("dve", e)

            @block.gpsimd
            def _(e):
                run("pool", e)


U8 = mybir.dt.uint8
ARENA_BYTES = 206 * 1024


class Arena:
    def __init__(self, nc, nbytes=ARENA_BYTES):
        self.ap = nc.alloc_sbuf_tensor("arena", [128, nbytes], U8).ap()
        self.nbytes = nbytes

    def view(self, off, shape, dt, p0=0):
        esz = 4 if dt == F32 else 2
        n = 1
        for s in shape[1:]:
            n *= s
        assert off % 4 == 0 and off + n * esz <= self.nbytes, (off, n, esz)
        v = self.ap[p0:p0 + shape[0], off:off + n * esz].bitcast(dt)
        if len(shape) == 3:
            v = v.rearrange("p (a b) -> p a b", a=shape[1])
        elif len(shape) == 4:
            v = v.rearrange("p (a b c) -> p a b c", a=shape[1], b=shape[2])
        return v


EPS = 1e-6
D = 1024
E_IN = 4768
NV_E = 8 + 8 + 8 + 8 + 3 + 2 + 8 * 31


class Bump:
    def __init__(self, A, start=0):
        self.A = A
        self.off = start

    def get(self, shape, dt, p0=0):
        esz = 4 if dt == F32 else 2
        n = 1
        for s in shape[1:]:
            n *= s
        nb = (n * esz + 31) // 32 * 32
        v = self.A.view(self.off, shape, dt, p0=p0)
        self.off += nb
        return v


def rmsnorm_tile(P, pfx, x3, nk, N, gcol, ones_bf, sq3, ss_ps, rstd, out3, inv_n, x_keys, out_key, x_is_psum=False, ssk=None, epsc=None):
    ssk = ssk if ssk is not None else (pfx, "ss")
    for k in range(nk):
        xin = x3[k] if x_is_psum else x3[:, k, :]
        P.add("act", lambda e, o=sq3[:, k, :], i=xin: e.activation(o, i, AF.Square),
              r=[x_keys[k]], w=[(pfx, "sq", k)])
    for k in range(nk):
        P.add("pe", lambda e, k=k: e.matmul(ss_ps, ones_bf, sq3[:, k, :], start=(k == 0), stop=(k == nk - 1)),
              r=[(pfx, "sq", k), "ones"], w=[ssk])
    P.add("act", lambda e: e.activation(rstd, ss_ps, AF.Ln, bias=epsc, scale=inv_n),
          r=[ssk, "epsc"], w=[(pfx, "rstd")])
    P.add("act", lambda e: e.activation(rstd, rstd, AF.Exp, scale=-0.5),
          r=[(pfx, "rstd")], w=[(pfx, "rstd")])
    for k in range(nk):
        xin = x3[k] if x_is_psum else x3[:, k, :]
        eng = "dve"
        P.add(eng, lambda e, o=out3[:, k, :], i=xin, g=gcol(k): e.scalar_tensor_tensor(
            o, i, g, rstd, ALU.mult, ALU.mult),
            r=[x_keys[k], (pfx, "rstd"), "vecs"], w=[(out_key, k)])


def build_l0(nc, P, A, PS, S, dr):
    T = S // 2
    NT2 = 2 * T // 512
    TT = 256
    NTT = T // TT
    HW = 15
    TW = TT + 2 * HW
    QT = 512
    NQT = T // QT
    NKT = 2 * T // 128
    xT = dr["xT"]
    xT3 = xT.rearrange("(k p) t -> p k t", p=128)

    B = Bump(A, 0)
    vecs = B.get([128, NV_E], F32)
    ones_bf = B.get([128, 128], BF16)
    ones_f = B.get([128, 64], F32)
    epsc = B.get([128, 1], F32)
    small_end = B.off
    kv_n = B.get([128, 2, 2 * T], BF16)
    kro_off = B.off
    kro = B.get([128, 2 * T], BF16)
    pers_end = B.off
    V_NE, V_CB, V_LG, V_LB, V_QG, V_KG, V_CW = 0, 8, 16, 24, 32, 35, 37

    P.add("sp", lambda e: e.dma_start(out=vecs, in_=dr["vecs"]), w=["vecs"], dma=True)
    P.add("dve", lambda e: e.memset(ones_bf, 1.0), w=["ones"])
    P.add("dve", lambda e: e.memset(ones_f, 1.0), w=["ones_f"])
    P.add("dve", lambda e: e.memset(epsc, EPS), w=["epsc"])

    B = Bump(A, pers_end)
    WINC = E_IN + 96
    w_in_bf = B.get([128, 8, WINC], BF16)
    ph12_base = B.off
    stg = [B.get([128, E_IN], F32) for _ in range(2)]
    w_in3 = dr["w_in"].rearrange("(k p) c -> p k c", p=128)
    for k in range(8):
        s = stg[k % 2]
        P.add("sp", lambda e, s=s, k=k: e.dma_start(out=s, in_=w_in3[:, k, :]), w=[("stg", k % 2)], dma=True)
        eng = ["dve", "pool"][k % 2]
        P.add(eng, lambda e, s=s, k=k: e.tensor_copy(w_in_bf[:, k, 0:E_IN], s), r=[("stg", k % 2)], w=[("winbf", k)])
        P.add(eng, lambda e, s=s, k=k: e.tensor_copy(w_in_bf[:, k, E_IN:E_IN + 64], s[:, 576:640]),
              r=[("stg", k % 2)], w=[("winbf", k)])
        P.add(eng, lambda e, s=s, k=k: e.tensor_copy(w_in_bf[:, k, E_IN + 64:E_IN + 80], s[:, 656:672]),
              r=[("stg", k % 2)], w=[("winbf", k)])
        P.add(eng, lambda e, s=s, k=k: e.tensor_copy(w_in_bf[:, k, E_IN + 80:E_IN + 96], s[:, 640:656]),
              r=[("stg", k % 2)], w=[("winbf", k)])
    winkeys = [("winbf", k) for k in range(8)]
    idf = B.get([128, 128], F32)
    idb = B.get([128, 128], BF16)
    dgs = [B.get([128, 31, 128], BF16) for _ in range(2)]
    assert B.off <= A.nbytes, B.off
    P.add("sp", lambda e: e.dma_start(out=idf, in_=dr["consts"][:, 512:640]), w=["idf"], dma=True)
    P.add("dve", lambda e: e.tensor_copy(idb, idf), r=["idf"], w=["idb"])
    for c in range(8):
        d = dgs[c % 2]
        for i in range(31):
            eng = "dve" if i % 2 == 0 else "pool"
            P.add(eng, lambda e, d=d, c=c, i=i: e.tensor_scalar(d[:, i, :], idb, vecs[:, V_CW + c * 31 + i:V_CW + c * 31 + i + 1],
                                                               None, ALU.mult), r=["idb", "vecs"], w=[("dgs", c % 2, i % 2)])
        P.add("sp", lambda e, d=d, c=c: e.dma_start(out=dr["dg"][c].rearrange("p (i m) -> p i m", i=31), in_=d),
              r=[("dgs", c % 2, 0), ("dgs", c % 2, 1)], w=[("dg", c)], dma=True)

    P.barrier()
    B = Bump(A, ph12_base)
    xt = [B.get([128, 8, 512], F32) for _ in range(2)]
    sq = B.get([128, 8, 512], BF16)
    xn = B.get([128, 8, 512], BF16)
    rstd = B.get([128, 512], F32)
    rstd2 = B.get([128, 512], F32)
    sq2 = B.get([128, 2, 512], BF16)
    cs = B.get([32, 512], F32, p0=64)
    sn = B.get([32, 512], F32, p0=64)
    tmp1 = B.get([32, 512], F32, p0=64)
    tmp2 = B.get([32, 512], F32, p0=64)
    ss_ps, kv0_ps, kv1_ps, kr_ps, krr_ps, ss2_ps = PS[0], PS[1], PS[2], PS[3], PS[4], PS[5]
    for j in range(NT2):
        t0 = 512 * j
        xb = xt[j % 2]
        P.add("sp", lambda e, xb=xb, t0=t0: e.dma_start(out=xb, in_=xT3[:, :, 16 + t0:16 + t0 + 512]),
              w=[("xt", j % 2, k) for k in range(8)], dma=True)
        P.add("sp", lambda e, t0=t0: e.dma_start(out=cs, in_=dr["rope"][0, :, t0:t0 + 512]), w=["cs"], dma=True)
        P.add("sp", lambda e, t0=t0: e.dma_start(out=sn, in_=dr["rope"][1, :, t0:t0 + 512]), w=["sn"], dma=True)
        rmsnorm_tile(P, "p1", xb, 8, 512, lambda k: vecs[:, V_NE + k:V_NE + k + 1], ones_bf, sq, ss_ps, rstd, xn,
                     1.0 / D, [("xt", j % 2, k) for k in range(8)], "xn", ssk=("PS", 0), epsc=epsc)
        outs = [(kv0_ps, 384, 128, ("PS", 1)), (kv1_ps, 512, 128, ("PS", 2)), (kr_ps[0:96, :], 576, 96, ("PS", 3)),
                (krr_ps[0:96, :], E_IN, 96, ("PS", 4))]
        for (ps, c0, m, nm) in outs:
            for k in range(8):
                P.add("pe", lambda e, ps=ps, c0=c0, m=m, k=k: e.matmul(
                    ps, w_in_bf[:, k, c0:c0 + m], xn[:, k, :], start=(k == 0), stop=(k == 7)),
                    r=[("xn", k), ("winbf", k)], w=[nm])
        rmsnorm_tile(P, "p1b", [kv0_ps, kv1_ps], 2, 512, lambda k: vecs[:, V_KG + k:V_KG + k + 1], ones_bf, sq2,
                     ss2_ps, rstd2, kv_n[:, :, t0:t0 + 512], 1.0 / 256, [("PS", 1), ("PS", 2)], ("kvn", j), x_is_psum=True,
                     ssk=("PS", 5), epsc=epsc)
        P.add("dve", lambda e: e.tensor_tensor(tmp1, kr_ps[64:96, :], cs, ALU.mult), r=[("PS", 3), "cs"], w=["tmp1"])
        P.add("dve", lambda e: e.tensor_tensor(tmp2, krr_ps[64:96, :], sn, ALU.mult), r=[("PS", 4), "sn"], w=["tmp2"])
        P.add("dve", lambda e, t0=t0: e.tensor_tensor(kro[64:96, t0:t0 + 512], tmp1, tmp2, ALU.add),
              r=["tmp1", "tmp2"], w=[("kro", j)])

    P.barrier()
    B = Bump(A, ph12_base)
    xt_p2 = [B.get([128, 8, TW], F32)] * 2
    sq_p2 = B.get([128, 8, TW], BF16)
    xn_p2 = B.get([128, 8, TW], BF16)
    rstd_p2 = B.get([128, TW], F32)
    rstdq = B.get([128, TT], F32)
    sqq = B.get([128, 3, TT], BF16)
    qn_t = B.get([128, 3, TT], BF16)
    th = B.get([128, TW], F32)
    gb_t = B.get([128, 8, TT], BF16)
    u = B.get([128, 8, TW], BF16)
    dgb = [B.get([128, 31, 128], BF16) for _ in range(2)]
    ga = B.get([128, TW], F32)
    acc = B.get([128, 8, TT], F32)
    accb = B.get([128, 8, TT], BF16)
    sqa = sq_p2[:, :, 0:TT]
    mean = B.get([128, TT], F32)
    var = B.get([128, TT], F32)
    za = B.get([128, TT], F32)
    oa_t = B.get([128, 8, TT], BF16)
    assert B.off <= A.nbytes, B.off
    for j in range(NTT):
        t0 = TT * j
        xb = xt_p2[j % 2]
        xk = [("xt2", 0, k) for k in range(8)]
        P.add("sp", lambda e, xb=xb, t0=t0: e.dma_start(out=xb, in_=xT3[:, :, 16 + t0 - HW:16 + t0 - HW + TW]),
              w=xk, dma=True)
        rmsnorm_tile(P, "p2", xb, 8, TW, lambda k: vecs[:, V_NE + k:V_NE + k + 1], ones_bf, sq_p2, PS[0][:, 0:TW],
                     rstd_p2, xn_p2, 1.0 / D, xk, "xn2", ssk=("PS", 0), epsc=epsc)
        xnc = xn_p2[:, :, HW:HW + TT]
        for c in range(3):
            for k in range(8):
                P.add("pe", lambda e, c=c, k=k: e.matmul(PS[1 + c][:, 0:TT], w_in_bf[:, k, c * 128:(c + 1) * 128],
                                                        xnc[:, k, :], start=(k == 0), stop=(k == 7)),
                      r=[("xn2", k), ("winbf", k)], w=[("PS", 1 + c)])
        rmsnorm_tile(P, "p2q", [PS[1][:, 0:TT], PS[2][:, 0:TT], PS[3][:, 0:TT]], 3, TT,
                     lambda k: vecs[:, V_QG + k:V_QG + k + 1], ones_bf, sqq, PS[4][:, 0:TT], rstdq, qn_t,
                     1.0 / 384, [("PS", 1 + c) for c in range(3)], "qn_t", x_is_psum=True, ssk=("PS", 4), epsc=epsc)
        P.add("pool", lambda e, t0=t0: e.dma_start(out=dr["qn"].rearrange("(c p) t -> p c t", p=128)[:, :, t0:t0 + TT],
                                                   in_=qn_t),
              r=[("qn_t", c) for c in range(3)], w=[("qn_d", t0 // QT)], dma=True)
        for c in range(8):
            ps = PS[5 + (c % 2)][:, 0:TT]
            pk = ("PS", 5 + c % 2)
            for k in range(8):
                P.add("pe", lambda e, ps=ps, c=c, k=k: e.matmul(ps, w_in_bf[:, k, 672 + c * 128:672 + (c + 1) * 128],
                                                               xnc[:, k, :], start=(k == 0), stop=(k == 7)),
                      r=[("xn2", k), ("winbf", k)], w=[pk])
            P.add("act", lambda e, ps=ps: e.activation(th[:, 0:TT], ps, AF.Tanh, scale=0.5), r=[pk], w=["th"])
            P.add("dve", lambda e, ps=ps, c=c: e.scalar_tensor_tensor(gb_t[:, c, :], th[:, 0:TT], 1.0, ps, ALU.add, ALU.mult),
                  r=["th", pk], w=[("gb_t", c)])
        P.add("pool", lambda e, t0=t0: e.dma_start(out=dr["gb"].rearrange("(c p) t -> p c t", p=128)[:, :, t0:t0 + TT],
                                                   in_=gb_t),
              r=[("gb_t", c) for c in range(8)], w=[("gb_d", t0 // QT)], dma=True)
        for c in range(8):
            psa = PS[5][:, 0:TW]
            psg = PS[6][:, 0:TW]
            for k in range(8):
                P.add("pe", lambda e, c=c, k=k: e.matmul(psa, w_in_bf[:, k, 1696 + c * 128:1696 + (c + 1) * 128],
                                                        xn_p2[:, k, :], start=(k == 0), stop=(k == 7)),
                      r=[("xn2", k), ("winbf", k)], w=[("PS", 5)])
            for k in range(8):
                P.add("pe", lambda e, c=c, k=k: e.matmul(psg, w_in_bf[:, k, 2720 + c * 128:2720 + (c + 1) * 128],
                                                        xn_p2[:, k, :], start=(k == 0), stop=(k == 7)),
                      r=[("xn2", k), ("winbf", k)], w=[("PS", 6)])
            P.add("act", lambda e: e.activation(th, psg, AF.Tanh, scale=0.5), r=[("PS", 6)], w=["th"])
            P.add("act", lambda e: e.mul(ga, psa, 0.5), r=[("PS", 5)], w=["ga"])
            P.add("dve", lambda e, c=c: e.scalar_tensor_tensor(u[:, c, :], th, 1.0, ga, ALU.add, ALU.mult),
                  r=["th", "ga"], w=[("u", c)])
        for c in range(8):
            d = dgb[c % 2]
            P.add("sp", lambda e, d=d, c=c: e.dma_start(out=d, in_=dr["dg"][c].rearrange("p (i m) -> p i m", i=31)),
                  r=[("dg", c)], w=[("dgb", c % 2)], dma=True)
            cps = (PS[7] if c % 2 == 0 else PS[0])[:, 0:TT]
            cpk = ("PS", 7) if c % 2 == 0 else ("PS", 0)
            for i in range(31):
                P.add("pe", lambda e, d=d, c=c, i=i, cps=cps: e.matmul(cps, d[:, i, :], u[:, c, i:i + TT], start=(i == 0), stop=(i == 30)),
                      r=[("dgb", c % 2), ("u", c)], w=[cpk])
            P.add("dve", lambda e, c=c, cps=cps: e.tensor_scalar(acc[:, c, :], cps, vecs[:, V_CB + c:V_CB + c + 1], None, ALU.add),
                  r=[cpk, "vecs"], w=[("acc", c)])
        for c in range(8):
            P.add("act", lambda e, c=c: e.activation(accb[:, c, :], acc[:, c, :], AF.Copy), r=[("acc", c)], w=[("accb", c)])
            P.add("act", lambda e, c=c: e.activation(sqa[:, c, :], acc[:, c, :], AF.Square), r=[("acc", c)], w=[("p2", "sq", c)])
        for c in range(8):
            P.add("pe", lambda e, c=c: e.matmul(PS[1][:, 0:TT], ones_bf, accb[:, c, :], start=(c == 0), stop=(c == 7)),
                  r=[("accb", c), "ones"], w=[("PS", 1)])
        for c in range(8):
            P.add("pe", lambda e, c=c: e.matmul(PS[2][:, 0:TT], ones_bf, sqa[:, c, :], start=(c == 0), stop=(c == 7)),
                  r=[("p2", "sq", c), "ones"], w=[("PS", 2)])
        P.add("dve", lambda e: e.tensor_scalar(mean, PS[1][:, 0:TT], 1.0 / 1024, None, ALU.mult), r=[("PS", 1)], w=["mean"])
        P.add("dve", lambda e: e.tensor_tensor(var, mean, mean, ALU.mult), r=["mean"], w=["var"])
        P.add("dve", lambda e: e.scalar_tensor_tensor(var, PS[2][:, 0:TT], 1.0 / 1024, var, ALU.mult, ALU.subtract),
              r=[("PS", 2), "var"], w=["var"])
        P.add("act", lambda e: e.activation(var, var, AF.Ln, bias=epsc), r=["var", "epsc"], w=["var"])
        P.add("act", lambda e: e.activation(var, var, AF.Exp, scale=-0.5), r=["var"], w=["var"])
        for c in range(8):
            eng = "dve"
            P.add(eng, lambda e, c=c: e.tensor_tensor(acc[:, c, :], acc[:, c, :], mean, ALU.subtract),
                  r=[("acc", c), "mean"], w=[("acc", c)])
            P.add(eng, lambda e, c=c: e.scalar_tensor_tensor(acc[:, c, :], acc[:, c, :], vecs[:, V_LG + c:V_LG + c + 1],
                                                             var, ALU.mult, ALU.mult),
                  r=[("acc", c), "var", "vecs"], w=[("acc", c)])
            P.add(eng, lambda e, c=c: e.tensor_scalar(acc[:, c, :], acc[:, c, :], vecs[:, V_LB + c:V_LB + c + 1], None, ALU.add),
                  r=[("acc", c)], w=[("acc", c)])
            pz = PS[3][:, 0:TT] if c % 2 == 0 else PS[4][:, 0:TT]
            pzk = ("PS", 3) if c % 2 == 0 else ("PS", 4)
            for k in range(8):
                P.add("pe", lambda e, pz=pz, c=c, k=k: e.matmul(pz, w_in_bf[:, k, 3744 + c * 128:3744 + (c + 1) * 128],
                                                               xnc[:, k, :], start=(k == 0), stop=(k == 7)),
                      r=[("xn2", k), ("winbf", k)], w=[pzk])
            P.add("act", lambda e, pz=pz: e.activation(th[:, 0:TT], pz, AF.Tanh, scale=0.5), r=[pzk], w=["th"])
            P.add("dve", lambda e, pz=pz: e.scalar_tensor_tensor(za, th[:, 0:TT], 1.0, pz, ALU.add, ALU.mult),
                  r=["th", pzk], w=["za"])
            P.add("act", lambda e, c=c: e.activation(th[:, 0:TT], acc[:, c, :], AF.Tanh, scale=0.5), r=[("acc", c)], w=["th"])
            P.add("dve", lambda e, c=c: e.scalar_tensor_tensor(mean if False else ga[:, 0:TT], th[:, 0:TT], 1.0, acc[:, c, :], ALU.add, ALU.mult),
                  r=["th", ("acc", c)], w=["ga"])
            P.add("dve", lambda e, c=c: e.scalar_tensor_tensor(oa_t[:, c, :], ga[:, 0:TT], 0.25, za, ALU.mult, ALU.mult),
                  r=["ga", "za"], w=[("oa_t", c)])
        P.add("pool", lambda e, t0=t0: e.dma_start(out=dr["oa"].rearrange("(c p) t -> p c t", p=128)[:, :, t0:t0 + TT],
                                                   in_=oa_t),
              r=[("oa_t", c) for c in range(8)], w=[("oa_d", t0 // QT)], dma=True)
    P.barrier()

    B = Bump(A, pers_end)
    KT = B.get([96, 4, 2 * T], BF16)
    Vg = B.get([128, NKT, 4, 65], BF16)
    wuq = B.get([128, 3, 2 * 1536], BF16)
    wukv = B.get([128, 2, 2048], BF16)
    pT = [B.get([128, 512], BF16) for _ in range(4)]
    QTb = [B.get([96, 512], BF16) for _ in range(2)]
    qn_q = [B.get([128, 3, 512], BF16) for _ in range(2)]
    csq = [B.get([32, 512], F32, p0=64) for _ in range(2)]
    snq = [B.get([32, 512], F32, p0=64) for _ in range(2)]
    stg3 = A.view(B.off - 4 * 2048, [128, 2048], F32)
    t1 = B.get([32, 512], F32, p0=64)
    t2 = B.get([32, 512], F32, p0=64)
    rc = B.get([1, 512], F32, p0=64)
    cst_f = [B.get([128, 516], F32) for _ in range(2)]
    cst_b = [B.get([128, 516], BF16) for _ in range(2)]
    B2 = Bump(A, kro_off) if 4 * T >= 14336 else B
    gb_q = [B2.get([64, 4, 512], BF16) for _ in range(2)]
    rb = B2.get([64, 512], F32)
    of = B2.get([64, 512], F32)
    ob_t = [B2.get([64, 512], BF16) for _ in range(2)]
    assert B2 is B or B2.off <= kro_off + 4 * T, (B2.off, kro_off)
    assert B.off <= A.nbytes, B.off
    wuq3 = dr["w_uq"].rearrange("(c p) n -> p c n", p=128)
    for c in range(3):
        P.add("sp", lambda e, c=c: e.dma_start(out=stg3[:, 0:1536], in_=wuq3[:, c, :]), w=["stg3"], dma=True)
        P.add("dve", lambda e, c=c: e.tensor_copy(wuq[:, c, 0:1536], stg3[:, 0:1536]), r=["stg3"], w=["wuq"])
        P.add("pool", lambda e, c=c: e.tensor_copy(wuq[:, c, 1536:3072], stg3[:, 0:1536]), r=["stg3"], w=["wuqr"])
        s4 = stg3[:, 0:1536].rearrange("p (h d) -> p h d", h=16)
        d4 = wuq[:, c, 1536:3072].rearrange("p (h d) -> p h d", h=16)
        P.add("pool", lambda e, s4=s4, d4=d4: e.tensor_copy(d4[:, :, 64:80], s4[:, :, 80:96]), r=["stg3"], w=["wuqr"])
        P.add("pool", lambda e, s4=s4, d4=d4: e.tensor_copy(d4[:, :, 80:96], s4[:, :, 64:80]), r=["stg3"], w=["wuqr"])
    wukv3 = dr["w_ukv"].rearrange("(c p) n -> p c n", p=128)
    for c in range(2):
        P.add("sp", lambda e, c=c: e.dma_start(out=stg3, in_=wukv3[:, c, :]), w=["stg3"], dma=True)
        P.add("dve", lambda e, c=c: e.tensor_copy(wukv[:, c, :], stg3), r=["stg3"], w=["wukv"])
    P.add("pool", lambda e: e.memset(Vg.rearrange("p k h d -> p (k h) d")[:, :, 64:65], 1.0), w=["Vones"])
    P.barrier()
    cast_jobs = []
    if "w_out_e_bf" in dr:
        for r in range(16):
            for hcol in range(2):
                cast_jobs.append((dr["w_out"][r * 128:(r + 1) * 128, hcol * 512:(hcol + 1) * 512],
                                  dr["w_out_e_bf"][r * 128:(r + 1) * 128, hcol * 512:(hcol + 1) * 512], 512, "w_out_e_bf"))
        for r in range(8):
            for cc in range(12):
                cast_jobs.append((dr["w_in_o"][r * 128:(r + 1) * 128, cc * 516:(cc + 1) * 516],
                                  dr["w_in_o_bf"][r * 128:(r + 1) * 128, cc * 516:(cc + 1) * 516], 516, "w_in_o_bf"))
        for r in range(16):
            for hcol in range(2):
                cast_jobs.append((dr["w_out_o"][r * 128:(r + 1) * 128, hcol * 512:(hcol + 1) * 512],
                                  dr["w_out_o_bf"][r * 128:(r + 1) * 128, hcol * 512:(hcol + 1) * 512], 512, "w_out_o_bf"))
    cast_state = [0]

    def emit_cast(nj):
        for _ in range(nj):
            i = cast_state[0]
            if i >= len(cast_jobs):
                return
            cast_state[0] += 1
            src, dst, w_, nm = cast_jobs[i]
            f_, b_ = cst_f[i % 2], cst_b[i % 2]
            P.add("sp", lambda e, f_=f_, src=src, w_=w_: e.dma_start(out=f_[:, 0:w_], in_=src), w=[("cst_f", i % 2)], dma=True)
            P.add("pool", lambda e, f_=f_, b_=b_, w_=w_: e.tensor_copy(b_[:, 0:w_], f_[:, 0:w_]), r=[("cst_f", i % 2)], w=[("cst_b", i % 2)])
            P.add("sp", lambda e, b_=b_, dst=dst, w_=w_: e.dma_start(out=dst, in_=b_[:, 0:w_]), r=[("cst_b", i % 2)], w=[nm], dma=True)

    scale = 96.0 ** -0.5
    kro_keys = [("kro", j) for j in range(NT2)]
    kvn_keys = [(("kvn", j), k) for j in range(NT2) for k in range(2)]
    for g in range(4):
        for hl in range(4):
            h = 4 * g + hl
            for j in range(NT2):
                ps = PS[j % 2][0:64, :]
                pk = ("PS", j % 2)
                for c in range(2):
                    P.add("pe", lambda e, ps=ps, c=c, h=h, j=j: e.matmul(
                        ps, wukv[:, c, h * 128:h * 128 + 64], kv_n[:, c, 512 * j:512 * j + 512],
                        start=(c == 0), stop=(c == 1)), r=["wukv", (("kvn", j), c)], w=[pk])
                eng = "dve" if j % 2 == 0 else "act"
                if eng == "dve":
                    P.add("dve", lambda e, ps=ps, hl=hl, j=j: e.tensor_copy(KT[0:64, hl, 512 * j:512 * j + 512], ps),
                          r=[pk], w=[("KT", hl, j)])
                else:
                    P.add("act", lambda e, ps=ps, hl=hl, j=j: e.activation(KT[0:64, hl, 512 * j:512 * j + 512], ps, AF.Copy),
                          r=[pk], w=[("KT", hl, j)])
            P.add("pool", lambda e, hl=hl: e.tensor_copy(KT[64:96, hl, :], kro[64:96, :]), r=kro_keys, w=[("KTr", hl)])
        wv = wukv.rearrange("p c (h d) -> p c h d", h=16)
        for kt in range(NKT):
            ps = PS[2 + kt % 2][:, 0:256]
            pk = ("PS", 2 + kt % 2)
            for c in range(2):
                P.add("pe", lambda e, ps=ps, c=c, kt=kt, g=g: e.matmul(
                    ps.rearrange("p (h d) -> p h d", h=4), kv_n[:, c, 128 * kt:128 * kt + 128],
                    wv[:, c, 4 * g:4 * g + 4, 64:128], start=(c == 0), stop=(c == 1)),
                    r=["wukv", (("kvn", kt // 4), c)], w=[pk])
            eng = "dve" if kt % 2 == 0 else "act"
            if eng == "dve":
                P.add("dve", lambda e, ps=ps, kt=kt: e.tensor_copy(Vg[:, kt, :, 0:64], ps.rearrange("p (h d) -> p h d", h=4)),
                      r=[pk], w=[("Vg", kt)])
            else:
                P.add("act", lambda e, ps=ps, kt=kt: e.activation(Vg[:, kt, :, 0:64], ps.rearrange("p (h d) -> p h d", h=4), AF.Copy),
                      r=[pk], w=[("Vg", kt)])
        def emit_loads(it_, g_=g):
            qt_ = it_ % NQT
            q0_ = QT * qt_
            P.add("sp", lambda e, q0_=q0_, it_=it_: e.dma_start(
                out=qn_q[it_ % 2], in_=dr["qn"].rearrange("(c p) t -> p c t", p=128)[:, :, q0_:q0_ + QT]),
                r=[("qn_d", qt_)], w=[("qn_q", it_ % 2)], dma=True)
            P.add("sp", lambda e, q0_=q0_, it_=it_, g_=g_: e.dma_start(
                out=gb_q[it_ % 2], in_=dr["gb"].rearrange("(h d) t -> d h t", d=64)[:, 4 * g_:4 * g_ + 4, q0_:q0_ + QT]),
                r=[("gb_d", qt_)], w=[("gb_q", it_ % 2)], dma=True)
            P.add("sp", lambda e, q0_=q0_, it_=it_: e.dma_start(out=csq[it_ % 2], in_=dr["rope"][0, :, q0_:q0_ + QT]),
                  w=[("csq", it_ % 2)], dma=True)
            P.add("sp", lambda e, q0_=q0_, it_=it_: e.dma_start(out=snq[it_ % 2], in_=dr["rope"][1, :, q0_:q0_ + QT]),
                  w=[("snq", it_ % 2)], dma=True)

        def emit_qproj(it_, hl_, g_=g):
            h_ = 4 * g_ + hl_
            hi_ = it_ * 4 + hl_
            qq_, cq_, sq__ = qn_q[it_ % 2], csq[it_ % 2], snq[it_ % 2]
            qa = PS[0][0:96, :]
            qb = PS[1][0:96, :]
            for (ps, base, nm) in ((qa, 0, ("PS", 0)), (qb, 1536, ("PS", 1))):
                for c in range(3):
                    P.add("pe", lambda e, ps=ps, base=base, c=c, h_=h_, qq_=qq_: e.matmul(
                        ps, wuq[:, c, base + h_ * 96:base + (h_ + 1) * 96], qq_[:, c, :],
                        start=(c == 0), stop=(c == 2)), r=["wuq", "wuqr", ("qn_q", it_ % 2)], w=[nm])
            Qb_ = QTb[hi_ % 2]
            qk_ = ("QT", hi_ % 2)
            P.add("act", lambda e, Qb_=Qb_: e.activation(Qb_[0:64, :], qa[0:64, :], AF.Copy), r=[("PS", 0)], w=[qk_])
            P.add("dve", lambda e, cq_=cq_: e.tensor_tensor(t1, qa[64:96, :], cq_, ALU.mult),
                  r=[("PS", 0), ("csq", it_ % 2)], w=["t1"])
            P.add("dve", lambda e, sq__=sq__: e.tensor_tensor(t2, qb[64:96, :], sq__, ALU.mult),
                  r=[("PS", 1), ("snq", it_ % 2)], w=["t2"])
            P.add("dve", lambda e, Qb_=Qb_: e.tensor_tensor(Qb_[64:96, :], t1, t2, ALU.add), r=["t1", "t2"], w=[qk_])

        def emit_epilogue(it_, hl_, g_=g):
            h_ = 4 * g_ + hl_
            hi_ = it_ * 4 + hl_
            q0_ = QT * (it_ % NQT)
            o_ps_ = PS[2 + hi_ % 2][0:65, :]
            ok_ = ("PS", 2 + hi_ % 2)
            gq_ = gb_q[it_ % 2]
            P.add("dve", lambda e, o_ps_=o_ps_: e.reciprocal(rc, o_ps_[64:65, :]), r=[ok_], w=["rc"])
            P.add("pe", lambda e: e.matmul(PS[7][0:64, :], ones_f[64:65, 0:64], rc, start=True, stop=True),
                  r=["rc", "ones_f"], w=[("PS", 7)])
            P.add("act", lambda e: e.activation(rb, PS[7][0:64, :], AF.Copy), r=[("PS", 7)], w=["rb"])
            P.add("dve", lambda e, o_ps_=o_ps_: e.tensor_tensor(of, o_ps_[0:64, :], rb, ALU.mult), r=[ok_, "rb"], w=["of"])
            obt = ob_t[hi_ % 2]
            P.add("dve", lambda e, obt=obt, gq_=gq_, hl_=hl_: e.scalar_tensor_tensor(
                obt, of, 0.5, gq_[:, hl_, :], ALU.mult, ALU.mult), r=["of", ("gb_q", it_ % 2)], w=[("ob_t", hi_ % 2)])
            P.add("pool", lambda e, obt=obt, h_=h_, q0_=q0_: e.dma_start(out=dr["ob"][h_, :, q0_:q0_ + QT], in_=obt),
                  r=[("ob_t", hi_ % 2)], w=[("ob_d", it_ % NQT)], dma=True)

        items = [(g * NQT + qt, hl) for qt in range(NQT) for hl in range(4)]
        emit_loads(items[0][0])
        emit_qproj(*items[0])
        pending_epi = None
        for n, (it, hl) in enumerate(items):
            hi = it * 4 + hl
            Qb = QTb[hi % 2]
            qk = ("QT", hi % 2)
            o_ps = PS[2 + hi % 2][0:65, :]
            ok = ("PS", 2 + hi % 2)
            def pv(kt, hl=hl, o_ps=o_ps, ok=ok):
                pt = pT[kt % 4]
                P.add("pe", lambda e, kt=kt, pt=pt: e.matmul(
                    o_ps, Vg[:, kt, hl, 0:65], pt, start=(kt == 0), stop=(kt == NKT - 1)),
                    r=[("Vg", kt), "Vones", ("pT", kt % 4)], w=[ok])
            for kt in range(NKT):
                s_ps = PS[4 + kt % 3]
                sk = ("PS", 4 + kt % 3)
                pt = pT[kt % 4]
                ptk = ("pT", kt % 4)
                P.add("pe", lambda e, s_ps=s_ps, hl=hl, kt=kt, Qb=Qb: e.matmul(
                    s_ps, KT[0:96, hl, 128 * kt:128 * kt + 128], Qb[0:96, :], start=True, stop=True),
                    r=[("KT", hl, kt // 4), ("KTr", hl), qk], w=[sk])
                P.add("act", lambda e, s_ps=s_ps, pt=pt: e.activation(pt, s_ps, AF.Exp, scale=scale), r=[sk], w=[ptk])
                if kt >= 2:
                    pv(kt - 2)
                if kt == min(3, NKT - 1):
                    if pending_epi is not None:
                        emit_epilogue(*pending_epi)
                        pending_epi = None
                    if hl == 0 and n + 4 < len(items):
                        emit_loads(items[n + 4][0])
                if kt == min(8, NKT - 1) and n + 1 < len(items):
                    emit_qproj(*items[n + 1])
                if kt == min(20, NKT - 1):
                    emit_cast(2)
            pv(NKT - 2)
            pv(NKT - 1)
            if n + 1 < len(items):
                pending_epi = (it, hl)
            else:
                emit_epilogue(it, hl)
    emit_cast(len(cast_jobs))
    P.barrier()

    B = Bump(A, small_end)
    wob = B.get([64, 16, 1024], BF16)
    woa = B.get([128, 8, 1024], BF16)
    stg4 = [B.get([128, 1024], F32) for _ in range(2)]
    ob_i = [B.get([64, 16, 512], BF16) for _ in range(2)]
    oa_i = [B.get([128, 8, 512], BF16) for _ in range(2)]
    x_i = [B.get([128, 8, 512], F32) for _ in range(2)]
    x_o = [B.get([128, 8, 512], F32) for _ in range(2)]
    assert B.off <= A.nbytes, B.off
    wo = dr["w_out"]
    if "w_out_e_bf" in dr:
        wob16 = dr["w_out_e_bf"]
        P.add("sp", lambda e: e.dma_start(out=wob, in_=wob16[0:1024, :].rearrange("(h d) o -> d h o", d=64)),
              r=["w_out_e_bf"], w=["wob"], dma=True)
        P.add("sp", lambda e: e.dma_start(out=woa, in_=wob16[1024:2048, :].rearrange("(k p) o -> p k o", p=128)),
              r=["w_out_e_bf"], w=["woa"], dma=True)
    else:
        for h in range(16):
            s = stg4[h % 2]
            P.add("sp", lambda e, s=s, h=h: e.dma_start(out=s[0:64, :], in_=wo[h * 64:(h + 1) * 64, :]),
                  w=[("stg4", h % 2)], dma=True)
            P.add(["dve", "pool"][h % 2], lambda e, s=s, h=h: e.tensor_copy(wob[:, h, :], s[0:64, :]),
                  r=[("stg4", h % 2)], w=["wob"])
        for k in range(8):
            s = stg4[k % 2]
            P.add("sp", lambda e, s=s, k=k: e.dma_start(out=s, in_=wo[1024 + k * 128:1024 + (k + 1) * 128, :]),
                  w=[("stg4", k % 2)], dma=True)
            P.add(["dve", "pool"][k % 2], lambda e, s=s, k=k: e.tensor_copy(woa[:, k, :], s), r=[("stg4", k % 2)], w=["woa"])
    x1T3 = dr["x1T"].rearrange("(k p) t -> p k t", p=128)
    X1OFF = dr.get("x1off", 0)
    for j in range(NQT):
        t0 = 512 * j
        ob_b, oa_b, xi, xo = ob_i[j % 2], oa_i[j % 2], x_i[j % 2], x_o[j % 2]
        P.add("sp", lambda e, ob_b=ob_b, t0=t0: e.dma_start(out=ob_b, in_=dr["ob"].rearrange("h d t -> d h t")[:, :, t0:t0 + 512]),
              r=[("ob_d", j)], w=[("ob_i", j % 2)], dma=True)
        P.add("sp", lambda e, oa_b=oa_b, t0=t0: e.dma_start(
            out=oa_b, in_=dr["oa"].rearrange("(c p) t -> p c t", p=128)[:, :, t0:t0 + 512]),
            r=[("oa_d", j)], w=[("oa_i", j % 2)], dma=True)
        P.add("sp", lambda e, xi=xi, t0=t0: e.dma_start(out=xi, in_=xT3[:, :, 16 + t0:16 + t0 + 512]),
              w=[("x_i", j % 2)], dma=True)
        for oc in range(8):
            ps = PS[oc % 4]
            pk = ("PS", oc % 4)
            for h in range(16):
                P.add("pe", lambda e, ps=ps, oc=oc, h=h, ob_b=ob_b: e.matmul(
                    ps, wob[:, h, oc * 128:(oc + 1) * 128], ob_b[:, h, :], start=(h == 0), stop=False),
                    r=["wob", ("ob_i", j % 2)], w=[pk])
            for k in range(8):
                P.add("pe", lambda e, ps=ps, oc=oc, k=k, oa_b=oa_b: e.matmul(
                    ps, woa[:, k, oc * 128:(oc + 1) * 128], oa_b[:, k, :], start=False, stop=(k == 7)),
                    r=["woa", ("oa_i", j % 2)], w=[pk])
            P.add("dve", lambda e, ps=ps, oc=oc, xi=xi, xo=xo: e.tensor_tensor(xo[:, oc, :], ps, xi[:, oc, :], ALU.add),
                  r=[pk, ("x_i", j % 2)], w=[("x_o", j % 2)])
        P.add("pool", lambda e, xo=xo, t0=t0: e.dma_start(out=x1T3[:, :, X1OFF + t0:X1OFF + t0 + 512], in_=xo),
              r=[("x_o", j % 2)], w=[("x1T", j)], dma=True)
    return [("x1T", j) for j in range(NQT)]


O_IN = 6192
XBC0, DT0, GB0, GC0, HD0, ZD0 = 1536, 4096, 4144, 4656, 5168, 5680
WP0 = 1536
NWP = O_IN - WP0
NH, NG, HPG, HP, NS = 24, 4, 6, 64, 128
SW = 1536
VO_NO, VO_CW, VO_CB, VO_DW, VO_FN, VO_M0, VO_M1 = 0, 8, 108, 128, 140, 148, 149
NV_O = 150
RO_DTB, RO_AL, RO_D, RO_SN = 0, 48, 96, 120
NR_O = 120 + 1536
NCONST = 5 * 128


def bc_last(ap2, n):
    return ap2.unsqueeze(2).broadcast_to([ap2.shape[0], ap2.shape[1], n])


class L1:
    def __init__(self, nc, P, A, PS, S, dr, base=0):
        self.nc, self.P, self.A, self.PS, self.S, self.dr = nc, P, A, PS, S, dr
        self.T = S // 2
        B = Bump(A, base)
        self.vecs = B.get([128, NV_O], F32)
        self.rows = B.get([128, NR_O], F32)
        self.cst = B.get([128, 5, 128], F32)
        self.ident_bf = B.get([128, 128], BF16)
        self.Mk_bf = B.get([128, 2, 128], BF16)
        self.ones_bf = B.get([128, 128], BF16)
        self.ones_f = B.get([128, 128], F32)
        self.epsc = B.get([128, 1], F32)
        self.onec = B.get([128, 1], F32)
        self.a_row = B.get([128, 48], F32)
        self.Sst = B.get([128, SW], F32)
        self.S_bf = B.get([128, SW], BF16)
        self.dA = B.get([128, 24], F32)
        self.acs = B.get([128, 24], F32)
        self.nacs = B.get([128, 24], F32)
        self.eacs = B.get([128, 24], F32)
        self.dec = B.get([128, 24], F32)
        self.etot = B.get([128, 24], F32)
        self.X = B.get([128, NH, HP], BF16)
        self.Xd = B.get([128, NH, HP], BF16)
        self.dAb = B.get([128, NH, 128], F32)
        self.L2 = [B.get([128, 3, 128], F32) for _ in range(2)]
        self.MT = [B.get([128, 3, 128], BF16) for _ in range(2)]
        self.base_end = B.off
        P, dr = self.P, self.dr
        P.add("sp", lambda e: e.dma_start(out=self.vecs, in_=dr["vecs_o"]), w=["vecs_o"], dma=True)
        P.add("sp", lambda e: e.dma_start(out=self.rows, in_=dr["rows_o"]), w=["rows_o"], dma=True)
        P.add("sp", lambda e: e.dma_start(out=self.cst, in_=dr["consts"].rearrange("p (a b) -> p a b", a=5)),
              w=["cst"], dma=True)
        P.add("dve", lambda e: e.tensor_copy(self.ident_bf, self.cst[:, 4, :]), r=["cst"], w=["ident_bf"])
        P.add("dve", lambda e: e.tensor_copy(self.Mk_bf, self.cst[:, 2:4, :]), r=["cst"], w=["Mk_bf"])
        P.add("dve", lambda e: e.memset(self.ones_bf, 1.0), w=["ones1"])
        P.add("dve", lambda e: e.memset(self.ones_f, 1.0), w=["ones_f1"])
        P.add("dve", lambda e: e.memset(self.epsc, EPS), w=["epsc1"])
        P.add("dve", lambda e: e.memset(self.onec, 1.0), w=["onec"])
        P.add("act", lambda e: e.activation(self.a_row, self.rows[:, RO_AL:RO_AL + 48], AF.Exp), r=["rows_o"], w=["a_row"])
        P.add("dve", lambda e: e.tensor_scalar(self.a_row, self.a_row, -1.0, None, ALU.mult), r=["a_row"], w=["a_row"])
        P.add("dve", lambda e: e.memset(self.Sst, 0.0), w=[("S", g) for g in range(4)])
        P.add("dve", lambda e: e.memset(self.S_bf, 0.0), w=[("S_bf", g) for g in range(4)])

    def ssd_chunk(self, pid, xs, Bt, BT, CT, dt, y, keys, y1=None, add_skip=False):
        P, PS = self.P, self.PS
        Uc = self.cst[:, pid, :]
        Mk = self.cst[:, 2 + pid, :]
        a_row = self.a_row[:, 24 * pid:24 * pid + 24]
        dA, acs, eacs, dec, etot, X, Xd = self.dA, self.acs, self.eacs, self.dec, self.etot, self.X, self.Xd
        acs_ps = PS[0][:, 0:24]
        tot_ps = PS[0][:, 32:56]
        P.add("dve", lambda e: e.tensor_tensor(dA, dt, a_row, ALU.mult), r=keys["dt"] + ["a_row"], w=["dA"])
        P.add("act", lambda e: e.activation(self.dAb[:, 0:6, :], bc_last(dA[:, 0:6], 128), AF.Copy), r=["dA"], w=[("dAb", 0)])
        P.add("dve", lambda e: e.tensor_copy(self.dAb[:, 6:15, :], bc_last(dA[:, 6:15], 128)), r=["dA"], w=[("dAb", 1)])
        P.add("act", lambda e: e.activation(self.dAb[:, 15:24, :], bc_last(dA[:, 15:24], 128), AF.Copy), r=["dA"], w=[("dAb", 2)])
        P.add("pe", lambda e: e.matmul(acs_ps, Uc, dA, start=True, stop=True), r=["cst", "dA"], w=[("PS", 0)])
        P.add("pe", lambda e: e.matmul(tot_ps, self.ones_f, dA, start=True, stop=True), r=["ones_f1", "dA"], w=[("PS", 0)])
        P.add("act", lambda e: e.activation(acs, acs_ps, AF.Copy), r=[("PS", 0)], w=["acs"])
        P.add("act", lambda e: e.mul(self.nacs, acs_ps, -1.0), r=[("PS", 0)], w=["nacs"])
        P.add("act", lambda e: e.activation(eacs, acs_ps, AF.Exp), r=[("PS", 0)], w=["eacs"])
        P.add("act", lambda e: e.activation(etot, tot_ps, AF.Exp), r=[("PS", 0)], w=["etot"])
        P.add("dve", lambda e: e.tensor_tensor(dec, tot_ps, acs, ALU.subtract), r=[("PS", 0), "acs"], w=["dec"])
        P.add("act", lambda e: e.activation(dec, dec, AF.Exp), r=["dec"], w=["dec"])
        xs3 = xs.rearrange("p (h d) -> p h d", h=NH)
        cb = PS[1].rearrange("p (g l) -> p g l", g=NG)
        for g in range(NG):
            P.add("pe", lambda e, g=g: e.matmul(cb[:, g, :], BT[:, g, :], CT[:, g, :], start=True, stop=True),
                  r=keys["BT"] + keys["CT"], w=[("PS", 1)])
        y3 = y.rearrange("p (h d) -> p h d", h=NH)
        S3 = self.Sst.rearrange("p (h d) -> p h d", h=NH)
        Sb3 = self.S_bf.rearrange("p (h d) -> p h d", h=NH)
        ident_f = self.cst[:, 4, :]
        Mk3 = self.Mk_bf[:, pid, :].unsqueeze(1).broadcast_to([128, 3, 128])
        yop = PS[6][:, 0:HPG * HP]
        yop3 = yop.rearrange("p (r d) -> p r d", r=HPG)
        zp = PS[0][:, 128:128 + HPG * HP]
        zp3 = zp.rearrange("p (r d) -> p r d", r=HPG)
        def stageA(hg):
            g = hg // 2
            h0 = 3 * hg
            L = self.L2[hg % 2]
            MT = self.MT[hg % 2]
            lk, mk = ("L", hg % 2), ("MT", hg % 2)
            ab = PS[2 + hg % 2].rearrange("p (r l) -> p r l", r=4)[:, 0:3, :]
            abk = ("PS", 2 + hg % 2)
            P.add("pe", lambda e, ab=ab: e.matmul(ab, self.ident_bf, Mk3, start=True, stop=False), r=["Mk_bf", "ident_bf"], w=[abk])
            for r in range(3):
                P.add("pe", lambda e, ab=ab, r=r, h0=h0: e.matmul(ab[:, r, :], self.dAb[:, h0 + r, :], Uc, start=False, stop=(r == 2)),
                      r=[("dAb", 0 if h0 + r < 6 else (1 if h0 + r < 15 else 2)), "cst"], w=[abk])
            for r in range(3):
                P.add("act", lambda e, ab=ab, r=r, h0=h0, L=L: e.activation(L[:, r, :], ab[:, r, :], AF.Exp,
                                                                           bias=self.nacs[:, h0 + r:h0 + r + 1]),
                      r=[abk, "nacs"], w=[lk])
            cbg = cb[:, g, :].unsqueeze(1).broadcast_to([128, 3, 128])
            P.add("dve", lambda e, MT=MT, cbg=cbg, L=L: e.tensor_tensor(MT, cbg, L, ALU.mult), r=[("PS", 1), lk], w=[mk])

        def stageB(hg):
            g, half = hg // 2, hg % 2
            h0 = 3 * hg
            MT = self.MT[hg % 2]
            mk = ("MT", hg % 2)
            if half == 0:
                P.add("pe", lambda e, g=g: e.matmul(yop, CT[:, g, :], self.S_bf[:, g * 384:(g + 1) * 384], start=True, stop=True),
                      r=keys["CT"] + [("S_bf", g)], w=[("PS", 6)])
            ydp = PS[4 + hg % 2][:, 0:3 * HP].rearrange("p (r d) -> p r d", r=3)
            ydk = ("PS", 4 + hg % 2)
            for r in range(3):
                P.add("pe", lambda e, r=r, h0=h0, MT=MT, ydp=ydp: e.matmul(ydp[:, r, :], MT[:, r, :], X[:, h0 + r, :], start=True, stop=True),
                      r=[mk, "X"], w=[ydk])
            yg = y3[:, h0:h0 + 3, :]
            P.add("dve", lambda e, yg=yg, h0=h0, half=half: e.tensor_tensor(yg, yop3[:, 3 * half:3 * half + 3, :],
                                                                           bc_last(eacs[:, h0:h0 + 3], HP), ALU.mult),
                  r=[("PS", 6), "eacs"], w=keys["y"])
            P.add("dve", lambda e, yg=yg, ydp=ydp: e.tensor_tensor(yg, yg, ydp, ALU.add), r=[ydk] + keys["y"], w=keys["y"])
            if half == 1:
                P.add("pe", lambda e, g=g: e.matmul(zp, Bt[:, g * 128:(g + 1) * 128], Xd.rearrange("p h d -> p (h d)")[:, g * 384:(g + 1) * 384],
                                                    start=True, stop=True), r=keys["Bt"] + ["Xd"], w=[("PS", 0)])
                Sg = S3[:, HPG * g:HPG * g + HPG, :]
                P.add("dve", lambda e, g=g, Sg=Sg: e.tensor_tensor(Sg, Sg, bc_last(etot[:, HPG * g:HPG * g + HPG], HP), ALU.mult),
                      r=["etot", ("S", g)], w=[("S", g)])
                P.add("dve", lambda e, Sg=Sg: e.tensor_tensor(Sg, Sg, zp3, ALU.add), r=[("PS", 0), ("S", g)], w=[("S", g)])
                P.add("act", lambda e, g=g: e.activation(Sb3[:, HPG * g:HPG * g + HPG, :], S3[:, HPG * g:HPG * g + HPG, :], AF.Copy),
                      r=[("S", g)], w=[("S_bf", g)])

        stageA(0)
        P.add("dve", lambda e: e.tensor_tensor(X, xs3, bc_last(dt, HP), ALU.mult), r=keys["xs"] + keys["dt"], w=["X"])
        for hg in range(2 * NG):
            if hg + 1 < 2 * NG:
                stageA(hg + 1)
            if hg == 1:
                P.add("dve", lambda e: e.tensor_tensor(Xd, X, bc_last(dec, HP), ALU.mult), r=["X", "dec"], w=["Xd"])
            stageB(hg)
        if add_skip:
            P.add("dve", lambda e: e.tensor_tensor(Xd, xs3, bc_last(self.rows[:, RO_D:RO_D + 24], HP), ALU.mult),
                  r=keys["xs"] + ["rows_o"], w=["Xd"])
            P.add("dve", lambda e: e.tensor_tensor(y3, y3, Xd, ALU.add), r=["Xd"] + keys["y"], w=keys["y"])
        if y1 is not None:
            P.add("dve", lambda e: e.tensor_tensor(y, y, y1, ALU.add), r=keys["y1"] + keys["y"], w=keys["y"])

    def prep_and_pass1(self):
        nc, P, A, PS, dr, T = self.nc, self.P, self.A, self.PS, self.dr, self.T
        vecs, rows = self.vecs, self.rows
        TT, TW = 256, 260
        NTT = T // TT
        B = Bump(A, self.base_end)
        wbf = B.get([128, 8, NWP], BF16)
        stg = [B.get([128, NWP], F32) for _ in range(1)]
        w3 = dr["w_in_o"].rearrange("(k p) c -> p k c", p=128)
        if "w_in_o_bf" in dr:
            w3b = dr["w_in_o_bf"].rearrange("(k p) c -> p k c", p=128)
            for k in range(8):
                P.add("sp", lambda e, k=k: e.dma_start(out=wbf[:, k, :], in_=w3b[:, k, WP0:O_IN]), r=["w_in_o_bf"],
                      w=[("wbf", k), ("wbfb", k)], dma=True)
        else:
            for k in range(8):
                s = stg[0]
                P.add("sp", lambda e, s=s, k=k: e.dma_start(out=s, in_=w3[:, k, WP0:O_IN]), w=["stg1"], dma=True)
                half = NWP // 2
                P.add("dve", lambda e, s=s, k=k: e.tensor_copy(wbf[:, k, 0:half], s[:, 0:half]), r=["stg1"], w=[("wbf", k)])
                P.add("pool", lambda e, s=s, k=k: e.tensor_copy(wbf[:, k, half:NWP], s[:, half:NWP]), r=["stg1"], w=[("wbfb", k)])
        P.barrier()
        B = Bump(A, B.off - (NWP * 4 + 31) // 32 * 32)
        xb_ = B.get([128, 8, TW], F32)
        sq_ = B.get([128, 8, TW], BF16)
        xn_ = B.get([128, 8, TW], BF16)
        rstd_ = B.get([128, TW], F32)
        dtr = B.get([128, 48], F32)
        dt_t = B.get([128, 2, 48], F32)
        pre = [B.get([128, TW], BF16) for _ in range(2)]
        dgb = [B.get([128, 5, 128], BF16) for _ in range(2)]
        hb = B.get([128, 20], F32)
        acc = [B.get([128, TT], F32)]
        acch = B.get([128, TT], F32)
        acch2 = [acch, B.get([128, TT], F32)]
        th2 = [B.get([128, TT], F32) for _ in range(2)]
        th = B.get([128, TW], F32)
        so = [B.get([128, TT], F32) for _ in range(2)]
        BT_t = B.get([128, 4, TT], BF16)
        CT_t = B.get([128, 4, TT], BF16)
        xs_tok = B.get([128, 2, SW], F32)
        Bt_tok = B.get([128, 2, 512], BF16)
        od_t = B.get([128, 4, TT], BF16)
        gcs = B.get([128, TW], F32)
        v = B.get([128, TW], F32)
        zz = B.get([128, TT], F32)
        yb = [B.get([128, SW], F32) for _ in range(2)]
        assert B.off <= A.nbytes, B.off
        x1T3 = dr["x1T"].rearrange("(k p) t -> p k t", p=128)
        wk = [("wbf", k) for k in range(8)] + [("wbfb", k) for k in range(8)]
        P.add("dve", lambda e: e.tensor_scalar(hb, vecs[:, VO_CB:VO_CB + 20], 0.5, None, ALU.mult), r=["vecs_o"], w=["hb"])
        for c in range(20):
            d = dgb[c % 2]
            for i in range(5):
                P.add("dve", lambda e, d=d, c=c, i=i: e.tensor_scalar(d[:, i, :], self.ident_bf, vecs[:, VO_CW + c * 5 + i:VO_CW + c * 5 + i + 1],
                                                                   None, ALU.mult), r=["ident_bf", "vecs_o"], w=[("dgb", c % 2)])
            P.add("sp", lambda e, d=d, c=c: e.dma_start(out=dr["dgc"][c].rearrange("p (i m) -> p i m", i=5), in_=d),
                  r=[("dgb", c % 2)], w=[("dgc", c)], dma=True)
        ident_f = self.cst[:, 4, :]
        for j in range(NTT):
            t0 = TT * j
            P.add("sp", lambda e, t0=t0: e.dma_start(out=xb_, in_=x1T3[:, :, t0:t0 + TW]),
                  r=[("x1T", max(t0 - 2, 0) // 512), ("x1T", min(t0 + 257, T - 1) // 512)] + (["x1halo"] if j == NTT - 1 else []) + ["x1zero"],
                  w=[("xb1", k) for k in range(8)], dma=True)
            rmsnorm_tile(P, "l1", xb_, 8, TW, lambda k: vecs[:, VO_NO + k:VO_NO + k + 1], self.ones_bf, sq_, PS[7][:, 0:TW],
                         rstd_, xn_, 1.0 / D, [("xb1", k) for k in range(8)], "xn1", ssk=("PS", 7), epsc=self.epsc)
            xnk = [("xn1", k) for k in range(8)]
            P.add("pool", lambda e, t0=t0: e.dma_start(
                out=dr["xnT"].rearrange("(k p) t -> p k t", p=128)[:, :, t0:t0 + TT], in_=xn_[:, :, 2:2 + TT]),
                r=xnk, w=[("xnT", j)], dma=True)
            for jj in range(2):
                dtp = PS[0][:, 64:112]
                for k in range(8):
                    P.add("pe", lambda e, k=k, jj=jj: e.matmul(dtp, xn_[:, k, 2 + 128 * jj:2 + 128 * jj + 128],
                                                              wbf[:, k, DT0 - WP0:DT0 - WP0 + 48], start=(k == 0), stop=(k == 7)),
                          r=[("xn1", k)] + wk, w=[("PS", 0)])
                P.add("dve", lambda e: e.tensor_tensor(dtr, dtp, rows[:, RO_DTB:RO_DTB + 48], ALU.add),
                      r=[("PS", 0), "rows_o"], w=["dtr"])
                P.add("act", lambda e: e.activation(dtr, dtr, AF.Exp), r=["dtr"], w=["dtr"])
                P.add("act", lambda e, jj=jj: e.activation(dt_t[:, jj, :], dtr, AF.Ln, bias=self.onec), r=["dtr", "onec"],
                      w=[("dt_t", jj)])
                P.add("pool", lambda e, jj=jj, t0=t0: e.dma_start(out=dr["dtT"][t0 + 128 * jj:t0 + 128 * jj + 128, :], in_=dt_t[:, jj, :]),
                      r=[("dt_t", jj)], w=[("dtT", 2 * j + jj)], dma=True)
            def xbcA(c):
                pp = PS[6 + c % 2][:, 0:TW]
                cps = PS[6 + c % 2][:, 0:TT]
                pk = ("PS", 6 + c % 2)
                pr, d = pre[c % 2], dgb[c % 2]
                P.add("sp", lambda e, d=d, c=c: e.dma_start(out=d, in_=dr["dgc"][c].rearrange("p (i m) -> p i m", i=5)),
                      r=[("dgc", c)], w=[("dgb", c % 2)], dma=True)
                for k in range(8):
                    P.add("pe", lambda e, pp=pp, c=c, k=k: e.matmul(
                        pp, wbf[:, k, XBC0 - WP0 + c * 128:XBC0 - WP0 + (c + 1) * 128], xn_[:, k, :], start=(k == 0), stop=(k == 7)),
                        r=[("xn1", k)] + wk, w=[pk])
                P.add("act", lambda e, pp=pp, pr=pr: e.activation(pr, pp, AF.Copy), r=[pk], w=[("pre", c % 2)])
                for i in range(5):
                    P.add("pe", lambda e, cps=cps, d=d, pr=pr, i=i: e.matmul(cps, d[:, i, :], pr[:, i:i + TT], start=(i == 0), stop=(i == 4)),
                          r=[("pre", c % 2), ("dgb", c % 2)], w=[pk])
                P.add("act", lambda e, cps=cps, c=c: e.activation(th2[c % 2], cps, AF.Tanh, bias=hb[:, c:c + 1], scale=0.5),
                      r=[pk, "hb"], w=[("th1", c % 2)])
                P.add("act", lambda e, cps=cps, c=c: e.activation(acch2[c % 2], cps, AF.Identity, bias=hb[:, c:c + 1], scale=0.5),
                      r=[pk, "hb"], w=[("acch", c % 2)])

            def xbcB(c):
                sout = so[c % 2]
                thc, acc_h = th2[c % 2], acch2[c % 2]
                tk = [("th1", c % 2), ("acch", c % 2)]
                if c < 12:
                    P.add("dve", lambda e, sout=sout, thc=thc, acc_h=acc_h: e.scalar_tensor_tensor(sout, thc, 1.0, acc_h, ALU.add, ALU.mult),
                          r=tk, w=[("so", c % 2)])
                    tpb = PS[4 + c % 2][:, 0:256]
                    tpk = ("PS", 4 + c % 2)
                    for jj in range(2):
                        P.add("pe", lambda e, tpb=tpb, sout=sout, jj=jj: e.transpose(tpb[:, 128 * jj:128 * jj + 128], sout[:, 128 * jj:128 * jj + 128], ident_f),
                              r=[("so", c % 2), "cst"], w=[tpk])
                    P.add("act", lambda e, tpb=tpb, c=c: e.activation(xs_tok[:, :, c * 128:(c + 1) * 128], tpb.rearrange("p (j l) -> p j l", j=2), AF.Copy),
                          r=[tpk], w=[("xs_tok", 0), ("xs_tok", 1)])
                elif c < 16:
                    g = c - 12
                    P.add("dve", lambda e, g=g, thc=thc, acc_h=acc_h: e.scalar_tensor_tensor(BT_t[:, g, :], thc, 1.0, acc_h, ALU.add, ALU.mult),
                          r=tk, w=[("BT_t", g)])
                    tpb = PS[4 + c % 2].bitcast(BF16)[:, 0:256]
                    tpk = ("PS", 4 + c % 2)
                    for jj in range(2):
                        P.add("pe", lambda e, tpb=tpb, g=g, jj=jj: e.transpose(tpb[:, 128 * jj:128 * jj + 128], BT_t[:, g, 128 * jj:128 * jj + 128], self.ident_bf),
                              r=[("BT_t", g), "ident_bf"], w=[tpk])
                    P.add("act", lambda e, tpb=tpb, g=g: e.activation(Bt_tok[:, :, g * 128:(g + 1) * 128], tpb.rearrange("p (j l) -> p j l", j=2), AF.Copy),
                          r=[tpk], w=[("Bt_tok", 0), ("Bt_tok", 1)])
                else:
                    g = c - 16
                    P.add("dve", lambda e, g=g, thc=thc, acc_h=acc_h: e.scalar_tensor_tensor(CT_t[:, g, :], thc, 1.0, acc_h, ALU.add, ALU.mult),
                          r=tk, w=[("CT_t", g)])

            xbcA(0)
            for c in range(20):
                if c + 1 < 20:
                    xbcA(c + 1)
                xbcB(c)
            P.add("pool", lambda e, t0=t0: e.dma_start(out=dr["xsT"][t0:t0 + TT, :].rearrange("(j p) c -> p j c", p=128), in_=xs_tok),
                  r=[("xs_tok", 0), ("xs_tok", 1)], w=[("xsT", j)], dma=True)
            P.add("pool", lambda e, t0=t0: e.dma_start(out=dr["Bts"][t0:t0 + TT, :].rearrange("(j p) c -> p j c", p=128), in_=Bt_tok),
                  r=[("Bt_tok", 0), ("Bt_tok", 1)], w=[("Bts", j)], dma=True)
            P.add("pool", lambda e, t0=t0: e.dma_start(out=dr["BTs"].rearrange("(g p) t -> p g t", p=128)[:, :, t0:t0 + TT], in_=BT_t),
                  r=[("BT_t", g) for g in range(4)], w=[("BTs", j)], dma=True)
            P.add("pool", lambda e, t0=t0: e.dma_start(out=dr["CTs"].rearrange("(g p) t -> p g t", p=128)[:, :, t0:t0 + TT], in_=CT_t),
                  r=[("CT_t", g) for g in range(4)], w=[("CTs", j)], dma=True)
            for c in range(4):
                pgc, phd = PS[6][:, 0:TW], PS[7][:, 0:TW]
                for (pp, col0, pk) in ((pgc, GC0, ("PS", 6)), (phd, HD0, ("PS", 7))):
                    for k in range(8):
                        P.add("pe", lambda e, pp=pp, col0=col0, c=c, k=k: e.matmul(
                            pp, wbf[:, k, col0 - WP0 + c * 128:col0 - WP0 + (c + 1) * 128], xn_[:, k, :], start=(k == 0), stop=(k == 7)),
                            r=[("xn1", k)] + wk, w=[pk])
                P.add("act", lambda e: e.activation(gcs, pgc, AF.Copy), r=[("PS", 6)], w=["gcs"])
                P.add("dve", lambda e: e.tensor_tensor(v, gcs, phd, ALU.mult), r=["gcs", ("PS", 7)], w=["v"])
                wd = lambda i, c=c: vecs[:, VO_DW + c * 3 + i:VO_DW + c * 3 + i + 1]
                ac = acc[0]
                P.add("dve", lambda e, wd=wd, ac=ac: e.tensor_scalar(ac, v[:, 1:1 + TT], wd(0), None, ALU.mult),
                      r=["v", "vecs_o"], w=[("acc1", 0)])
                for i in (1, 2):
                    P.add("dve", lambda e, wd=wd, ac=ac, i=i: e.scalar_tensor_tensor(ac, v[:, 1 + i:1 + i + TT], wd(i), ac, ALU.mult, ALU.add),
                          r=["v", ("acc1", 0)], w=[("acc1", 0)])
                pgb, pzd = PS[4][:, 0:TT], PS[5][:, 0:TT]
                for (pp, col0, pk) in ((pgb, GB0, ("PS", 4)), (pzd, ZD0, ("PS", 5))):
                    for k in range(8):
                        P.add("pe", lambda e, pp=pp, col0=col0, c=c, k=k: e.matmul(
                            pp, wbf[:, k, col0 - WP0 + c * 128:col0 - WP0 + (c + 1) * 128], xn_[:, k, 2:2 + TT], start=(k == 0), stop=(k == 7)),
                            r=[("xn1", k)] + wk, w=[pk])
                P.add("dve", lambda e, ac=ac: e.tensor_tensor(ac, ac, pgb, ALU.mult), r=[("acc1", 0), ("PS", 4)], w=[("acc1", 0)])
                P.add("act", lambda e: e.activation(th[:, 0:TT], pzd, AF.Tanh, scale=0.5), r=[("PS", 5)], w=["th1"])
                P.add("dve", lambda e: e.scalar_tensor_tensor(zz, th[:, 0:TT], 1.0, pzd, ALU.add, ALU.mult), r=["th1", ("PS", 5)], w=["zz"])
                P.add("dve", lambda e, c=c, ac=ac: e.scalar_tensor_tensor(od_t[:, c, :], ac, 0.5, zz, ALU.mult, ALU.mult),
                      r=[("acc1", 0), "zz"], w=[("od_t", c)])
            P.add("pool", lambda e, t0=t0: e.dma_start(out=dr["odT"].rearrange("(c p) t -> p c t", p=128)[:, :, t0:t0 + TT], in_=od_t),
                  r=[("od_t", c) for c in range(4)], w=[("odT", j // 2)], dma=True)
            for jj in range(2):
                y = yb[jj]
                keys = {"xs": [("xs_tok", jj)], "Bt": [("Bt_tok", jj)], "BT": [("BT_t", g) for g in range(4)],
                        "CT": [("CT_t", g) for g in range(4)], "dt": [("dt_t", jj)], "y": [("yb", jj)]}
                self._BTk = [("BT_t", g) for g in range(4)]
                self._CTk = [("CT_t", g) for g in range(4)]
                self.ssd_chunk(0, xs_tok[:, jj, :], Bt_tok[:, jj, :], BT_t[:, :, 128 * jj:128 * jj + 128],
                               CT_t[:, :, 128 * jj:128 * jj + 128], dt_t[:, jj, 0:24], y, keys, add_skip=True)
                P.add("pool", lambda e, y=y, t0=t0, jj=jj: e.dma_start(out=dr["y1"][t0 + 128 * jj:t0 + 128 * jj + 128, :], in_=y),
                      r=[("yb", jj)], w=[("y1", 2 * j + jj)], dma=True)
        self.prep_end = B.off

    def exchange_states(self):
        P, dr, A = self.P, self.dr, self.A
        pairs = getattr(self, 'pairs', [[0, 1], [2, 3], [4, 5], [6, 7]])
        P.add("sp", lambda e: e.dma_start(out=dr["cc_in2"], in_=self.Sst), r=[("S", g) for g in range(4)], w=["cc_in2"], dma=True)
        P.add("pool", lambda e: e.collective_compute("AllGather", ALU.bypass, replica_groups=pairs,
                                                     ins=[dr["cc_in2"]], outs=[dr["cc_out2"]]),
              r=["cc_in2"], w=["cc_out2"], dma=True, inc=1)
        P.barrier()
        B = Bump(A, self.base_end)
        g2 = B.get([128, 2, SW], F32)
        P.add("sp", lambda e: e.dma_start(out=g2, in_=dr["cc_out2"].rearrange("(r p) c -> p r c", r=2)), r=["cc_out2"], w=["g2"], dma=True)
        m0 = self.vecs[:, VO_M0:VO_M0 + 1]
        m1 = self.vecs[:, VO_M1:VO_M1 + 1]
        P.add("dve", lambda e: e.tensor_scalar(g2[:, 0, :], g2[:, 0, :], m1, None, ALU.mult), r=["g2", "vecs_o"], w=["g2"])
        P.add("dve", lambda e: e.scalar_tensor_tensor(self.Sst, g2[:, 1, :], m0, g2[:, 0, :], ALU.mult, ALU.add),
              r=["g2"], w=[("S", g) for g in range(4)])
        P.add("act", lambda e: e.activation(self.S_bf, self.Sst, AF.Copy), r=[("S", g) for g in range(4)],
              w=[("S_bf", g) for g in range(4)])
        P.barrier()

    def pass2_and_out(self):
        nc, P, A, PS, dr, T = self.nc, self.P, self.A, self.PS, self.dr, self.T
        vecs, rows = self.vecs, self.rows
        NC = T // 128
        B = Bump(A, self.base_end)
        wz = B.get([128, 8, SW], BF16)
        wo = B.get([128, 16, 1024], BF16)
        yz = B.get([128, SW], F32)
        stg = yz
        w3 = dr["w_in_o"].rearrange("(k p) c -> p k c", p=128)
        wo3 = dr["w_out_o"].rearrange("(k p) c -> p k c", p=128)
        if "w_in_o_bf" in dr:
            P.add("sp", lambda e: e.dma_start(out=wz, in_=dr["w_in_o_bf"].rearrange("(k p) c -> p k c", p=128)[:, :, 0:SW]),
                  r=["w_in_o_bf"], w=["wz"], dma=True)
            P.add("sp", lambda e: e.dma_start(out=wo, in_=dr["w_out_o_bf"].rearrange("(k p) c -> p k c", p=128)),
                  r=["w_out_o_bf"], w=["wo"], dma=True)
        else:
            for k in range(8):
                P.add("sp", lambda e, k=k: e.dma_start(out=stg, in_=w3[:, k, 0:SW]), w=["stg2"], dma=True)
                P.add("dve", lambda e, k=k: e.tensor_copy(wz[:, k, :], stg), r=["stg2"], w=["wz"])
            for k in range(16):
                P.add("sp", lambda e, k=k: e.dma_start(out=stg[:, 0:1024], in_=wo3[:, k, :]), w=["stg2"], dma=True)
                P.add("pool", lambda e, k=k: e.tensor_copy(wo[:, k, :], stg[:, 0:1024]), r=["stg2"], w=["wo"])
        P.barrier()
        xs_i = [B.get([128, SW], F32)] * 2
        y1_i = [B.get([128, SW], F32)] * 2
        Bt_i = [B.get([128, 512], BF16) for _ in range(2)]
        BT_i = [B.get([128, 4, 128], BF16) for _ in range(2)]
        CT_i = [B.get([128, 4, 128], BF16) for _ in range(2)]
        dt_i = [B.get([128, 24], F32) for _ in range(2)]
        xn_i = [B.get([128, 8, 128], BF16) for _ in range(2)]
        y = B.get([128, SW], F32)
        th = B.get([128, 512], F32)
        thb = B.get([128, 512], F32)
        zg = B.get([128, SW], F32)
        junk = B.get([128, 4, 384], BF16)
        ssg = B.get([128, 4], F32)
        rs4 = B.get([128, 4], F32)
        oc = B.get([128, SW], BF16)
        ocT = [B.get([128, 12, 512], BF16)] * 2
        od_i = [B.get([128, 4, 512], BF16)] * 2
        xio = B.get([128, 8, 512], F32)
        x1_i = [xio, xio]
        x2 = xio
        sq2 = B.get([128, 8, 512], BF16)
        rstd2 = B.get([128, 512], F32)
        xo = [xio, xio]
        assert B.off <= A.nbytes, B.off
        x1T3 = dr["x1T"].rearrange("(k p) t -> p k t", p=128)
        outT3 = dr["outT"].rearrange("(k p) t -> p k t", p=128)
        fin = []
        for ci, c in enumerate(range(NC - 1, -1, -1)):
            b = ci % 2
            r0 = 128 * c
            grp = c // 4
            gi = (NC // 4 - 1 - grp)
            P.add("sp", lambda e, b=b, r0=r0: e.dma_start(out=xs_i[b], in_=dr["xsT"][r0:r0 + 128, :]), r=[("xsT", c // 2)], w=[("xs_i", 0)], dma=True)
            P.add("sp", lambda e, b=b, r0=r0: e.dma_start(out=y1_i[b], in_=dr["y1"][r0:r0 + 128, :]), r=[("y1", c)], w=[("y1_i", 0)], dma=True)
            P.add("sp", lambda e, b=b, r0=r0: e.dma_start(out=Bt_i[b], in_=dr["Bts"][r0:r0 + 128, :]), r=[("Bts", c // 2)], w=[("Bt_i", b)], dma=True)
            P.add("sp", lambda e, b=b, r0=r0: e.dma_start(out=BT_i[b], in_=dr["BTs"].rearrange("(g p) t -> p g t", p=128)[:, :, r0:r0 + 128]),
                  r=[("BTs", c // 2)], w=[("BT_i", b)], dma=True)
            P.add("sp", lambda e, b=b, r0=r0: e.dma_start(out=CT_i[b], in_=dr["CTs"].rearrange("(g p) t -> p g t", p=128)[:, :, r0:r0 + 128]),
                  r=[("CTs", c // 2)], w=[("CT_i", b)], dma=True)
            P.add("sp", lambda e, b=b, r0=r0: e.dma_start(out=dt_i[b], in_=dr["dtT"][r0:r0 + 128, 24:48]), r=[("dtT", c)], w=[("dt_i", b)], dma=True)
            P.add("sp", lambda e, b=b, r0=r0: e.dma_start(out=xn_i[b], in_=dr["xnT"].rearrange("(k p) t -> p k t", p=128)[:, :, r0:r0 + 128]),
                  r=[("xnT", c // 2)], w=[("xn_i", b)], dma=True)
            keys = {"xs": [("xs_i", 0)], "Bt": [("Bt_i", b)], "BT": [("BT_i", b)], "CT": [("CT_i", b)], "dt": [("dt_i", b)],
                    "y": ["y2"], "y1": [("y1_i", 0)]}
            for i in range(3):
                zp = PS[7 - i % 2]
                zk = ("PS", 7 - i % 2)
                thz = th if i % 2 == 0 else thb
                tk = "th2" if i % 2 == 0 else "th2b"
                for k in range(8):
                    P.add("pe", lambda e, b=b, k=k, i=i, zp=zp: e.matmul(zp, xn_i[b][:, k, :], wz[:, k, i * 512:(i + 1) * 512],
                                                                        start=(k == 0), stop=(k == 7)), r=[("xn_i", b), "wz"], w=[zk])
                P.add("act", lambda e, zp=zp, thz=thz: e.activation(thz, zp, AF.Tanh, scale=0.5), r=[zk], w=[tk])
                P.add("dve", lambda e, i=i, zp=zp, thz=thz: e.scalar_tensor_tensor(zg[:, i * 512:(i + 1) * 512], thz, 1.0, zp, ALU.add, ALU.mult),
                      r=[tk, zk], w=[("zg", i)])
            self.ssd_chunk(1, xs_i[b], Bt_i[b], BT_i[b], CT_i[b], dt_i[b], y, keys, y1=y1_i[b])
            for i in range(3):
                P.add("dve", lambda e, i=i: e.scalar_tensor_tensor(yz[:, i * 512:(i + 1) * 512], y[:, i * 512:(i + 1) * 512], 0.5,
                                                                   zg[:, i * 512:(i + 1) * 512], ALU.mult, ALU.mult),
                      r=["y2", ("zg", i)], w=[("yz", i)])
            yzk = [("yz", i) for i in range(3)]
            P.add("dve", lambda e: e.memset(ssg, 0.0), w=[("ssg", g) for g in range(4)])
            for g in range(4):
                P.add("act", lambda e, g=g: e.activation(junk[:, g, :], yz[:, g * 384:(g + 1) * 384], AF.Square, accum_out=ssg[:, g:g + 1]),
                      r=yzk, w=[("ssg", g), ("junk", g)])
            P.add("act", lambda e: e.activation(rs4, ssg, AF.Ln, bias=self.epsc, scale=1.0 / 384), r=[("ssg", g) for g in range(4)] + ["epsc1"], w=["rs4"])
            P.add("act", lambda e: e.activation(rs4, rs4, AF.Exp, scale=-0.5), r=["rs4"], w=["rs4"])
            for g in range(4):
                P.add("dve", lambda e, g=g: e.scalar_tensor_tensor(oc[:, g * 384:(g + 1) * 384], yz[:, g * 384:(g + 1) * 384],
                                                                   rs4[:, g:g + 1], rows[:, RO_SN + g * 384:RO_SN + (g + 1) * 384], ALU.mult, ALU.mult),
                      r=yzk + ["rs4", "rows_o"], w=[("oc", g)])
            ock = [("oc", g) for g in range(4)]
            oT = ocT[gi % 2]
            lo = 128 * (c % 4)
            for q4 in range(3):
                tpb = PS[6 + q4 % 2].bitcast(BF16)[:, 0:512]
                pk = ("PS", 6 + q4 % 2)
                for u4 in range(4):
                    cc = 4 * q4 + u4
                    P.add("pe", lambda e, tpb=tpb, cc=cc, u4=u4: e.transpose(tpb[:, 128 * u4:128 * u4 + 128], oc[:, cc * 128:(cc + 1) * 128], self.ident_bf),
                          r=ock + ["ident_bf"], w=[pk])
                dst = oT[:, 4 * q4:4 * q4 + 4, lo:lo + 128]
                src = tpb.rearrange("p (u l) -> p u l", u=4)
                if q4 % 2 == 0:
                    P.add("act", lambda e, dst=dst, src=src: e.activation(dst, src, AF.Copy), r=[pk], w=[("ocT", 0)])
                else:
                    P.add("dve", lambda e, dst=dst, src=src: e.tensor_copy(dst, src), r=[pk], w=[("ocT", 0)])
            if c % 4 == 0:
                t0 = 512 * grp
                gb = gi % 2
                P.add("sp", lambda e, gb=gb, t0=t0: e.dma_start(out=od_i[gb], in_=dr["odT"].rearrange("(c p) t -> p c t", p=128)[:, :, t0:t0 + 512]),
                      r=[("odT", grp)], w=[("od_i", 0)], dma=True)
                P.add("sp", lambda e, gb=gb, t0=t0: e.dma_start(out=x1_i[gb], in_=x1T3[:, :, 2 + t0:2 + t0 + 512]),
                      r=[("x1T", grp)], w=[("x2", o) for o in range(8)], dma=True)
                for o in range(8):
                    ps = PS[4 + o % 2]
                    pk = ("PS", 4 + o % 2)
                    for cc in range(12):
                        P.add("pe", lambda e, ps=ps, o=o, cc=cc, oT=oT: e.matmul(ps, wo[:, cc, o * 128:(o + 1) * 128], oT[:, cc, :],
                                                                                 start=(cc == 0), stop=False), r=["wo", ("ocT", 0)], w=[pk])
                    for cc in range(4):
                        P.add("pe", lambda e, ps=ps, o=o, cc=cc, gb=gb: e.matmul(ps, wo[:, 12 + cc, o * 128:(o + 1) * 128], od_i[gb][:, cc, :],
                                                                                 start=False, stop=(cc == 3)), r=["wo", ("od_i", 0)], w=[pk])
                    P.add("dve", lambda e, ps=ps, o=o, gb=gb: e.tensor_tensor(x2[:, o, :], ps, x1_i[gb][:, o, :], ALU.add),
                          r=[pk, ("x2", o)], w=[("x2", o)])
                xout = xo[gi % 2]
                rmsnorm_tile(P, "fin", x2, 8, 512, lambda k: vecs[:, VO_FN + k:VO_FN + k + 1], self.ones_bf, sq2, PS[7], rstd2, xout,
                             1.0 / D, [("x2", o) for o in range(8)], "x2", ssk=("PS", 7), epsc=self.epsc)
                P.add("pool", lambda e, xout=xout, t0=t0: e.dma_start(out=outT3[:, :, t0:t0 + 512], in_=xout),
                      r=[("x2", k) for k in range(8)], w=[("outT", grp)], dma=True)
                fin.append(("outT", grp))
        return fin

import numpy as np

ROPE_THETA = 10000.0


def chunkcol(v):
    v = np.asarray(v, np.float32)
    return np.ascontiguousarray(v.reshape(-1, 128).T)


def prep_l0(inp, b, hh, S):
    x = inp["x"][b]
    xs = x if hh == 0 else x[::-1]
    xT = np.zeros((1024, 16 + S + 16), np.float32)
    xT[:, 16:16 + S] = xs.T
    pos = np.arange(S, dtype=np.float32)
    if hh == 1:
        pos = pos[::-1]
    inv = (ROPE_THETA ** (-np.arange(16, dtype=np.float32) / 16)).astype(np.float32)
    ang = (pos[:, None] * inv[None, :]).astype(np.float32)
    cos = np.cos(ang).astype(np.float32)
    sin = np.sin(ang).astype(np.float32)
    rope = np.stack([np.concatenate([cos, cos], 1).T, np.concatenate([-sin, sin], 1).T]).astype(np.float32)
    cw = inp["conv_a_w"][0]
    if hh == 1:
        cw = cw[::-1]
    cwv = cw.T.reshape(8, 128, 31).transpose(1, 0, 2).reshape(128, 8 * 31)
    vecs = np.concatenate([chunkcol(inp["norm_e"][0]), chunkcol(inp["conv_a_b"][0]), chunkcol(inp["ln_a_g"][0]),
                           chunkcol(inp["ln_a_b"][0]), chunkcol(inp["q_norm"][0]), chunkcol(inp["kv_norm"][0]),
                           cwv], axis=1).astype(np.float32)
    return {
        "xT": xT, "rope": np.ascontiguousarray(rope), "vecs": np.ascontiguousarray(vecs),
        "w_in_e": np.ascontiguousarray(inp["w_in_e"][0]), "w_uq": np.ascontiguousarray(inp["w_uq"][0]),
        "w_ukv": np.ascontiguousarray(inp["w_ukv"][0]), "w_out_e": np.ascontiguousarray(inp["w_out_e"][0]),
    }


def l1_consts():
    k = np.arange(128)
    U1 = (k[:, None] <= k[None, :]).astype(np.float32)
    U2 = (k[:, None] >= k[None, :]).astype(np.float32)
    M1 = np.where(k[None, :] >= k[:, None], 0.0, -30000.0).astype(np.float32)
    M2 = np.where(k[None, :] <= k[:, None], 0.0, -30000.0).astype(np.float32)
    I = np.eye(128, dtype=np.float32)
    return np.ascontiguousarray(np.concatenate([U1, U2, M1, M2, I], axis=1))


def prep_l1(inp, b, hh):
    w_in = np.array(inp["w_in_o"][0], np.float32, copy=True)
    dbf, dbb = inp["dt_bias_f"][0], inp["dt_bias_b"][0]
    alf, alb = inp["a_log_f"][0], inp["a_log_b"][0]
    ccw = inp["conv_c_w"][0]
    cdw = inp["conv_d_w"][0]
    if hh == 1:
        w_in[:, 4096:4120], w_in[:, 4120:4144] = inp["w_in_o"][0][:, 4120:4144], inp["w_in_o"][0][:, 4096:4120]
        dbf, dbb = dbb, dbf
        alf, alb = alb, alf
        ccw = ccw[::-1]
        cdw = cdw[::-1]
    rows = np.concatenate([dbf, dbb, alf, alb, inp["d_skip"][0], inp["ssd_norm"][0]]).astype(np.float32)
    rows = np.ascontiguousarray(np.broadcast_to(rows[None, :], (128, rows.shape[0])))
    ccwv = ccw.T.reshape(20, 128, 5).transpose(1, 0, 2).reshape(128, 100)
    cdwv = cdw.T.reshape(4, 128, 3).transpose(1, 0, 2).reshape(128, 12)
    m = np.zeros((128, 2), np.float32)
    m[:, hh] = 1.0
    vecs = np.concatenate([chunkcol(inp["norm_o"][0]), ccwv, chunkcol(inp["conv_c_b"][0]), cdwv,
                           chunkcol(inp["final_norm"]), m], axis=1).astype(np.float32)
    return {"w_in_o": np.ascontiguousarray(w_in), "w_out_o": np.ascontiguousarray(inp["w_out_o"][0]),
            "vecs_o": np.ascontiguousarray(vecs), "rows_o": rows, "consts": l1_consts()}


def l1_drams(nc, T, dr):
    dr["w_in_o"] = nc.dram_tensor("w_in_o", [1024, O_IN], F32, kind="ExternalInput").ap()
    dr["w_out_o"] = nc.dram_tensor("w_out_o", [2048, 1024], F32, kind="ExternalInput").ap()
    dr["vecs_o"] = nc.dram_tensor("vecs_o", [128, NV_O], F32, kind="ExternalInput").ap()
    dr["rows_o"] = nc.dram_tensor("rows_o", [128, NR_O], F32, kind="ExternalInput").ap()
    if "consts" not in dr:
        dr["consts"] = nc.dram_tensor("consts", [128, NCONST], F32, kind="ExternalInput").ap()
    dr["xnT"] = nc.dram_tensor("xnT", [1024, T], BF16).ap()
    dr["dtT"] = nc.dram_tensor("dtT", [T, 48], F32).ap()
    dr["xsT"] = nc.dram_tensor("xsT", [T, 1536], F32).ap()
    dr["Bts"] = nc.dram_tensor("Bts", [T, 512], BF16).ap()
    dr["BTs"] = nc.dram_tensor("BTs", [512, T], BF16).ap()
    dr["CTs"] = nc.dram_tensor("CTs", [512, T], BF16).ap()
    dr["odT"] = nc.dram_tensor("odT", [512, T], BF16).ap()
    dr["y1"] = nc.dram_tensor("y1", [T, 1536], F32).ap()
    dr["dgc"] = nc.dram_tensor("dgc", [20, 128, 5 * 128], BF16).ap()
    dr["cc_in2"] = nc.dram_tensor("cc_in2", [128, 1536], F32).ap()
    dr["cc_out2"] = nc.dram_tensor("cc_out2", [256, 1536], F32).ap()
    dr["outT"] = nc.dram_tensor("outT", [1024, T], F32, kind="ExternalOutput").ap()


def l0_drams(nc, S, dr):
    T = S // 2
    dr["xT"] = nc.dram_tensor("xT", [1024, 16 + S + 16], F32, kind="ExternalInput").ap()
    dr["rope"] = nc.dram_tensor("rope", [2, 32, S], F32, kind="ExternalInput").ap()
    dr["vecs"] = nc.dram_tensor("vecs", [128, NV_E], F32, kind="ExternalInput").ap()
    dr["w_in"] = nc.dram_tensor("w_in_e", [1024, E_IN], F32, kind="ExternalInput").ap()
    dr["w_uq"] = nc.dram_tensor("w_uq", [384, 1536], F32, kind="ExternalInput").ap()
    dr["w_ukv"] = nc.dram_tensor("w_ukv", [256, 2048], F32, kind="ExternalInput").ap()
    dr["w_out"] = nc.dram_tensor("w_out_e", [2048, 1024], F32, kind="ExternalInput").ap()
    dr["qn"] = nc.dram_tensor("qn", [384, T], BF16).ap()
    dr["gb"] = nc.dram_tensor("gb", [1024, T], BF16).ap()
    dr["oa"] = nc.dram_tensor("oa", [1024, T], BF16).ap()
    dr["ob"] = nc.dram_tensor("ob", [16, 64, T], BF16).ap()
    dr["dg"] = nc.dram_tensor("dg", [8, 128, 31 * 128], BF16).ap()


def build_fused(S, pairs):
    T = S // 2
    nc = bass.Bass("TRN2", target_bir_lowering=False)
    dr = {}
    l0_drams(nc, S, dr)
    dr["consts"] = nc.dram_tensor("consts", [128, NCONST], F32, kind="ExternalInput").ap()
    dr["x1T"] = nc.dram_tensor("x1T", [1024, T + 4], F32).ap()
    dr["x1off"] = 2
    dr["cc_in1"] = nc.dram_tensor("cc_in1", [128, 16], F32).ap()
    dr["cc_out1"] = nc.dram_tensor("cc_out1", [256, 16], F32).ap()
    l1_drams(nc, T, dr)
    A = Arena(nc)
    PS = [nc.alloc_psum_tensor("ps%d" % i, [128, 512], F32).ap() for i in range(8)]
    P = Prog(nc)
    dr["w_out_e_bf"] = nc.dram_tensor("w_out_e_bf", [2048, 1024], BF16).ap()
    dr["w_in_o_bf"] = nc.dram_tensor("w_in_o_bf", [1024, O_IN], BF16).ap()
    dr["w_out_o_bf"] = nc.dram_tensor("w_out_o_bf", [2048, 1024], BF16).ap()
    fin0 = build_l0(nc, P, A, PS, S, dr)
    P.barrier()
    x1T3 = dr["x1T"].rearrange("(k p) t -> p k t", p=128)
    B = Bump(A, 0)
    mvec = B.get([128, 2], F32)
    zer = B.get([128, 8, 2], F32)
    hs = B.get([128, 8, 2], F32)
    g1 = B.get([128, 2, 16], F32)
    sel = B.get([128, 8, 2], F32)
    halo = B.get([128, 8, 2], F32)
    P.add("sp", lambda e: e.dma_start(out=mvec, in_=dr["vecs_o"][:, VO_M0:VO_M0 + 2]), w=["mvec"], dma=True)
    P.add("dve", lambda e: e.memset(zer, 0.0), w=["zer"])
    P.add("sp", lambda e: e.dma_start(out=x1T3[:, :, 0:2], in_=zer), r=["zer"], w=["x1zero"], dma=True)
    P.add("sp", lambda e: e.dma_start(out=hs, in_=x1T3[:, :, T:T + 2]), r=fin0, w=["hs"], dma=True)
    P.add("sp", lambda e: e.dma_start(out=dr["cc_in1"].rearrange("p (k c) -> p k c", c=2), in_=hs), r=["hs"], w=["cc_in1"], dma=True)
    P.add("pool", lambda e: e.collective_compute("AllGather", ALU.bypass, replica_groups=pairs,
                                                 ins=[dr["cc_in1"]], outs=[dr["cc_out1"]]),
          r=["cc_in1"], w=["cc_out1"], dma=True, inc=1)
    P.add("sp", lambda e: e.dma_start(out=g1, in_=dr["cc_out1"].rearrange("(r p) c -> p r c", r=2)), r=["cc_out1"], w=["g1"], dma=True)
    g1v = g1.rearrange("p r (k c) -> p r k c", c=2)
    P.add("dve", lambda e: e.tensor_scalar(sel, g1v[:, 0, :, :], mvec[:, 1:2], None, ALU.mult), r=["g1", "mvec"], w=["sel"])
    P.add("dve", lambda e: e.scalar_tensor_tensor(sel, g1v[:, 1, :, :], mvec[:, 0:1], sel, ALU.mult, ALU.add), r=["g1", "mvec", "sel"], w=["sel"])
    P.add("dve", lambda e: e.tensor_copy(halo[:, :, 0:1], sel[:, :, 1:2]), r=["sel"], w=["halo"])
    P.add("dve", lambda e: e.tensor_copy(halo[:, :, 1:2], sel[:, :, 0:1]), r=["sel", "halo"], w=["halo"])
    P.add("sp", lambda e: e.dma_start(out=x1T3[:, :, T + 2:T + 4], in_=halo), r=["halo"], w=["x1halo"], dma=True)
    P.barrier()
    L = L1(nc, P, A, PS, S, dr)
    L.pairs = pairs
    L.prep_and_pass1()
    L.exchange_states()
    fin = L.pass2_and_out()
    P.add("sp", None, r=fin)
    P.emit()
    return nc, len(P.ops)


def kernel_impl(inputs, S, B, run):
    T = S // 2
    ncores = 2 * B
    pairs = [[2 * i, 2 * i + 1] for i in range(B)]
    nc, nops = build_fused(S, pairs)
    in_maps = []
    for core in range(ncores):
        b, hh = core // 2, core % 2
        m = prep_l0(inputs, b, hh, S)
        m.update(prep_l1(inputs, b, hh))
        in_maps.append(m)
    results = run(nc, in_maps, ncores)
    out = np.empty((B, S, 1024), np.float32)
    for core in range(ncores):
        b, hh = core // 2, core % 2
        o = np.asarray(results[core]["outT"]).T
        if hh == 0:
            out[b, :T] = o
        else:
            out[b, T:] = o[::-1]
    return out


def _run_hw(nc, in_maps, n):
    return run_bass_kernel_spmd(nc, in_maps, core_ids=list(range(n))).results


def kernel(**inputs):
    inputs = {k: np.asarray(v) for k, v in inputs.items()}
    Bn, S, _ = inputs["x"].shape
    return kernel_impl(inputs, S, Bn, _run_hw)
```

```python
import numpy as np
from contextlib import ExitStack
import concourse.bass as bass
import concourse.mybir as mybir
from concourse.bass_utils import run_bass_kernel_spmd

F32 = mybir.dt.float32
BF16 = mybir.dt.bfloat16
AF = mybir.ActivationFunctionType
ALU = mybir.AluOpType
AX = mybir.AxisListType

NDMASEM = 40


class Op:
    __slots__ = ("id", "eng", "fn", "deps", "dom", "sig", "sigval", "dma", "inc")


class Prog:
    ENGS = ["pe", "act", "dve", "pool", "sp"]

    def __init__(self, nc):
        self.nc = nc
        self.ops = []
        self.lastw = {}
        self.readers = {}
        self.ndma = 0
        self.ndma_eng = {}
        self.slot_last = {}
        self.last_on_eng = {}

    def add(self, eng, fn, r=(), w=(), dma=False, extra_deps=(), inc=None):
        op = Op()
        op.id = len(self.ops)
        op.eng = eng
        op.fn = fn
        op.dma = dma
        op.sig = False
        op.inc = inc if inc is not None else (16 if dma else 1)
        op.sigval = 0
        deps = set(extra_deps)
        for k in r:
            lw = self.lastw.get(k)
            if lw is not None:
                deps.add(lw)
        for k in w:
            lw = self.lastw.get(k)
            if lw is not None:
                deps.add(lw)
            for rd in self.readers.get(k, ()):
                deps.add(rd)
        if dma:
            if inc is not None and inc != 16:
                slot = ("cc", self.ndma)
                self.ndma += 1
            else:
                n = self.ndma_eng.get(eng, 0)
                self.ndma_eng[eng] = n + 1
                slot = (eng, n % (24 if eng == "sp" else 12))
            op.dom = ("dma", slot)
            prev = self.slot_last.get(slot)
            if prev is not None:
                deps.add(prev)
            self.slot_last[slot] = op.id
        else:
            op.dom = eng
        deps.discard(op.id)
        if eng == "pe" and not dma:
            deps = {d for d in deps if self.ops[d].dma or self.ops[d].eng != "pe"}
        op.deps = deps
        for k in r:
            self.readers.setdefault(k, []).append(op.id)
        for k in w:
            self.lastw[k] = op.id
            self.readers[k] = []
        self.ops.append(op)
        if fn is not None:
            self.last_on_eng[eng] = op.id
        return op.id

    def barrier(self):
        lasts = [v for v in self.last_on_eng.values()]
        lasts += list(self.slot_last.values())
        for e in self.ENGS:
            self.add(e, None, extra_deps=lasts)

    def emit(self):
        nc = self.nc
        for op in self.ops:
            for d in op.deps:
                self.ops[d].sig = True
        cnt = {}
        for op in self.ops:
            if op.sig:
                cnt[op.dom] = cnt.get(op.dom, 0) + op.inc
                op.sigval = cnt[op.dom]
        doms = sorted(cnt.keys(), key=str)
        with ExitStack() as st:
            sems = {}
            for i, d in enumerate(doms):
                sems[d] = st.enter_context(nc.semaphore("s%d" % i))
            block = st.enter_context(nc.Block())
            ops = self.ops

            def run(engname, e):
                known = {}
                for op in ops:
                    if op.eng != engname:
                        continue
                    need = {}
                    for d in op.deps:
                        o = ops[d]
                        if need.get(o.dom, 0) < o.sigval:
                            need[o.dom] = o.sigval
                    for dom, v in need.items():
                        if known.get(dom, 0) < v:
                            e.wait_ge(sems[dom], v)
                            known[dom] = v
                    if op.fn is not None:
                        ins = op.fn(e)
                        if op.sig:
                            ins.then_inc(sems[op.dom], op.inc)
                    elif op.sig:
                        raise RuntimeError("fn None op cannot signal")

            @block.sync
            def _(e):
                run("sp", e)

            @block.tensor
            def _(e):
                run("pe", e)

            @block.scalar
            def _(e):
                run("act", e)

            @block.vector
            def _(e):
                run("dve", e)

            @block.gpsimd
            def _(e):
                run("pool", e)


U8 = mybir.dt.uint8
ARENA_BYTES = 206 * 1024


class Arena:
    def __init__(self, nc, nbytes=ARENA_BYTES):
        self.ap = nc.alloc_sbuf_tensor("arena", [128, nbytes], U8).ap()
        self.nbytes = nbytes

    def view(self, off, shape, dt, p0=0):
        esz = 4 if dt == F32 else 2
        n = 1
        for s in shape[1:]:
            n *= s
        assert off % 4 == 0 and off + n * esz <= self.nbytes, (off, n, esz)
        v = self.ap[p0:p0 + shape[0], off:off + n * esz].bitcast(dt)
        if len(shape) == 3:
            v = v.rearrange("p (a b) -> p a b", a=shape[1])
        elif len(shape) == 4:
            v = v.rearrange("p (a b c) -> p a b c", a=shape[1], b=shape[2])
        return v


EPS = 1e-6
D = 1024
E_IN = 4768
NV_E = 8 + 8 + 8 + 8 + 3 + 2 + 8 * 31


class Bump:
    def __init__(self, A, start=0):
        self.A = A
        self.off = start

    def get(self, shape, dt, p0=0):
        esz = 4 if dt == F32 else 2
        n = 1
        for s in shape[1:]:
            n *= s
        nb = (n * esz + 31) // 32 * 32
        v = self.A.view(self.off, shape, dt, p0=p0)
        self.off += nb
        return v


def rmsnorm_tile(P, pfx, x3, nk, N, gcol, ones_bf, sq3, ss_ps, rstd, out3, inv_n, x_keys, out_key, x_is_psum=False, ssk=None, epsc=None):
    ssk = ssk if ssk is not None else (pfx, "ss")
    for k in range(nk):
        xin = x3[k] if x_is_psum else x3[:, k, :]
        P.add("act", lambda e, o=sq3[:, k, :], i=xin: e.activation(o, i, AF.Square),
              r=[x_keys[k]], w=[(pfx, "sq", k)])
    for k in range(nk):
        P.add("pe", lambda e, k=k: e.matmul(ss_ps, ones_bf, sq3[:, k, :], start=(k == 0), stop=(k == nk - 1)),
              r=[(pfx, "sq", k), "ones"], w=[ssk])
    P.add("act", lambda e: e.activation(rstd, ss_ps, AF.Ln, bias=epsc, scale=inv_n),
          r=[ssk, "epsc"], w=[(pfx, "rstd")])
    P.add("act", lambda e: e.activation(rstd, rstd, AF.Exp, scale=-0.5),
          r=[(pfx, "rstd")], w=[(pfx, "rstd")])
    for k in range(nk):
        xin = x3[k] if x_is_psum else x3[:, k, :]
        eng = "dve"
        P.add(eng, lambda e, o=out3[:, k, :], i=xin, g=gcol(k): e.scalar_tensor_tensor(
            o, i, g, rstd, ALU.mult, ALU.mult),
            r=[x_keys[k], (pfx, "rstd"), "vecs"], w=[(out_key, k)])


def build_l0(nc, P, A, PS, S, dr):
    T = S // 2
    NT2 = 2 * T // 512
    TT = 256
    NTT = T // TT
    HW = 15
    TW = TT + 2 * HW
    QT = 512
    NQT = T // QT
    NKT = 2 * T // 128
    xT = dr["xT"]
    xT3 = xT.rearrange("(k p) t -> p k t", p=128)

    B = Bump(A, 0)
    vecs = B.get([128, NV_E], F32)
    ones_bf = B.get([128, 128], BF16)
    ones_f = B.get([128, 64], F32)
    epsc = B.get([128, 1], F32)
    small_end = B.off
    kv_n = B.get([128, 2, 2 * T], BF16)
    kro_off = B.off
    kro = B.get([128, 2 * T], BF16)
    pers_end = B.off
    V_NE, V_CB, V_LG, V_LB, V_QG, V_KG, V_CW = 0, 8, 16, 24, 32, 35, 37

    P.add("sp", lambda e: e.dma_start(out=vecs, in_=dr["vecs"]), w=["vecs"], dma=True)
    P.add("dve", lambda e: e.memset(ones_bf, 1.0), w=["ones"])
    P.add("dve", lambda e: e.memset(ones_f, 1.0), w=["ones_f"])
    P.add("dve", lambda e: e.memset(epsc, EPS), w=["epsc"])

    B = Bump(A, pers_end)
    WINC = E_IN + 96
    w_in_bf = B.get([128, 8, WINC], BF16)
    ph12_base = B.off
    stg = [B.get([128, E_IN], F32) for _ in range(2)]
    w_in3 = dr["w_in"].rearrange("(k p) c -> p k c", p=128)
    for k in range(8):
        s = stg[k % 2]
        P.add("sp", lambda e, s=s, k=k: e.dma_start(out=s, in_=w_in3[:, k, :]), w=[("stg", k % 2)], dma=True)
        eng = "dve"
        P.add(eng, lambda e, s=s, k=k: e.tensor_copy(w_in_bf[:, k, 0:E_IN], s), r=[("stg", k % 2)], w=[("winbf", k)])
        P.add(eng, lambda e, s=s, k=k: e.tensor_copy(w_in_bf[:, k, E_IN:E_IN + 64], s[:, 576:640]),
              r=[("stg", k % 2)], w=[("winbf", k)])
        P.add(eng, lambda e, s=s, k=k: e.tensor_copy(w_in_bf[:, k, E_IN + 64:E_IN + 80], s[:, 656:672]),
              r=[("stg", k % 2)], w=[("winbf", k)])
        P.add(eng, lambda e, s=s, k=k: e.tensor_copy(w_in_bf[:, k, E_IN + 80:E_IN + 96], s[:, 640:656]),
              r=[("stg", k % 2)], w=[("winbf", k)])
    winkeys = [("winbf", k) for k in range(8)]
    idf = B.get([128, 128], F32)
    idb = B.get([128, 128], BF16)
    dgs = [B.get([128, 31, 128], BF16) for _ in range(2)]
    assert B.off <= A.nbytes, B.off
    P.add("sp", lambda e: e.dma_start(out=idf, in_=dr["consts"][:, 512:640]), w=["idf"], dma=True)
    P.add("dve", lambda e: e.tensor_copy(idb, idf), r=["idf"], w=["idb"])
    for c in range(8):
        d = dgs[c % 2]
        for i in range(31):
            eng = "dve"
            P.add(eng, lambda e, d=d, c=c, i=i: e.tensor_scalar(d[:, i, :], idb, vecs[:, V_CW + c * 31 + i:V_CW + c * 31 + i + 1],
                                                               None, ALU.mult), r=["idb", "vecs"], w=[("dgs", c % 2, i % 2)])
        P.add("sp", lambda e, d=d, c=c: e.dma_start(out=dr["dg"][c].rearrange("p (i m) -> p i m", i=31), in_=d),
              r=[("dgs", c % 2, 0), ("dgs", c % 2, 1)], w=[("dg", c)], dma=True)

    P.barrier()
    B = Bump(A, ph12_base)
    xt = [B.get([128, 8, 512], F32) for _ in range(2)]
    sq = B.get([128, 8, 512], BF16)
    xn = B.get([128, 8, 512], BF16)
    rstd = B.get([128, 512], F32)
    rstd2 = B.get([128, 512], F32)
    sq2 = B.get([128, 2, 512], BF16)
    cs = B.get([32, 512], F32, p0=64)
    sn = B.get([32, 512], F32, p0=64)
    tmp1 = B.get([32, 512], F32, p0=64)
    tmp2 = B.get([32, 512], F32, p0=64)
    ss_ps, kv0_ps, kv1_ps, kr_ps, krr_ps, ss2_ps = PS[0], PS[1], PS[2], PS[3], PS[4], PS[5]
    for j in range(NT2):
        t0 = 512 * j
        xb = xt[j % 2]
        P.add("sp", lambda e, xb=xb, t0=t0: e.dma_start(out=xb, in_=xT3[:, :, 16 + t0:16 + t0 + 512]),
              w=[("xt", j % 2, k) for k in range(8)], dma=True)
        P.add("sp", lambda e, t0=t0: e.dma_start(out=cs, in_=dr["rope"][0, :, t0:t0 + 512]), w=["cs"], dma=True)
        P.add("sp", lambda e, t0=t0: e.dma_start(out=sn, in_=dr["rope"][1, :, t0:t0 + 512]), w=["sn"], dma=True)
        rmsnorm_tile(P, "p1", xb, 8, 512, lambda k: vecs[:, V_NE + k:V_NE + k + 1], ones_bf, sq, ss_ps, rstd, xn,
                     1.0 / D, [("xt", j % 2, k) for k in range(8)], "xn", ssk=("PS", 0), epsc=epsc)
        outs = [(kv0_ps, 384, 128, ("PS", 1)), (kv1_ps, 512, 128, ("PS", 2)), (kr_ps[0:96, :], 576, 96, ("PS", 3)),
                (krr_ps[0:96, :], E_IN, 96, ("PS", 4))]
        for (ps, c0, m, nm) in outs:
            for k in range(8):
                P.add("pe", lambda e, ps=ps, c0=c0, m=m, k=k: e.matmul(
                    ps, w_in_bf[:, k, c0:c0 + m], xn[:, k, :], start=(k == 0), stop=(k == 7)),
                    r=[("xn", k), ("winbf", k)], w=[nm])
        rmsnorm_tile(P, "p1b", [kv0_ps, kv1_ps], 2, 512, lambda k: vecs[:, V_KG + k:V_KG + k + 1], ones_bf, sq2,
                     ss2_ps, rstd2, kv_n[:, :, t0:t0 + 512], 1.0 / 256, [("PS", 1), ("PS", 2)], ("kvn", j), x_is_psum=True,
                     ssk=("PS", 5), epsc=epsc)
        P.add("dve", lambda e: e.tensor_tensor(tmp1, kr_ps[64:96, :], cs, ALU.mult), r=[("PS", 3), "cs"], w=["tmp1"])
        P.add("dve", lambda e: e.tensor_tensor(tmp2, krr_ps[64:96, :], sn, ALU.mult), r=[("PS", 4), "sn"], w=["tmp2"])
        P.add("dve", lambda e, t0=t0: e.tensor_tensor(kro[64:96, t0:t0 + 512], tmp1, tmp2, ALU.add),
              r=["tmp1", "tmp2"], w=[("kro", j)])

    P.barrier()
    B = Bump(A, ph12_base)
    xt_p2 = [B.get([128, 8, TW], F32)] * 2
    sq_p2 = B.get([128, 8, TW], BF16)
    xn_p2 = B.get([128, 8, TW], BF16)
    rstd_p2 = B.get([128, TW], F32)
    rstdq = B.get([128, TT], F32)
    sqq = B.get([128, 3, TT], BF16)
    qn_t = B.get([128, 3, TT], BF16)
    th = B.get([128, TW], F32)
    gb_t = B.get([128, 8, TT], BF16)
    u = B.get([128, 8, TW], BF16)
    dgb = [B.get([128, 31, 128], BF16) for _ in range(2)]
    ga = B.get([128, TW], F32)
    acc = B.get([128, 8, TT], F32)
    accb = B.get([128, 8, TT], BF16)
    sqa = sq_p2[:, :, 0:TT]
    mean = B.get([128, TT], F32)
    var = B.get([128, TT], F32)
    za = B.get([128, TT], F32)
    oa_t = B.get([128, 8, TT], BF16)
    assert B.off <= A.nbytes, B.off
    for j in range(NTT):
        t0 = TT * j
        xb = xt_p2[j % 2]
        xk = [("xt2", 0, k) for k in range(8)]
        P.add("sp", lambda e, xb=xb, t0=t0: e.dma_start(out=xb, in_=xT3[:, :, 16 + t0 - HW:16 + t0 - HW + TW]),
              w=xk, dma=True)
        rmsnorm_tile(P, "p2", xb, 8, TW, lambda k: vecs[:, V_NE + k:V_NE + k + 1], ones_bf, sq_p2, PS[0][:, 0:TW],
                     rstd_p2, xn_p2, 1.0 / D, xk, "xn2", ssk=("PS", 0), epsc=epsc)
        xnc = xn_p2[:, :, HW:HW + TT]
        for c in range(3):
            for k in range(8):
                P.add("pe", lambda e, c=c, k=k: e.matmul(PS[1 + c][:, 0:TT], w_in_bf[:, k, c * 128:(c + 1) * 128],
                                                        xnc[:, k, :], start=(k == 0), stop=(k == 7)),
                      r=[("xn2", k), ("winbf", k)], w=[("PS", 1 + c)])
        rmsnorm_tile(P, "p2q", [PS[1][:, 0:TT], PS[2][:, 0:TT], PS[3][:, 0:TT]], 3, TT,
                     lambda k: vecs[:, V_QG + k:V_QG + k + 1], ones_bf, sqq, PS[4][:, 0:TT], rstdq, qn_t,
                     1.0 / 384, [("PS", 1 + c) for c in range(3)], "qn_t", x_is_psum=True, ssk=("PS", 4), epsc=epsc)
        P.add("pool", lambda e, t0=t0: e.dma_start(out=dr["qn"].rearrange("(c p) t -> p c t", p=128)[:, :, t0:t0 + TT],
                                                   in_=qn_t),
              r=[("qn_t", c) for c in range(3)], w=[("qn_d", t0 // QT)], dma=True)
        for c in range(8):
            ps = PS[5 + (c % 2)][:, 0:TT]
            pk = ("PS", 5 + c % 2)
            for k in range(8):
                P.add("pe", lambda e, ps=ps, c=c, k=k: e.matmul(ps, w_in_bf[:, k, 672 + c * 128:672 + (c + 1) * 128],
                                                               xnc[:, k, :], start=(k == 0), stop=(k == 7)),
                      r=[("xn2", k), ("winbf", k)], w=[pk])
            P.add("act", lambda e, ps=ps: e.activation(th[:, 0:TT], ps, AF.Tanh, scale=0.5), r=[pk], w=["th"])
            P.add("dve", lambda e, ps=ps, c=c: e.scalar_tensor_tensor(gb_t[:, c, :], th[:, 0:TT], 1.0, ps, ALU.add, ALU.mult),
                  r=["th", pk], w=[("gb_t", c)])
        P.add("pool", lambda e, t0=t0: e.dma_start(out=dr["gb"].rearrange("(c p) t -> p c t", p=128)[:, :, t0:t0 + TT],
                                                   in_=gb_t),
              r=[("gb_t", c) for c in range(8)], w=[("gb_d", t0 // QT)], dma=True)
        for c in range(8):
            psa = PS[5][:, 0:TW]
            psg = PS[6][:, 0:TW]
            for k in range(8):
                P.add("pe", lambda e, c=c, k=k: e.matmul(psa, w_in_bf[:, k, 1696 + c * 128:1696 + (c + 1) * 128],
                                                        xn_p2[:, k, :], start=(k == 0), stop=(k == 7)),
                      r=[("xn2", k), ("winbf", k)], w=[("PS", 5)])
            for k in range(8):
                P.add("pe", lambda e, c=c, k=k: e.matmul(psg, w_in_bf[:, k, 2720 + c * 128:2720 + (c + 1) * 128],
                                                        xn_p2[:, k, :], start=(k == 0), stop=(k == 7)),
                      r=[("xn2", k), ("winbf", k)], w=[("PS", 6)])
            P.add("act", lambda e: e.activation(th, psg, AF.Tanh, scale=0.5), r=[("PS", 6)], w=["th"])
            P.add("act", lambda e: e.mul(ga, psa, 0.5), r=[("PS", 5)], w=["ga"])
            P.add("dve", lambda e, c=c: e.scalar_tensor_tensor(u[:, c, :], th, 1.0, ga, ALU.add, ALU.mult),
                  r=["th", "ga"], w=[("u", c)])
        for c in range(8):
            d = dgb[c % 2]
            P.add("sp", lambda e, d=d, c=c: e.dma_start(out=d, in_=dr["dg"][c].rearrange("p (i m) -> p i m", i=31)),
                  r=[("dg", c)], w=[("dgb", c % 2)], dma=True)
            cps = (PS[7] if c % 2 == 0 else PS[0])[:, 0:TT]
            cpk = ("PS", 7) if c % 2 == 0 else ("PS", 0)
            for i in range(31):
                P.add("pe", lambda e, d=d, c=c, i=i, cps=cps: e.matmul(cps, d[:, i, :], u[:, c, i:i + TT], start=(i == 0), stop=(i == 30)),
                      r=[("dgb", c % 2), ("u", c)], w=[cpk])
            P.add("dve", lambda e, c=c, cps=cps: e.tensor_scalar(acc[:, c, :], cps, vecs[:, V_CB + c:V_CB + c + 1], None, ALU.add),
                  r=[cpk, "vecs"], w=[("acc", c)])
        for c in range(8):
            P.add("act", lambda e, c=c: e.activation(accb[:, c, :], acc[:, c, :], AF.Copy), r=[("acc", c)], w=[("accb", c)])
            P.add("act", lambda e, c=c: e.activation(sqa[:, c, :], acc[:, c, :], AF.Square), r=[("acc", c)], w=[("p2", "sq", c)])
        for c in range(8):
            P.add("pe", lambda e, c=c: e.matmul(PS[1][:, 0:TT], ones_bf, accb[:, c, :], start=(c == 0), stop=(c == 7)),
                  r=[("accb", c), "ones"], w=[("PS", 1)])
        for c in range(8):
            P.add("pe", lambda e, c=c: e.matmul(PS[2][:, 0:TT], ones_bf, sqa[:, c, :], start=(c == 0), stop=(c == 7)),
                  r=[("p2", "sq", c), "ones"], w=[("PS", 2)])
        P.add("dve", lambda e: e.tensor_scalar(mean, PS[1][:, 0:TT], 1.0 / 1024, None, ALU.mult), r=[("PS", 1)], w=["mean"])
        P.add("dve", lambda e: e.tensor_tensor(var, mean, mean, ALU.mult), r=["mean"], w=["var"])
        P.add("dve", lambda e: e.scalar_tensor_tensor(var, PS[2][:, 0:TT], 1.0 / 1024, var, ALU.mult, ALU.subtract),
              r=[("PS", 2), "var"], w=["var"])
        P.add("act", lambda e: e.activation(var, var, AF.Ln, bias=epsc), r=["var", "epsc"], w=["var"])
        P.add("act", lambda e: e.activation(var, var, AF.Exp, scale=-0.5), r=["var"], w=["var"])
        for c in range(8):
            eng = "dve"
            P.add(eng, lambda e, c=c: e.tensor_tensor(acc[:, c, :], acc[:, c, :], mean, ALU.subtract),
                  r=[("acc", c), "mean"], w=[("acc", c)])
            P.add(eng, lambda e, c=c: e.scalar_tensor_tensor(acc[:, c, :], acc[:, c, :], vecs[:, V_LG + c:V_LG + c + 1],
                                                             var, ALU.mult, ALU.mult),
                  r=[("acc", c), "var", "vecs"], w=[("acc", c)])
            P.add(eng, lambda e, c=c: e.tensor_scalar(acc[:, c, :], acc[:, c, :], vecs[:, V_LB + c:V_LB + c + 1], None, ALU.add),
                  r=[("acc", c)], w=[("acc", c)])
            pz = PS[3][:, 0:TT] if c % 2 == 0 else PS[4][:, 0:TT]
            pzk = ("PS", 3) if c % 2 == 0 else ("PS", 4)
            for k in range(8):
                P.add("pe", lambda e, pz=pz, c=c, k=k: e.matmul(pz, w_in_bf[:, k, 3744 + c * 128:3744 + (c + 1) * 128],
                                                               xnc[:, k, :], start=(k == 0), stop=(k == 7)),
                      r=[("xn2", k), ("winbf", k)], w=[pzk])
            P.add("act", lambda e, pz=pz: e.activation(th[:, 0:TT], pz, AF.Tanh, scale=0.5), r=[pzk], w=["th"])
            P.add("dve", lambda e, pz=pz: e.scalar_tensor_tensor(za, th[:, 0:TT], 1.0, pz, ALU.add, ALU.mult),
                  r=["th", pzk], w=["za"])
            P.add("act", lambda e, c=c: e.activation(th[:, 0:TT], acc[:, c, :], AF.Tanh, scale=0.5), r=[("acc", c)], w=["th"])
            P.add("dve", lambda e, c=c: e.scalar_tensor_tensor(mean if False else ga[:, 0:TT], th[:, 0:TT], 1.0, acc[:, c, :], ALU.add, ALU.mult),
                  r=["th", ("acc", c)], w=["ga"])
            P.add("dve", lambda e, c=c: e.scalar_tensor_tensor(oa_t[:, c, :], ga[:, 0:TT], 0.25, za, ALU.mult, ALU.mult),
                  r=["ga", "za"], w=[("oa_t", c)])
        P.add("pool", lambda e, t0=t0: e.dma_start(out=dr["oa"].rearrange("(c p) t -> p c t", p=128)[:, :, t0:t0 + TT],
                                                   in_=oa_t),
              r=[("oa_t", c) for c in range(8)], w=[("oa_d", t0 // QT)], dma=True)
    P.barrier()

    B = Bump(A, pers_end)
    KT = B.get([96, 4, 2 * T], BF16)
    Vg = B.get([128, NKT, 4, 65], BF16)
    wuq = B.get([128, 3, 2 * 1536], BF16)
    wukv = B.get([128, 2, 2048], BF16)
    pT = [B.get([128, 512], BF16) for _ in range(4)]
    QTb = [B.get([96, 512], BF16) for _ in range(2)]
    qn_q = [B.get([128, 3, 512], BF16) for _ in range(2)]
    csq = [B.get([32, 512], F32, p0=64) for _ in range(2)]
    snq = [B.get([32, 512], F32, p0=64) for _ in range(2)]
    stg3 = A.view(B.off - 4 * 2048, [128, 2048], F32)
    t1 = B.get([32, 512], F32, p0=64)
    t2 = B.get([32, 512], F32, p0=64)
    rc = B.get([1, 512], F32, p0=64)
    cst_f = [B.get([128, 516], F32) for _ in range(2)]
    cst_b = [B.get([128, 516], BF16) for _ in range(2)]
    B2 = Bump(A, kro_off) if 4 * T >= 14336 else B
    gb_q = [B2.get([64, 4, 512], BF16) for _ in range(2)]
    rb = B2.get([64, 512], F32)
    of = B2.get([64, 512], F32)
    ob_t = [B2.get([64, 512], BF16) for _ in range(2)]
    assert B2 is B or B2.off <= kro_off + 4 * T, (B2.off, kro_off)
    assert B.off <= A.nbytes, B.off
    wuq3 = dr["w_uq"].rearrange("(c p) n -> p c n", p=128)
    for c in range(3):
        P.add("sp", lambda e, c=c: e.dma_start(out=stg3[:, 0:1536], in_=wuq3[:, c, :]), w=["stg3"], dma=True)
        P.add("dve", lambda e, c=c: e.tensor_copy(wuq[:, c, 0:1536], stg3[:, 0:1536]), r=["stg3"], w=["wuq"])
        P.add("act", lambda e, c=c: e.activation(wuq[:, c, 1536:3072], stg3[:, 0:1536], AF.Copy), r=["stg3"], w=["wuqr"])
        s4 = stg3[:, 0:1536].rearrange("p (h d) -> p h d", h=16)
        d4 = wuq[:, c, 1536:3072].rearrange("p (h d) -> p h d", h=16)
        P.add("dve", lambda e, s4=s4, d4=d4: e.tensor_copy(d4[:, :, 64:80], s4[:, :, 80:96]), r=["stg3"], w=["wuqr"])
        P.add("dve", lambda e, s4=s4, d4=d4: e.tensor_copy(d4[:, :, 80:96], s4[:, :, 64:80]), r=["stg3"], w=["wuqr"])
    wukv3 = dr["w_ukv"].rearrange("(c p) n -> p c n", p=128)
    for c in range(2):
        P.add("sp", lambda e, c=c: e.dma_start(out=stg3, in_=wukv3[:, c, :]), w=["stg3"], dma=True)
        P.add("dve", lambda e, c=c: e.tensor_copy(wukv[:, c, :], stg3), r=["stg3"], w=["wukv"])
    P.add("dve", lambda e: e.memset(Vg.rearrange("p k h d -> p (k h) d")[:, :, 64:65], 1.0), w=["Vones"])
    P.barrier()
    cast_jobs = []
    if "w_out_e_bf" in dr:
        for r in range(16):
            for hcol in range(2):
                cast_jobs.append((dr["w_out"][r * 128:(r + 1) * 128, hcol * 512:(hcol + 1) * 512],
                                  dr["w_out_e_bf"][r * 128:(r + 1) * 128, hcol * 512:(hcol + 1) * 512], 512, "w_out_e_bf"))
        for r in range(8):
            for cc in range(12):
                cast_jobs.append((dr["w_in_o"][r * 128:(r + 1) * 128, cc * 516:(cc + 1) * 516],
                                  dr["w_in_o_bf"][r * 128:(r + 1) * 128, cc * 516:(cc + 1) * 516], 516, "w_in_o_bf"))
        for r in range(16):
            for hcol in range(2):
                cast_jobs.append((dr["w_out_o"][r * 128:(r + 1) * 128, hcol * 512:(hcol + 1) * 512],
                                  dr["w_out_o_bf"][r * 128:(r + 1) * 128, hcol * 512:(hcol + 1) * 512], 512, "w_out_o_bf"))
    cast_state = [0]

    def emit_cast(nj):
        for _ in range(nj):
            i = cast_state[0]
            if i >= len(cast_jobs):
                return
            cast_state[0] += 1
            src, dst, w_, nm = cast_jobs[i]
            f_, b_ = cst_f[i % 2], cst_b[i % 2]
            P.add("sp", lambda e, f_=f_, src=src, w_=w_: e.dma_start(out=f_[:, 0:w_], in_=src), w=[("cst_f", i % 2)], dma=True)
            P.add("pool", lambda e, f_=f_, b_=b_, w_=w_: e.tensor_copy(b_[:, 0:w_], f_[:, 0:w_]), r=[("cst_f", i % 2)], w=[("cst_b", i % 2)])
            P.add("sp", lambda e, b_=b_, dst=dst, w_=w_: e.dma_start(out=dst, in_=b_[:, 0:w_]), r=[("cst_b", i % 2)], w=[nm], dma=True)

    scale = 96.0 ** -0.5
    kro_keys = [("kro", j) for j in range(NT2)]
    kvn_keys = [(("kvn", j), k) for j in range(NT2) for k in range(2)]
    for g in range(4):
        for hl in range(4):
            h = 4 * g + hl
            for j in range(NT2):
                ps = PS[j % 2][0:64, :]
                pk = ("PS", j % 2)
                for c in range(2):
                    P.add("pe", lambda e, ps=ps, c=c, h=h, j=j: e.matmul(
                        ps, wukv[:, c, h * 128:h * 128 + 64], kv_n[:, c, 512 * j:512 * j + 512],
                        start=(c == 0), stop=(c == 1)), r=["wukv", (("kvn", j), c)], w=[pk])
                eng = "dve" if j % 2 == 0 else "act"
                if eng == "dve":
                    P.add("dve", lambda e, ps=ps, hl=hl, j=j: e.tensor_copy(KT[0:64, hl, 512 * j:512 * j + 512], ps),
                          r=[pk], w=[("KT", hl, j)])
                else:
                    P.add("act", lambda e, ps=ps, hl=hl, j=j: e.activation(KT[0:64, hl, 512 * j:512 * j + 512], ps, AF.Copy),
                          r=[pk], w=[("KT", hl, j)])
            P.add("dve", lambda e, hl=hl: e.tensor_copy(KT[64:96, hl, :], kro[64:96, :]), r=kro_keys, w=[("KTr", hl)])
        wv = wukv.rearrange("p c (h d) -> p c h d", h=16)
        for kt in range(NKT):
            ps = PS[2 + kt % 2][:, 0:256]
            pk = ("PS", 2 + kt % 2)
            for c in range(2):
                P.add("pe", lambda e, ps=ps, c=c, kt=kt, g=g: e.matmul(
                    ps.rearrange("p (h d) -> p h d", h=4), kv_n[:, c, 128 * kt:128 * kt + 128],
                    wv[:, c, 4 * g:4 * g + 4, 64:128], start=(c == 0), stop=(c == 1)),
                    r=["wukv", (("kvn", kt // 4), c)], w=[pk])
            eng = "dve" if kt % 2 == 0 else "act"
            if eng == "dve":
                P.add("dve", lambda e, ps=ps, kt=kt: e.tensor_copy(Vg[:, kt, :, 0:64], ps.rearrange("p (h d) -> p h d", h=4)),
                      r=[pk], w=[("Vg", kt)])
            else:
                P.add("act", lambda e, ps=ps, kt=kt: e.activation(Vg[:, kt, :, 0:64], ps.rearrange("p (h d) -> p h d", h=4), AF.Copy),
                      r=[pk], w=[("Vg", kt)])
        def emit_loads(it_, g_=g):
            qt_ = it_ % NQT
            q0_ = QT * qt_
            P.add("sp", lambda e, q0_=q0_, it_=it_: e.dma_start(
                out=qn_q[it_ % 2], in_=dr["qn"].rearrange("(c p) t -> p c t", p=128)[:, :, q0_:q0_ + QT]),
                r=[("qn_d", qt_)], w=[("qn_q", it_ % 2)], dma=True)
            P.add("sp", lambda e, q0_=q0_, it_=it_, g_=g_: e.dma_start(
                out=gb_q[it_ % 2], in_=dr["gb"].rearrange("(h d) t -> d h t", d=64)[:, 4 * g_:4 * g_ + 4, q0_:q0_ + QT]),
                r=[("gb_d", qt_)], w=[("gb_q", it_ % 2)], dma=True)
            P.add("sp", lambda e, q0_=q0_, it_=it_: e.dma_start(out=csq[it_ % 2], in_=dr["rope"][0, :, q0_:q0_ + QT]),
                  w=[("csq", it_ % 2)], dma=True)
            P.add("sp", lambda e, q0_=q0_, it_=it_: e.dma_start(out=snq[it_ % 2], in_=dr["rope"][1, :, q0_:q0_ + QT]),
                  w=[("snq", it_ % 2)], dma=True)

        def emit_qproj(it_, hl_, g_=g):
            h_ = 4 * g_ + hl_
            hi_ = it_ * 4 + hl_
            qq_, cq_, sq__ = qn_q[it_ % 2], csq[it_ % 2], snq[it_ % 2]
            qa = PS[0][0:96, :]
            qb = PS[1][0:96, :]
            for (ps, base, nm) in ((qa, 0, ("PS", 0)), (qb, 1536, ("PS", 1))):
                for c in range(3):
                    P.add("pe", lambda e, ps=ps, base=base, c=c, h_=h_, qq_=qq_: e.matmul(
                        ps, wuq[:, c, base + h_ * 96:base + (h_ + 1) * 96], qq_[:, c, :],
                        start=(c == 0), stop=(c == 2)), r=["wuq", "wuqr", ("qn_q", it_ % 2)], w=[nm])
            Qb_ = QTb[hi_ % 2]
            qk_ = ("QT", hi_ % 2)
            P.add("act", lambda e, Qb_=Qb_: e.activation(Qb_[0:64, :], qa[0:64, :], AF.Copy), r=[("PS", 0)], w=[qk_])
            P.add("dve", lambda e, cq_=cq_: e.tensor_tensor(t1, qa[64:96, :], cq_, ALU.mult),
                  r=[("PS", 0), ("csq", it_ % 2)], w=["t1"])
            P.add("dve", lambda e, sq__=sq__: e.tensor_tensor(t2, qb[64:96, :], sq__, ALU.mult),
                  r=[("PS", 1), ("snq", it_ % 2)], w=["t2"])
            P.add("dve", lambda e, Qb_=Qb_: e.tensor_tensor(Qb_[64:96, :], t1, t2, ALU.add), r=["t1", "t2"], w=[qk_])

        def emit_epilogue(it_, hl_, g_=g):
            h_ = 4 * g_ + hl_
            hi_ = it_ * 4 + hl_
            q0_ = QT * (it_ % NQT)
            o_ps_ = PS[2 + hi_ % 2][0:65, :]
            ok_ = ("PS", 2 + hi_ % 2)
            gq_ = gb_q[it_ % 2]
            P.add("dve", lambda e, o_ps_=o_ps_: e.reciprocal(rc, o_ps_[64:65, :]), r=[ok_], w=["rc"])
            P.add("pe", lambda e: e.matmul(PS[7][0:64, :], ones_f[64:65, 0:64], rc, start=True, stop=True),
                  r=["rc", "ones_f"], w=[("PS", 7)])
            P.add("act", lambda e: e.activation(rb, PS[7][0:64, :], AF.Copy), r=[("PS", 7)], w=["rb"])
            P.add("dve", lambda e, o_ps_=o_ps_: e.tensor_tensor(of, o_ps_[0:64, :], rb, ALU.mult), r=[ok_, "rb"], w=["of"])
            obt = ob_t[hi_ % 2]
            P.add("dve", lambda e, obt=obt, gq_=gq_, hl_=hl_: e.scalar_tensor_tensor(
                obt, of, 0.5, gq_[:, hl_, :], ALU.mult, ALU.mult), r=["of", ("gb_q", it_ % 2)], w=[("ob_t", hi_ % 2)])
            P.add("pool", lambda e, obt=obt, h_=h_, q0_=q0_: e.dma_start(out=dr["ob"][h_, :, q0_:q0_ + QT], in_=obt),
                  r=[("ob_t", hi_ % 2)], w=[("ob_d", it_ % NQT)], dma=True)

        items = [(g * NQT + qt, hl) for qt in range(NQT) for hl in range(4)]
        emit_loads(items[0][0])
        emit_qproj(*items[0])
        pending_epi = None
        for n, (it, hl) in enumerate(items):
            hi = it * 4 + hl
            Qb = QTb[hi % 2]
            qk = ("QT", hi % 2)
            o_ps = PS[2 + hi % 2][0:65, :]
            ok = ("PS", 2 + hi % 2)
            def pv(kt, hl=hl, o_ps=o_ps, ok=ok):
                pt = pT[kt % 4]
                P.add("pe", lambda e, kt=kt, pt=pt: e.matmul(
                    o_ps, Vg[:, kt, hl, 0:65], pt, start=(kt == 0), stop=(kt == NKT - 1)),
                    r=[("Vg", kt), "Vones", ("pT", kt % 4)], w=[ok])
            for kt in range(NKT):
                s_ps = PS[4 + kt % 3]
                sk = ("PS", 4 + kt % 3)
                pt = pT[kt % 4]
                ptk = ("pT", kt % 4)
                P.add("pe", lambda e, s_ps=s_ps, hl=hl, kt=kt, Qb=Qb: e.matmul(
                    s_ps, KT[0:96, hl, 128 * kt:128 * kt + 128], Qb[0:96, :], start=True, stop=True),
                    r=[("KT", hl, kt // 4), ("KTr", hl), qk], w=[sk])
                P.add("act", lambda e, s_ps=s_ps, pt=pt: e.activation(pt, s_ps, AF.Exp, scale=scale), r=[sk], w=[ptk])
                if kt >= 2:
                    pv(kt - 2)
                if kt == min(3, NKT - 1):
                    if pending_epi is not None:
                        emit_epilogue(*pending_epi)
                        pending_epi = None
                    if hl == 0 and n + 4 < len(items):
                        emit_loads(items[n + 4][0])
                if kt == min(8, NKT - 1) and n + 1 < len(items):
                    emit_qproj(*items[n + 1])
                if kt == min(20, NKT - 1):
                    emit_cast(2)
            pv(NKT - 2)
            pv(NKT - 1)
            if n + 1 < len(items):
                pending_epi = (it, hl)
            else:
                emit_epilogue(it, hl)
    emit_cast(len(cast_jobs))
    P.barrier()

    B = Bump(A, small_end)
    wob = B.get([64, 16, 1024], BF16)
    woa = B.get([128, 8, 1024], BF16)
    stg4 = [B.get([128, 1024], F32) for _ in range(2)]
    ob_i = [B.get([64, 16, 512], BF16) for _ in range(2)]
    oa_i = [B.get([128, 8, 512], BF16) for _ in range(2)]
    x_i = [B.get([128, 8, 512], F32) for _ in range(2)]
    x_o = [B.get([128, 8, 512], F32) for _ in range(2)]
    assert B.off <= A.nbytes, B.off
    wo = dr["w_out"]
    if "w_out_e_bf" in dr:
        wob16 = dr["w_out_e_bf"]
        P.add("sp", lambda e: e.dma_start(out=wob, in_=wob16[0:1024, :].rearrange("(h d) o -> d h o", d=64)),
              r=["w_out_e_bf"], w=["wob"], dma=True)
        P.add("sp", lambda e: e.dma_start(out=woa, in_=wob16[1024:2048, :].rearrange("(k p) o -> p k o", p=128)),
              r=["w_out_e_bf"], w=["woa"], dma=True)
    else:
        for h in range(16):
            s = stg4[h % 2]
            P.add("sp", lambda e, s=s, h=h: e.dma_start(out=s[0:64, :], in_=wo[h * 64:(h + 1) * 64, :]),
                  w=[("stg4", h % 2)], dma=True)
            P.add(["dve", "pool"][h % 2], lambda e, s=s, h=h: e.tensor_copy(wob[:, h, :], s[0:64, :]),
                  r=[("stg4", h % 2)], w=["wob"])
        for k in range(8):
            s = stg4[k % 2]
            P.add("sp", lambda e, s=s, k=k: e.dma_start(out=s, in_=wo[1024 + k * 128:1024 + (k + 1) * 128, :]),
                  w=[("stg4", k % 2)], dma=True)
            P.add(["dve", "pool"][k % 2], lambda e, s=s, k=k: e.tensor_copy(woa[:, k, :], s), r=[("stg4", k % 2)], w=["woa"])
    x1T3 = dr["x1T"].rearrange("(k p) t -> p k t", p=128)
    X1OFF = dr.get("x1off", 0)
    for j in range(NQT):
        t0 = 512 * j
        ob_b, oa_b, xi, xo = ob_i[j % 2], oa_i[j % 2], x_i[j % 2], x_o[j % 2]
        P.add("sp", lambda e, ob_b=ob_b, t0=t0: e.dma_start(out=ob_b, in_=dr["ob"].rearrange("h d t -> d h t")[:, :, t0:t0 + 512]),
              r=[("ob_d", j)], w=[("ob_i", j % 2)], dma=True)
        P.add("sp", lambda e, oa_b=oa_b, t0=t0: e.dma_start(
            out=oa_b, in_=dr["oa"].rearrange("(c p) t -> p c t", p=128)[:, :, t0:t0 + 512]),
            r=[("oa_d", j)], w=[("oa_i", j % 2)], dma=True)
        P.add("sp", lambda e, xi=xi, t0=t0: e.dma_start(out=xi, in_=xT3[:, :, 16 + t0:16 + t0 + 512]),
              w=[("x_i", j % 2)], dma=True)
        for oc in range(8):
            ps = PS[oc % 4]
            pk = ("PS", oc % 4)
            for h in range(16):
                P.add("pe", lambda e, ps=ps, oc=oc, h=h, ob_b=ob_b: e.matmul(
                    ps, wob[:, h, oc * 128:(oc + 1) * 128], ob_b[:, h, :], start=(h == 0), stop=False),
                    r=["wob", ("ob_i", j % 2)], w=[pk])
            for k in range(8):
                P.add("pe", lambda e, ps=ps, oc=oc, k=k, oa_b=oa_b: e.matmul(
                    ps, woa[:, k, oc * 128:(oc + 1) * 128], oa_b[:, k, :], start=False, stop=(k == 7)),
                    r=["woa", ("oa_i", j % 2)], w=[pk])
            P.add("dve", lambda e, ps=ps, oc=oc, xi=xi, xo=xo: e.tensor_tensor(xo[:, oc, :], ps, xi[:, oc, :], ALU.add),
                  r=[pk, ("x_i", j % 2)], w=[("x_o", j % 2)])
        P.add("pool", lambda e, xo=xo, t0=t0: e.dma_start(out=x1T3[:, :, X1OFF + t0:X1OFF + t0 + 512], in_=xo),
              r=[("x_o", j % 2)], w=[("x1T", j)], dma=True)
    return [("x1T", j) for j in range(NQT)]


O_IN = 6192
XBC0, DT0, GB0, GC0, HD0, ZD0 = 1536, 4096, 4144, 4656, 5168, 5680
WP0 = 1536
NWP = O_IN - WP0
NH, NG, HPG, HP, NS = 24, 4, 6, 64, 128
SW = 1536
VO_NO, VO_CW, VO_CB, VO_DW, VO_FN, VO_M0, VO_M1 = 0, 8, 108, 128, 140, 148, 149
NV_O = 150
RO_DTB, RO_AL, RO_D, RO_SN = 0, 48, 96, 120
NR_O = 120 + 1536
NCONST = 5 * 128


def bc_last(ap2, n):
    return ap2.unsqueeze(2).broadcast_to([ap2.shape[0], ap2.shape[1], n])


class L1:
    def __init__(self, nc, P, A, PS, S, dr, base=0):
        self.nc, self.P, self.A, self.PS, self.S, self.dr = nc, P, A, PS, S, dr
        self.T = S // 2
        B = Bump(A, base)
        self.vecs = B.get([128, NV_O], F32)
        self.rows = B.get([128, NR_O], F32)
        self.cst = B.get([128, 5, 128], F32)
        self.ident_bf = B.get([128, 128], BF16)
        self.Mk_bf = B.get([128, 2, 128], BF16)
        self.ones_bf = B.get([128, 128], BF16)
        self.ones_f = B.get([128, 128], F32)
        self.epsc = B.get([128, 1], F32)
        self.onec = B.get([128, 1], F32)
        self.a_row = B.get([128, 48], F32)
        self.Sst = B.get([128, SW], F32)
        self.S_bf = B.get([128, SW], BF16)
        self.dA = B.get([128, 24], F32)
        self.acs = B.get([128, 24], F32)
        self.nacs = B.get([128, 24], F32)
        self.eacs = B.get([128, 24], F32)
        self.dec = B.get([128, 24], F32)
        self.etot = B.get([128, 24], F32)
        self.X = B.get([128, NH, HP], BF16)
        self.Xd = B.get([128, NH, HP], BF16)
        self.dAb = B.get([128, NH, 128], F32)
        self.L2 = [B.get([128, 3, 128], F32) for _ in range(2)]
        self.MT = [B.get([128, 3, 128], BF16) for _ in range(2)]
        self.base_end = B.off
        P, dr = self.P, self.dr
        P.add("sp", lambda e: e.dma_start(out=self.vecs, in_=dr["vecs_o"]), w=["vecs_o"], dma=True)
        P.add("sp", lambda e: e.dma_start(out=self.rows, in_=dr["rows_o"]), w=["rows_o"], dma=True)
        P.add("sp", lambda e: e.dma_start(out=self.cst, in_=dr["consts"].rearrange("p (a b) -> p a b", a=5)),
              w=["cst"], dma=True)
        P.add("dve", lambda e: e.tensor_copy(self.ident_bf, self.cst[:, 4, :]), r=["cst"], w=["ident_bf"])
        P.add("dve", lambda e: e.tensor_copy(self.Mk_bf, self.cst[:, 2:4, :]), r=["cst"], w=["Mk_bf"])
        P.add("dve", lambda e: e.memset(self.ones_bf, 1.0), w=["ones1"])
        P.add("dve", lambda e: e.memset(self.ones_f, 1.0), w=["ones_f1"])
        P.add("dve", lambda e: e.memset(self.epsc, EPS), w=["epsc1"])
        P.add("dve", lambda e: e.memset(self.onec, 1.0), w=["onec"])
        P.add("act", lambda e: e.activation(self.a_row, self.rows[:, RO_AL:RO_AL + 48], AF.Exp), r=["rows_o"], w=["a_row"])
        P.add("dve", lambda e: e.tensor_scalar(self.a_row, self.a_row, -1.0, None, ALU.mult), r=["a_row"], w=["a_row"])
        P.add("dve", lambda e: e.memset(self.Sst, 0.0), w=[("S", g) for g in range(4)])
        P.add("dve", lambda e: e.memset(self.S_bf, 0.0), w=[("S_bf", g) for g in range(4)])

    def ssd_chunk(self, pid, xs, Bt, BT, CT, dt, y, keys, y1=None, add_skip=False):
        P, PS = self.P, self.PS
        Uc = self.cst[:, pid, :]
        Mk = self.cst[:, 2 + pid, :]
        a_row = self.a_row[:, 24 * pid:24 * pid + 24]
        dA, acs, eacs, dec, etot, X, Xd = self.dA, self.acs, self.eacs, self.dec, self.etot, self.X, self.Xd
        acs_ps = PS[0][:, 0:24]
        tot_ps = PS[0][:, 32:56]
        P.add("dve", lambda e: e.tensor_tensor(dA, dt, a_row, ALU.mult), r=keys["dt"] + ["a_row"], w=["dA"])
        P.add("act", lambda e: e.activation(self.dAb, bc_last(dA, 128), AF.Copy), r=["dA"], w=["dAb"])
        P.add("pe", lambda e: e.matmul(acs_ps, Uc, dA, start=True, stop=True), r=["cst", "dA"], w=[("PS", 0)])
        P.add("pe", lambda e: e.matmul(tot_ps, self.ones_f, dA, start=True, stop=True), r=["ones_f1", "dA"], w=[("PS", 0)])
        P.add("act", lambda e: e.activation(acs, acs_ps, AF.Copy), r=[("PS", 0)], w=["acs"])
        P.add("act", lambda e: e.mul(self.nacs, acs_ps, -1.0), r=[("PS", 0)], w=["nacs"])
        P.add("act", lambda e: e.activation(eacs, acs_ps, AF.Exp), r=[("PS", 0)], w=["eacs"])
        P.add("act", lambda e: e.activation(etot, tot_ps, AF.Exp), r=[("PS", 0)], w=["etot"])
        P.add("dve", lambda e: e.tensor_tensor(dec, tot_ps, acs, ALU.subtract), r=[("PS", 0), "acs"], w=["dec"])
        P.add("act", lambda e: e.activation(dec, dec, AF.Exp), r=["dec"], w=["dec"])
        xs3 = xs.rearrange("p (h d) -> p h d", h=NH)
        P.add("dve", lambda e: e.tensor_tensor(X, xs3, bc_last(dt, HP), ALU.mult), r=keys["xs"] + keys["dt"], w=["X"])
        P.add("dve", lambda e: e.tensor_tensor(Xd, X, bc_last(dec, HP), ALU.mult), r=["X", "dec"], w=["Xd"])
        cb = PS[1].rearrange("p (g l) -> p g l", g=NG)
        for g in range(NG):
            P.add("pe", lambda e, g=g: e.matmul(cb[:, g, :], BT[:, g, :], CT[:, g, :], start=True, stop=True),
                  r=keys["BT"] + keys["CT"], w=[("PS", 1)])
        y3 = y.rearrange("p (h d) -> p h d", h=NH)
        S3 = self.Sst.rearrange("p (h d) -> p h d", h=NH)
        Sb3 = self.S_bf.rearrange("p (h d) -> p h d", h=NH)
        ident_f = self.cst[:, 4, :]
        Mk3 = self.Mk_bf[:, pid, :].unsqueeze(1).broadcast_to([128, 3, 128])
        yop = PS[6][:, 0:HPG * HP]
        yop3 = yop.rearrange("p (r d) -> p r d", r=HPG)
        zp = PS[0][:, 128:128 + HPG * HP]
        zp3 = zp.rearrange("p (r d) -> p r d", r=HPG)
        def stageA(hg):
            g = hg // 2
            h0 = 3 * hg
            L = self.L2[hg % 2]
            MT = self.MT[hg % 2]
            lk, mk = ("L", hg % 2), ("MT", hg % 2)
            ab = PS[2 + hg % 2].rearrange("p (r l) -> p r l", r=4)[:, 0:3, :]
            abk = ("PS", 2 + hg % 2)
            P.add("pe", lambda e, ab=ab: e.matmul(ab, self.ident_bf, Mk3, start=True, stop=False), r=["Mk_bf", "ident_bf"], w=[abk])
            for r in range(3):
                P.add("pe", lambda e, ab=ab, r=r, h0=h0: e.matmul(ab[:, r, :], self.dAb[:, h0 + r, :], Uc, start=False, stop=(r == 2)),
                      r=["dAb", "cst"], w=[abk])
            for r in range(3):
                P.add("act", lambda e, ab=ab, r=r, h0=h0, L=L: e.activation(L[:, r, :], ab[:, r, :], AF.Exp,
                                                                           bias=self.nacs[:, h0 + r:h0 + r + 1]),
                      r=[abk, "nacs"], w=[lk])
            cbg = cb[:, g, :].unsqueeze(1).broadcast_to([128, 3, 128])
            P.add("dve", lambda e, MT=MT, cbg=cbg, L=L: e.tensor_tensor(MT, cbg, L, ALU.mult), r=[("PS", 1), lk], w=[mk])

        def stageB(hg):
            g, half = hg // 2, hg % 2
            h0 = 3 * hg
            MT = self.MT[hg % 2]
            mk = ("MT", hg % 2)
            if half == 0:
                P.add("pe", lambda e, g=g: e.matmul(yop, CT[:, g, :], self.S_bf[:, g * 384:(g + 1) * 384], start=True, stop=True),
                      r=keys["CT"] + [("S_bf", g)], w=[("PS", 6)])
            ydp = PS[4 + hg % 2][:, 0:3 * HP].rearrange("p (r d) -> p r d", r=3)
            ydk = ("PS", 4 + hg % 2)
            for r in range(3):
                P.add("pe", lambda e, r=r, h0=h0, MT=MT, ydp=ydp: e.matmul(ydp[:, r, :], MT[:, r, :], X[:, h0 + r, :], start=True, stop=True),
                      r=[mk, "X"], w=[ydk])
            yg = y3[:, h0:h0 + 3, :]
            P.add("dve", lambda e, yg=yg, h0=h0, half=half: e.tensor_tensor(yg, yop3[:, 3 * half:3 * half + 3, :],
                                                                           bc_last(eacs[:, h0:h0 + 3], HP), ALU.mult),
                  r=[("PS", 6), "eacs"], w=keys["y"])
            P.add("dve", lambda e, yg=yg, ydp=ydp: e.tensor_tensor(yg, yg, ydp, ALU.add), r=[ydk] + keys["y"], w=keys["y"])
            if half == 1:
                P.add("pe", lambda e, g=g: e.matmul(zp, Bt[:, g * 128:(g + 1) * 128], Xd.rearrange("p h d -> p (h d)")[:, g * 384:(g + 1) * 384],
                                                    start=True, stop=True), r=keys["Bt"] + ["Xd"], w=[("PS", 0)])
                Sg = S3[:, HPG * g:HPG * g + HPG, :]
                P.add("dve", lambda e, g=g, Sg=Sg: e.tensor_tensor(Sg, Sg, bc_last(etot[:, HPG * g:HPG * g + HPG], HP), ALU.mult),
                      r=["etot", ("S", g)], w=[("S", g)])
                P.add("dve", lambda e, Sg=Sg: e.tensor_tensor(Sg, Sg, zp3, ALU.add), r=[("PS", 0), ("S", g)], w=[("S", g)])
                P.add("act", lambda e, g=g: e.activation(Sb3[:, HPG * g:HPG * g + HPG, :], S3[:, HPG * g:HPG * g + HPG, :], AF.Copy),
                      r=[("S", g)], w=[("S_bf", g)])

        stageA(0)
        for hg in range(2 * NG):
            if hg + 1 < 2 * NG:
                stageA(hg + 1)
            stageB(hg)
        if add_skip:
            P.add("dve", lambda e: e.tensor_tensor(Xd, xs3, bc_last(self.rows[:, RO_D:RO_D + 24], HP), ALU.mult),
                  r=keys["xs"] + ["rows_o"], w=["Xd"])
            P.add("dve", lambda e: e.tensor_tensor(y3, y3, Xd, ALU.add), r=["Xd"] + keys["y"], w=keys["y"])
        if y1 is not None:
            P.add("dve", lambda e: e.tensor_tensor(y, y, y1, ALU.add), r=keys["y1"] + keys["y"], w=keys["y"])

    def prep_and_pass1(self):
        nc, P, A, PS, dr, T = self.nc, self.P, self.A, self.PS, self.dr, self.T
        vecs, rows = self.vecs, self.rows
        TT, TW = 256, 260
        NTT = T // TT
        B = Bump(A, self.base_end)
        wbf = B.get([128, 8, NWP], BF16)
        stg = [B.get([128, NWP], F32) for _ in range(1)]
        w3 = dr["w_in_o"].rearrange("(k p) c -> p k c", p=128)
        if "w_in_o_bf" in dr:
            w3b = dr["w_in_o_bf"].rearrange("(k p) c -> p k c", p=128)
            for k in range(8):
                P.add("sp", lambda e, k=k: e.dma_start(out=wbf[:, k, :], in_=w3b[:, k, WP0:O_IN]), r=["w_in_o_bf"],
                      w=[("wbf", k), ("wbfb", k)], dma=True)
        else:
            for k in range(8):
                s = stg[0]
                P.add("sp", lambda e, s=s, k=k: e.dma_start(out=s, in_=w3[:, k, WP0:O_IN]), w=["stg1"], dma=True)
                half = NWP // 2
                P.add("dve", lambda e, s=s, k=k: e.tensor_copy(wbf[:, k, 0:half], s[:, 0:half]), r=["stg1"], w=[("wbf", k)])
                P.add("pool", lambda e, s=s, k=k: e.tensor_copy(wbf[:, k, half:NWP], s[:, half:NWP]), r=["stg1"], w=[("wbfb", k)])
        P.barrier()
        B = Bump(A, B.off - (NWP * 4 + 31) // 32 * 32)
        xb_ = B.get([128, 8, TW], F32)
        sq_ = B.get([128, 8, TW], BF16)
        xn_ = B.get([128, 8, TW], BF16)
        rstd_ = B.get([128, TW], F32)
        dtr = B.get([128, 48], F32)
        dt_t = B.get([128, 2, 48], F32)
        pre = [B.get([128, TW], BF16) for _ in range(2)]
        dgb = [B.get([128, 5, 128], BF16) for _ in range(2)]
        hb = B.get([128, 20], F32)
        acc = [B.get([128, TT], F32)]
        acch = B.get([128, TT], F32)
        acch2 = [acch, B.get([128, TT], F32)]
        th2 = [B.get([128, TT], F32) for _ in range(2)]
        th = B.get([128, TW], F32)
        so = [B.get([128, TT], F32) for _ in range(2)]
        BT_t = B.get([128, 4, TT], BF16)
        CT_t = B.get([128, 4, TT], BF16)
        xs_tok = B.get([128, 2, SW], F32)
        Bt_tok = B.get([128, 2, 512], BF16)
        od_t = B.get([128, 4, TT], BF16)
        gcs = B.get([128, TW], F32)
        v = B.get([128, TW], F32)
        zz = B.get([128, TT], F32)
        yb = [B.get([128, SW], F32) for _ in range(2)]
        assert B.off <= A.nbytes, B.off
        x1T3 = dr["x1T"].rearrange("(k p) t -> p k t", p=128)
        wk = [("wbf", k) for k in range(8)] + [("wbfb", k) for k in range(8)]
        P.add("dve", lambda e: e.tensor_scalar(hb, vecs[:, VO_CB:VO_CB + 20], 0.5, None, ALU.mult), r=["vecs_o"], w=["hb"])
        for c in range(20):
            d = dgb[c % 2]
            for i in range(5):
                P.add("dve", lambda e, d=d, c=c, i=i: e.tensor_scalar(d[:, i, :], self.ident_bf, vecs[:, VO_CW + c * 5 + i:VO_CW + c * 5 + i + 1],
                                                                   None, ALU.mult), r=["ident_bf", "vecs_o"], w=[("dgb", c % 2)])
            P.add("sp", lambda e, d=d, c=c: e.dma_start(out=dr["dgc"][c].rearrange("p (i m) -> p i m", i=5), in_=d),
                  r=[("dgb", c % 2)], w=[("dgc", c)], dma=True)
        ident_f = self.cst[:, 4, :]
        for j in range(NTT):
            t0 = TT * j
            P.add("sp", lambda e, t0=t0: e.dma_start(out=xb_, in_=x1T3[:, :, t0:t0 + TW]),
                  r=[("x1T", max(t0 - 2, 0) // 512), ("x1T", min(t0 + 257, T - 1) // 512)] + (["x1halo"] if j == NTT - 1 else []) + ["x1zero"],
                  w=[("xb1", k) for k in range(8)], dma=True)
            rmsnorm_tile(P, "l1", xb_, 8, TW, lambda k: vecs[:, VO_NO + k:VO_NO + k + 1], self.ones_bf, sq_, PS[7][:, 0:TW],
                         rstd_, xn_, 1.0 / D, [("xb1", k) for k in range(8)], "xn1", ssk=("PS", 7), epsc=self.epsc)
            xnk = [("xn1", k) for k in range(8)]
            P.add("pool", lambda e, t0=t0: e.dma_start(
                out=dr["xnT"].rearrange("(k p) t -> p k t", p=128)[:, :, t0:t0 + TT], in_=xn_[:, :, 2:2 + TT]),
                r=xnk, w=[("xnT", j)], dma=True)
            for jj in range(2):
                dtp = PS[0][:, 64:112]
                for k in range(8):
                    P.add("pe", lambda e, k=k, jj=jj: e.matmul(dtp, xn_[:, k, 2 + 128 * jj:2 + 128 * jj + 128],
                                                              wbf[:, k, DT0 - WP0:DT0 - WP0 + 48], start=(k == 0), stop=(k == 7)),
                          r=[("xn1", k)] + wk, w=[("PS", 0)])
                P.add("dve", lambda e: e.tensor_tensor(dtr, dtp, rows[:, RO_DTB:RO_DTB + 48], ALU.add),
                      r=[("PS", 0), "rows_o"], w=["dtr"])
                P.add("act", lambda e: e.activation(dtr, dtr, AF.Exp), r=["dtr"], w=["dtr"])
                P.add("act", lambda e, jj=jj: e.activation(dt_t[:, jj, :], dtr, AF.Ln, bias=self.onec), r=["dtr", "onec"],
                      w=[("dt_t", jj)])
                P.add("pool", lambda e, jj=jj, t0=t0: e.dma_start(out=dr["dtT"][t0 + 128 * jj:t0 + 128 * jj + 128, :], in_=dt_t[:, jj, :]),
                      r=[("dt_t", jj)], w=[("dtT", 2 * j + jj)], dma=True)
            def xbcA(c):
                pp = PS[6 + c % 2][:, 0:TW]
                cps = PS[6 + c % 2][:, 0:TT]
                pk = ("PS", 6 + c % 2)
                pr, d = pre[c % 2], dgb[c % 2]
                P.add("sp", lambda e, d=d, c=c: e.dma_start(out=d, in_=dr["dgc"][c].rearrange("p (i m) -> p i m", i=5)),
                      r=[("dgc", c)], w=[("dgb", c % 2)], dma=True)
                for k in range(8):
                    P.add("pe", lambda e, pp=pp, c=c, k=k: e.matmul(
                        pp, wbf[:, k, XBC0 - WP0 + c * 128:XBC0 - WP0 + (c + 1) * 128], xn_[:, k, :], start=(k == 0), stop=(k == 7)),
                        r=[("xn1", k)] + wk, w=[pk])
                P.add("act", lambda e, pp=pp, pr=pr: e.activation(pr, pp, AF.Copy), r=[pk], w=[("pre", c % 2)])
                for i in range(5):
                    P.add("pe", lambda e, cps=cps, d=d, pr=pr, i=i: e.matmul(cps, d[:, i, :], pr[:, i:i + TT], start=(i == 0), stop=(i == 4)),
                          r=[("pre", c % 2), ("dgb", c % 2)], w=[pk])
                P.add("act", lambda e, cps=cps, c=c: e.activation(th2[c % 2], cps, AF.Tanh, bias=hb[:, c:c + 1], scale=0.5),
                      r=[pk, "hb"], w=[("th1", c % 2)])
                P.add("act", lambda e, cps=cps, c=c: e.activation(acch2[c % 2], cps, AF.Identity, bias=hb[:, c:c + 1], scale=0.5),
                      r=[pk, "hb"], w=[("acch", c % 2)])

            def xbcB(c):
                sout = so[c % 2]
                thc, acc_h = th2[c % 2], acch2[c % 2]
                tk = [("th1", c % 2), ("acch", c % 2)]
                if c < 12:
                    P.add("dve", lambda e, sout=sout, thc=thc, acc_h=acc_h: e.scalar_tensor_tensor(sout, thc, 1.0, acc_h, ALU.add, ALU.mult),
                          r=tk, w=[("so", c % 2)])
                    tpb = PS[4 + c % 2][:, 0:256]
                    tpk = ("PS", 4 + c % 2)
                    for jj in range(2):
                        P.add("pe", lambda e, tpb=tpb, sout=sout, jj=jj: e.transpose(tpb[:, 128 * jj:128 * jj + 128], sout[:, 128 * jj:128 * jj + 128], ident_f),
                              r=[("so", c % 2), "cst"], w=[tpk])
                    P.add("act", lambda e, tpb=tpb, c=c: e.activation(xs_tok[:, :, c * 128:(c + 1) * 128], tpb.rearrange("p (j l) -> p j l", j=2), AF.Copy),
                          r=[tpk], w=[("xs_tok", 0), ("xs_tok", 1)])
                elif c < 16:
                    g = c - 12
                    P.add("dve", lambda e, g=g, thc=thc, acc_h=acc_h: e.scalar_tensor_tensor(BT_t[:, g, :], thc, 1.0, acc_h, ALU.add, ALU.mult),
                          r=tk, w=[("BT_t", g)])
                    tpb = PS[4 + c % 2].bitcast(BF16)[:, 0:256]
                    tpk = ("PS", 4 + c % 2)
                    for jj in range(2):
                        P.add("pe", lambda e, tpb=tpb, g=g, jj=jj: e.transpose(tpb[:, 128 * jj:128 * jj + 128], BT_t[:, g, 128 * jj:128 * jj + 128], self.ident_bf),
                              r=[("BT_t", g), "ident_bf"], w=[tpk])
                    P.add("act", lambda e, tpb=tpb, g=g: e.activation(Bt_tok[:, :, g * 128:(g + 1) * 128], tpb.rearrange("p (j l) -> p j l", j=2), AF.Copy),
                          r=[tpk], w=[("Bt_tok", 0), ("Bt_tok", 1)])
                else:
                    g = c - 16
                    P.add("dve", lambda e, g=g, thc=thc, acc_h=acc_h: e.scalar_tensor_tensor(CT_t[:, g, :], thc, 1.0, acc_h, ALU.add, ALU.mult),
                          r=tk, w=[("CT_t", g)])

            xbcA(0)
            for c in range(20):
                if c + 1 < 20:
                    xbcA(c + 1)
                xbcB(c)
            P.add("pool", lambda e, t0=t0: e.dma_start(out=dr["xsT"][t0:t0 + TT, :].rearrange("(j p) c -> p j c", p=128), in_=xs_tok),
                  r=[("xs_tok", 0), ("xs_tok", 1)], w=[("xsT", j)], dma=True)
            P.add("pool", lambda e, t0=t0: e.dma_start(out=dr["Bts"][t0:t0 + TT, :].rearrange("(j p) c -> p j c", p=128), in_=Bt_tok),
                  r=[("Bt_tok", 0), ("Bt_tok", 1)], w=[("Bts", j)], dma=True)
            P.add("pool", lambda e, t0=t0: e.dma_start(out=dr["BTs"].rearrange("(g p) t -> p g t", p=128)[:, :, t0:t0 + TT], in_=BT_t),
                  r=[("BT_t", g) for g in range(4)], w=[("BTs", j)], dma=True)
            P.add("pool", lambda e, t0=t0: e.dma_start(out=dr["CTs"].rearrange("(g p) t -> p g t", p=128)[:, :, t0:t0 + TT], in_=CT_t),
                  r=[("CT_t", g) for g in range(4)], w=[("CTs", j)], dma=True)
            for c in range(4):
                pgc, phd = PS[6][:, 0:TW], PS[7][:, 0:TW]
                for (pp, col0, pk) in ((pgc, GC0, ("PS", 6)), (phd, HD0, ("PS", 7))):
                    for k in range(8):
                        P.add("pe", lambda e, pp=pp, col0=col0, c=c, k=k: e.matmul(
                            pp, wbf[:, k, col0 - WP0 + c * 128:col0 - WP0 + (c + 1) * 128], xn_[:, k, :], start=(k == 0), stop=(k == 7)),
                            r=[("xn1", k)] + wk, w=[pk])
                P.add("act", lambda e: e.activation(gcs, pgc, AF.Copy), r=[("PS", 6)], w=["gcs"])
                P.add("dve", lambda e: e.tensor_tensor(v, gcs, phd, ALU.mult), r=["gcs", ("PS", 7)], w=["v"])
                wd = lambda i, c=c: vecs[:, VO_DW + c * 3 + i:VO_DW + c * 3 + i + 1]
                ac = acc[0]
                P.add("dve", lambda e, wd=wd, ac=ac: e.tensor_scalar(ac, v[:, 1:1 + TT], wd(0), None, ALU.mult),
                      r=["v", "vecs_o"], w=[("acc1", 0)])
                for i in (1, 2):
                    P.add("dve", lambda e, wd=wd, ac=ac, i=i: e.scalar_tensor_tensor(ac, v[:, 1 + i:1 + i + TT], wd(i), ac, ALU.mult, ALU.add),
                          r=["v", ("acc1", 0)], w=[("acc1", 0)])
                pgb, pzd = PS[4][:, 0:TT], PS[5][:, 0:TT]
                for (pp, col0, pk) in ((pgb, GB0, ("PS", 4)), (pzd, ZD0, ("PS", 5))):
                    for k in range(8):
                        P.add("pe", lambda e, pp=pp, col0=col0, c=c, k=k: e.matmul(
                            pp, wbf[:, k, col0 - WP0 + c * 128:col0 - WP0 + (c + 1) * 128], xn_[:, k, 2:2 + TT], start=(k == 0), stop=(k == 7)),
                            r=[("xn1", k)] + wk, w=[pk])
                P.add("dve", lambda e, ac=ac: e.tensor_tensor(ac, ac, pgb, ALU.mult), r=[("acc1", 0), ("PS", 4)], w=[("acc1", 0)])
                P.add("act", lambda e: e.activation(th[:, 0:TT], pzd, AF.Tanh, scale=0.5), r=[("PS", 5)], w=["th1"])
                P.add("dve", lambda e: e.scalar_tensor_tensor(zz, th[:, 0:TT], 1.0, pzd, ALU.add, ALU.mult), r=["th1", ("PS", 5)], w=["zz"])
                P.add("dve", lambda e, c=c, ac=ac: e.scalar_tensor_tensor(od_t[:, c, :], ac, 0.5, zz, ALU.mult, ALU.mult),
                      r=[("acc1", 0), "zz"], w=[("od_t", c)])
            P.add("pool", lambda e, t0=t0: e.dma_start(out=dr["odT"].rearrange("(c p) t -> p c t", p=128)[:, :, t0:t0 + TT], in_=od_t),
                  r=[("od_t", c) for c in range(4)], w=[("odT", j // 2)], dma=True)
            for jj in range(2):
                y = yb[jj]
                keys = {"xs": [("xs_tok", jj)], "Bt": [("Bt_tok", jj)], "BT": [("BT_t", g) for g in range(4)],
                        "CT": [("CT_t", g) for g in range(4)], "dt": [("dt_t", jj)], "y": [("yb", jj)]}
                self._BTk = [("BT_t", g) for g in range(4)]
                self._CTk = [("CT_t", g) for g in range(4)]
                self.ssd_chunk(0, xs_tok[:, jj, :], Bt_tok[:, jj, :], BT_t[:, :, 128 * jj:128 * jj + 128],
                               CT_t[:, :, 128 * jj:128 * jj + 128], dt_t[:, jj, 0:24], y, keys, add_skip=True)
                P.add("pool", lambda e, y=y, t0=t0, jj=jj: e.dma_start(out=dr["y1"][t0 + 128 * jj:t0 + 128 * jj + 128, :], in_=y),
                      r=[("yb", jj)], w=[("y1", 2 * j + jj)], dma=True)
        self.prep_end = B.off

    def exchange_states(self):
        P, dr, A = self.P, self.dr, self.A
        pairs = getattr(self, 'pairs', [[0, 1], [2, 3], [4, 5], [6, 7]])
        P.add("sp", lambda e: e.dma_start(out=dr["cc_in2"], in_=self.Sst), r=[("S", g) for g in range(4)], w=["cc_in2"], dma=True)
        P.add("pool", lambda e: e.collective_compute("AllGather", ALU.bypass, replica_groups=pairs,
                                                     ins=[dr["cc_in2"]], outs=[dr["cc_out2"]]),
              r=["cc_in2"], w=["cc_out2"], dma=True, inc=1)
        P.barrier()
        B = Bump(A, self.base_end)
        g2 = B.get([128, 2, SW], F32)
        P.add("sp", lambda e: e.dma_start(out=g2, in_=dr["cc_out2"].rearrange("(r p) c -> p r c", r=2)), r=["cc_out2"], w=["g2"], dma=True)
        m0 = self.vecs[:, VO_M0:VO_M0 + 1]
        m1 = self.vecs[:, VO_M1:VO_M1 + 1]
        P.add("dve", lambda e: e.tensor_scalar(g2[:, 0, :], g2[:, 0, :], m1, None, ALU.mult), r=["g2", "vecs_o"], w=["g2"])
        P.add("dve", lambda e: e.scalar_tensor_tensor(self.Sst, g2[:, 1, :], m0, g2[:, 0, :], ALU.mult, ALU.add),
              r=["g2"], w=[("S", g) for g in range(4)])
        P.add("act", lambda e: e.activation(self.S_bf, self.Sst, AF.Copy), r=[("S", g) for g in range(4)],
              w=[("S_bf", g) for g in range(4)])
        P.barrier()

    def pass2_and_out(self):
        nc, P, A, PS, dr, T = self.nc, self.P, self.A, self.PS, self.dr, self.T
        vecs, rows = self.vecs, self.rows
        NC = T // 128
        B = Bump(A, self.base_end)
        wz = B.get([128, 8, SW], BF16)
        wo = B.get([128, 16, 1024], BF16)
        yz = B.get([128, SW], F32)
        stg = yz
        w3 = dr["w_in_o"].rearrange("(k p) c -> p k c", p=128)
        wo3 = dr["w_out_o"].rearrange("(k p) c -> p k c", p=128)
        if "w_in_o_bf" in dr:
            P.add("sp", lambda e: e.dma_start(out=wz, in_=dr["w_in_o_bf"].rearrange("(k p) c -> p k c", p=128)[:, :, 0:SW]),
                  r=["w_in_o_bf"], w=["wz"], dma=True)
            P.add("sp", lambda e: e.dma_start(out=wo, in_=dr["w_out_o_bf"].rearrange("(k p) c -> p k c", p=128)),
                  r=["w_out_o_bf"], w=["wo"], dma=True)
        else:
            for k in range(8):
                P.add("sp", lambda e, k=k: e.dma_start(out=stg, in_=w3[:, k, 0:SW]), w=["stg2"], dma=True)
                P.add("dve", lambda e, k=k: e.tensor_copy(wz[:, k, :], stg), r=["stg2"], w=["wz"])
            for k in range(16):
                P.add("sp", lambda e, k=k: e.dma_start(out=stg[:, 0:1024], in_=wo3[:, k, :]), w=["stg2"], dma=True)
                P.add("pool", lambda e, k=k: e.tensor_copy(wo[:, k, :], stg[:, 0:1024]), r=["stg2"], w=["wo"])
        P.barrier()
        xs_i = [B.get([128, SW], F32)] * 2
        y1_i = [B.get([128, SW], F32)] * 2
        Bt_i = [B.get([128, 512], BF16) for _ in range(2)]
        BT_i = [B.get([128, 4, 128], BF16) for _ in range(2)]
        CT_i = [B.get([128, 4, 128], BF16) for _ in range(2)]
        dt_i = [B.get([128, 24], F32) for _ in range(2)]
        xn_i = [B.get([128, 8, 128], BF16) for _ in range(2)]
        y = B.get([128, SW], F32)
        th = B.get([128, 512], F32)
        thb = B.get([128, 512], F32)
        zg = B.get([128, SW], F32)
        junk = B.get([128, 4, 384], BF16)
        ssg = B.get([128, 4], F32)
        rs4 = B.get([128, 4], F32)
        oc = B.get([128, SW], BF16)
        ocT = [B.get([128, 12, 512], BF16)] * 2
        od_i = [B.get([128, 4, 512], BF16)] * 2
        xio = B.get([128, 8, 512], F32)
        x1_i = [xio, xio]
        x2 = xio
        sq2 = B.get([128, 8, 512], BF16)
        rstd2 = B.get([128, 512], F32)
        xo = [xio, xio]
        assert B.off <= A.nbytes, B.off
        x1T3 = dr["x1T"].rearrange("(k p) t -> p k t", p=128)
        outT3 = dr["outT"].rearrange("(k p) t -> p k t", p=128)
        fin = []
        for ci, c in enumerate(range(NC - 1, -1, -1)):
            b = ci % 2
            r0 = 128 * c
            grp = c // 4
            gi = (NC // 4 - 1 - grp)
            P.add("sp", lambda e, b=b, r0=r0: e.dma_start(out=xs_i[b], in_=dr["xsT"][r0:r0 + 128, :]), r=[("xsT", c // 2)], w=[("xs_i", 0)], dma=True)
            P.add("sp", lambda e, b=b, r0=r0: e.dma_start(out=y1_i[b], in_=dr["y1"][r0:r0 + 128, :]), r=[("y1", c)], w=[("y1_i", 0)], dma=True)
            P.add("sp", lambda e, b=b, r0=r0: e.dma_start(out=Bt_i[b], in_=dr["Bts"][r0:r0 + 128, :]), r=[("Bts", c // 2)], w=[("Bt_i", b)], dma=True)
            P.add("sp", lambda e, b=b, r0=r0: e.dma_start(out=BT_i[b], in_=dr["BTs"].rearrange("(g p) t -> p g t", p=128)[:, :, r0:r0 + 128]),
                  r=[("BTs", c // 2)], w=[("BT_i", b)], dma=True)
            P.add("sp", lambda e, b=b, r0=r0: e.dma_start(out=CT_i[b], in_=dr["CTs"].rearrange("(g p) t -> p g t", p=128)[:, :, r0:r0 + 128]),
                  r=[("CTs", c // 2)], w=[("CT_i", b)], dma=True)
            P.add("sp", lambda e, b=b, r0=r0: e.dma_start(out=dt_i[b], in_=dr["dtT"][r0:r0 + 128, 24:48]), r=[("dtT", c)], w=[("dt_i", b)], dma=True)
            P.add("sp", lambda e, b=b, r0=r0: e.dma_start(out=xn_i[b], in_=dr["xnT"].rearrange("(k p) t -> p k t", p=128)[:, :, r0:r0 + 128]),
                  r=[("xnT", c // 2)], w=[("xn_i", b)], dma=True)
            keys = {"xs": [("xs_i", 0)], "Bt": [("Bt_i", b)], "BT": [("BT_i", b)], "CT": [("CT_i", b)], "dt": [("dt_i", b)],
                    "y": ["y2"], "y1": [("y1_i", 0)]}
            for i in range(3):
                zp = PS[7 - i % 2]
                zk = ("PS", 7 - i % 2)
                thz = th if i % 2 == 0 else thb
                tk = "th2" if i % 2 == 0 else "th2b"
                for k in range(8):
                    P.add("pe", lambda e, b=b, k=k, i=i, zp=zp: e.matmul(zp, xn_i[b][:, k, :], wz[:, k, i * 512:(i + 1) * 512],
                                                                        start=(k == 0), stop=(k == 7)), r=[("xn_i", b), "wz"], w=[zk])
                P.add("act", lambda e, zp=zp, thz=thz: e.activation(thz, zp, AF.Tanh, scale=0.5), r=[zk], w=[tk])
                P.add("dve", lambda e, i=i, zp=zp, thz=thz: e.scalar_tensor_tensor(zg[:, i * 512:(i + 1) * 512], thz, 1.0, zp, ALU.add, ALU.mult),
                      r=[tk, zk], w=[("zg", i)])
            self.ssd_chunk(1, xs_i[b], Bt_i[b], BT_i[b], CT_i[b], dt_i[b], y, keys, y1=y1_i[b])
            for i in range(3):
                P.add("dve", lambda e, i=i: e.scalar_tensor_tensor(yz[:, i * 512:(i + 1) * 512], y[:, i * 512:(i + 1) * 512], 0.5,
                                                                   zg[:, i * 512:(i + 1) * 512], ALU.mult, ALU.mult),
                      r=["y2", ("zg", i)], w=[("yz", i)])
            yzk = [("yz", i) for i in range(3)]
            P.add("dve", lambda e: e.memset(ssg, 0.0), w=[("ssg", g) for g in range(4)])
            for g in range(4):
                P.add("act", lambda e, g=g: e.activation(junk[:, g, :], yz[:, g * 384:(g + 1) * 384], AF.Square, accum_out=ssg[:, g:g + 1]),
                      r=yzk, w=[("ssg", g), ("junk", g)])
            P.add("act", lambda e: e.activation(rs4, ssg, AF.Ln, bias=self.epsc, scale=1.0 / 384), r=[("ssg", g) for g in range(4)] + ["epsc1"], w=["rs4"])
            P.add("act", lambda e: e.activation(rs4, rs4, AF.Exp, scale=-0.5), r=["rs4"], w=["rs4"])
            for g in range(4):
                P.add("dve", lambda e, g=g: e.scalar_tensor_tensor(oc[:, g * 384:(g + 1) * 384], yz[:, g * 384:(g + 1) * 384],
                                                                   rs4[:, g:g + 1], rows[:, RO_SN + g * 384:RO_SN + (g + 1) * 384], ALU.mult, ALU.mult),
                      r=yzk + ["rs4", "rows_o"], w=[("oc", g)])
            ock = [("oc", g) for g in range(4)]
            oT = ocT[gi % 2]
            lo = 128 * (c % 4)
            for q4 in range(3):
                tpb = PS[6 + q4 % 2].bitcast(BF16)[:, 0:512]
                pk = ("PS", 6 + q4 % 2)
                for u4 in range(4):
                    cc = 4 * q4 + u4
                    P.add("pe", lambda e, tpb=tpb, cc=cc, u4=u4: e.transpose(tpb[:, 128 * u4:128 * u4 + 128], oc[:, cc * 128:(cc + 1) * 128], self.ident_bf),
                          r=ock + ["ident_bf"], w=[pk])
                dst = oT[:, 4 * q4:4 * q4 + 4, lo:lo + 128]
                src = tpb.rearrange("p (u l) -> p u l", u=4)
                if q4 % 2 == 0:
                    P.add("act", lambda e, dst=dst, src=src: e.activation(dst, src, AF.Copy), r=[pk], w=[("ocT", 0)])
                else:
                    P.add("dve", lambda e, dst=dst, src=src: e.tensor_copy(dst, src), r=[pk], w=[("ocT", 0)])
            if c % 4 == 0:
                t0 = 512 * grp
                gb = gi % 2
                P.add("sp", lambda e, gb=gb, t0=t0: e.dma_start(out=od_i[gb], in_=dr["odT"].rearrange("(c p) t -> p c t", p=128)[:, :, t0:t0 + 512]),
                      r=[("odT", grp)], w=[("od_i", 0)], dma=True)
                P.add("sp", lambda e, gb=gb, t0=t0: e.dma_start(out=x1_i[gb], in_=x1T3[:, :, 2 + t0:2 + t0 + 512]),
                      r=[("x1T", grp)], w=[("x2", o) for o in range(8)], dma=True)
                for o in range(8):
                    ps = PS[4 + o % 2]
                    pk = ("PS", 4 + o % 2)
                    for cc in range(12):
                        P.add("pe", lambda e, ps=ps, o=o, cc=cc, oT=oT: e.matmul(ps, wo[:, cc, o * 128:(o + 1) * 128], oT[:, cc, :],
                                                                                 start=(cc == 0), stop=False), r=["wo", ("ocT", 0)], w=[pk])
                    for cc in range(4):
                        P.add("pe", lambda e, ps=ps, o=o, cc=cc, gb=gb: e.matmul(ps, wo[:, 12 + cc, o * 128:(o + 1) * 128], od_i[gb][:, cc, :],
                                                                                 start=False, stop=(cc == 3)), r=["wo", ("od_i", 0)], w=[pk])
                    P.add("dve", lambda e, ps=ps, o=o, gb=gb: e.tensor_tensor(x2[:, o, :], ps, x1_i[gb][:, o, :], ALU.add),
                          r=[pk, ("x2", o)], w=[("x2", o)])
                xout = xo[gi % 2]
                rmsnorm_tile(P, "fin", x2, 8, 512, lambda k: vecs[:, VO_FN + k:VO_FN + k + 1], self.ones_bf, sq2, PS[7], rstd2, xout,
                             1.0 / D, [("x2", o) for o in range(8)], "x2", ssk=("PS", 7), epsc=self.epsc)
                P.add("pool", lambda e, xout=xout, t0=t0: e.dma_start(out=outT3[:, :, t0:t0 + 512], in_=xout),
                      r=[("x2", k) for k in range(8)], w=[("outT", grp)], dma=True)
                fin.append(("outT", grp))
        return fin

import numpy as np

ROPE_THETA = 10000.0


def chunkcol(v):
    v = np.asarray(v, np.float32)
    return np.ascontiguousarray(v.reshape(-1, 128).T)


def prep_l0(inp, b, hh, S):
    x = inp["x"][b]
    xs = x if hh == 0 else x[::-1]
    xT = np.zeros((1024, 16 + S + 16), np.float32)
    xT[:, 16:16 + S] = xs.T
    pos = np.arange(S, dtype=np.float32)
    if hh == 1:
        pos = pos[::-1]
    inv = (ROPE_THETA ** (-np.arange(16, dtype=np.float32) / 16)).astype(np.float32)
    ang = (pos[:, None] * inv[None, :]).astype(np.float32)
    cos = np.cos(ang).astype(np.float32)
    sin = np.sin(ang).astype(np.float32)
    rope = np.stack([np.concatenate([cos, cos], 1).T, np.concatenate([-sin, sin], 1).T]).astype(np.float32)
    cw = inp["conv_a_w"][0]
    if hh == 1:
        cw = cw[::-1]
    cwv = cw.T.reshape(8, 128, 31).transpose(1, 0, 2).reshape(128, 8 * 31)
    vecs = np.concatenate([chunkcol(inp["norm_e"][0]), chunkcol(inp["conv_a_b"][0]), chunkcol(inp["ln_a_g"][0]),
                           chunkcol(inp["ln_a_b"][0]), chunkcol(inp["q_norm"][0]), chunkcol(inp["kv_norm"][0]),
                           cwv], axis=1).astype(np.float32)
    return {
        "xT": xT, "rope": np.ascontiguousarray(rope), "vecs": np.ascontiguousarray(vecs),
        "w_in_e": np.ascontiguousarray(inp["w_in_e"][0]), "w_uq": np.ascontiguousarray(inp["w_uq"][0]),
        "w_ukv": np.ascontiguousarray(inp["w_ukv"][0]), "w_out_e": np.ascontiguousarray(inp["w_out_e"][0]),
    }


def l1_consts():
    k = np.arange(128)
    U1 = (k[:, None] <= k[None, :]).astype(np.float32)
    U2 = (k[:, None] >= k[None, :]).astype(np.float32)
    M1 = np.where(k[None, :] >= k[:, None], 0.0, -30000.0).astype(np.float32)
    M2 = np.where(k[None, :] <= k[:, None], 0.0, -30000.0).astype(np.float32)
    I = np.eye(128, dtype=np.float32)
    return np.ascontiguousarray(np.concatenate([U1, U2, M1, M2, I], axis=1))


def prep_l1(inp, b, hh):
    w_in = np.array(inp["w_in_o"][0], np.float32, copy=True)
    dbf, dbb = inp["dt_bias_f"][0], inp["dt_bias_b"][0]
    alf, alb = inp["a_log_f"][0], inp["a_log_b"][0]
    ccw = inp["conv_c_w"][0]
    cdw = inp["conv_d_w"][0]
    if hh == 1:
        w_in[:, 4096:4120], w_in[:, 4120:4144] = inp["w_in_o"][0][:, 4120:4144], inp["w_in_o"][0][:, 4096:4120]
        dbf, dbb = dbb, dbf
        alf, alb = alb, alf
        ccw = ccw[::-1]
        cdw = cdw[::-1]
    rows = np.concatenate([dbf, dbb, alf, alb, inp["d_skip"][0], inp["ssd_norm"][0]]).astype(np.float32)
    rows = np.ascontiguousarray(np.broadcast_to(rows[None, :], (128, rows.shape[0])))
    ccwv = ccw.T.reshape(20, 128, 5).transpose(1, 0, 2).reshape(128, 100)
    cdwv = cdw.T.reshape(4, 128, 3).transpose(1, 0, 2).reshape(128, 12)
    m = np.zeros((128, 2), np.float32)
    m[:, hh] = 1.0
    vecs = np.concatenate([chunkcol(inp["norm_o"][0]), ccwv, chunkcol(inp["conv_c_b"][0]), cdwv,
                           chunkcol(inp["final_norm"]), m], axis=1).astype(np.float32)
    return {"w_in_o": np.ascontiguousarray(w_in), "w_out_o": np.ascontiguousarray(inp["w_out_o"][0]),
            "vecs_o": np.ascontiguousarray(vecs), "rows_o": rows, "consts": l1_consts()}


def l1_drams(nc, T, dr):
    dr["w_in_o"] = nc.dram_tensor("w_in_o", [1024, O_IN], F32, kind="ExternalInput").ap()
    dr["w_out_o"] = nc.dram_tensor("w_out_o", [2048, 1024], F32, kind="ExternalInput").ap()
    dr["vecs_o"] = nc.dram_tensor("vecs_o", [128, NV_O], F32, kind="ExternalInput").ap()
    dr["rows_o"] = nc.dram_tensor("rows_o", [128, NR_O], F32, kind="ExternalInput").ap()
    if "consts" not in dr:
        dr["consts"] = nc.dram_tensor("consts", [128, NCONST], F32, kind="ExternalInput").ap()
    dr["xnT"] = nc.dram_tensor("xnT", [1024, T], BF16).ap()
    dr["dtT"] = nc.dram_tensor("dtT", [T, 48], F32).ap()
    dr["xsT"] = nc.dram_tensor("xsT", [T, 1536], F32).ap()
    dr["Bts"] = nc.dram_tensor("Bts", [T, 512], BF16).ap()
    dr["BTs"] = nc.dram_tensor("BTs", [512, T], BF16).ap()
    dr["CTs"] = nc.dram_tensor("CTs", [512, T], BF16).ap()
    dr["odT"] = nc.dram_tensor("odT", [512, T], BF16).ap()
    dr["y1"] = nc.dram_tensor("y1", [T, 1536], F32).ap()
    dr["dgc"] = nc.dram_tensor("dgc", [20, 128, 5 * 128], BF16).ap()
    dr["cc_in2"] = nc.dram_tensor("cc_in2", [128, 1536], F32).ap()
    dr["cc_out2"] = nc.dram_tensor("cc_out2", [256, 1536], F32).ap()
    dr["outT"] = nc.dram_tensor("outT", [1024, T], F32, kind="ExternalOutput").ap()


def l0_drams(nc, S, dr):
    T = S // 2
    dr["xT"] = nc.dram_tensor("xT", [1024, 16 + S + 16], F32, kind="ExternalInput").ap()
    dr["rope"] = nc.dram_tensor("rope", [2, 32, S], F32, kind="ExternalInput").ap()
    dr["vecs"] = nc.dram_tensor("vecs", [128, NV_E], F32, kind="ExternalInput").ap()
    dr["w_in"] = nc.dram_tensor("w_in_e", [1024, E_IN], F32, kind="ExternalInput").ap()
    dr["w_uq"] = nc.dram_tensor("w_uq", [384, 1536], F32, kind="ExternalInput").ap()
    dr["w_ukv"] = nc.dram_tensor("w_ukv", [256, 2048], F32, kind="ExternalInput").ap()
    dr["w_out"] = nc.dram_tensor("w_out_e", [2048, 1024], F32, kind="ExternalInput").ap()
    dr["qn"] = nc.dram_tensor("qn", [384, T], BF16).ap()
    dr["gb"] = nc.dram_tensor("gb", [1024, T], BF16).ap()
    dr["oa"] = nc.dram_tensor("oa", [1024, T], BF16).ap()
    dr["ob"] = nc.dram_tensor("ob", [16, 64, T], BF16).ap()
    dr["dg"] = nc.dram_tensor("dg", [8, 128, 31 * 128], BF16).ap()


def build_fused(S, pairs):
    T = S // 2
    nc = bass.Bass("TRN2", target_bir_lowering=False)
    dr = {}
    l0_drams(nc, S, dr)
    dr["consts"] = nc.dram_tensor("consts", [128, NCONST], F32, kind="ExternalInput").ap()
    dr["x1T"] = nc.dram_tensor("x1T", [1024, T + 4], F32).ap()
    dr["x1off"] = 2
    dr["cc_in1"] = nc.dram_tensor("cc_in1", [128, 16], F32).ap()
    dr["cc_out1"] = nc.dram_tensor("cc_out1", [256, 16], F32).ap()
    l1_drams(nc, T, dr)
    A = Arena(nc)
    PS = [nc.alloc_psum_tensor("ps%d" % i, [128, 512], F32).ap() for i in range(8)]
    P = Prog(nc)
    dr["w_out_e_bf"] = nc.dram_tensor("w_out_e_bf", [2048, 1024], BF16).ap()
    dr["w_in_o_bf"] = nc.dram_tensor("w_in_o_bf", [1024, O_IN], BF16).ap()
    dr["w_out_o_bf"] = nc.dram_tensor("w_out_o_bf", [2048, 1024], BF16).ap()
    fin0 = build_l0(nc, P, A, PS, S, dr)
    P.barrier()
    x1T3 = dr["x1T"].rearrange("(k p) t -> p k t", p=128)
    B = Bump(A, 0)
    mvec = B.get([128, 2], F32)
    zer = B.get([128, 8, 2], F32)
    hs = B.get([128, 8, 2], F32)
    g1 = B.get([128, 2, 16], F32)
    sel = B.get([128, 8, 2], F32)
    halo = B.get([128, 8, 2], F32)
    P.add("sp", lambda e: e.dma_start(out=mvec, in_=dr["vecs_o"][:, VO_M0:VO_M0 + 2]), w=["mvec"], dma=True)
    P.add("dve", lambda e: e.memset(zer, 0.0), w=["zer"])
    P.add("sp", lambda e: e.dma_start(out=x1T3[:, :, 0:2], in_=zer), r=["zer"], w=["x1zero"], dma=True)
    P.add("sp", lambda e: e.dma_start(out=hs, in_=x1T3[:, :, T:T + 2]), r=fin0, w=["hs"], dma=True)
    P.add("sp", lambda e: e.dma_start(out=dr["cc_in1"].rearrange("p (k c) -> p k c", c=2), in_=hs), r=["hs"], w=["cc_in1"], dma=True)
    P.add("pool", lambda e: e.collective_compute("AllGather", ALU.bypass, replica_groups=pairs,
                                                 ins=[dr["cc_in1"]], outs=[dr["cc_out1"]]),
          r=["cc_in1"], w=["cc_out1"], dma=True, inc=1)
    P.add("sp", lambda e: e.dma_start(out=g1, in_=dr["cc_out1"].rearrange("(r p) c -> p r c", r=2)), r=["cc_out1"], w=["g1"], dma=True)
    g1v = g1.rearrange("p r (k c) -> p r k c", c=2)
    P.add("dve", lambda e: e.tensor_scalar(sel, g1v[:, 0, :, :], mvec[:, 1:2], None, ALU.mult), r=["g1", "mvec"], w=["sel"])
    P.add("dve", lambda e: e.scalar_tensor_tensor(sel, g1v[:, 1, :, :], mvec[:, 0:1], sel, ALU.mult, ALU.add), r=["g1", "mvec", "sel"], w=["sel"])
    P.add("dve", lambda e: e.tensor_copy(halo[:, :, 0:1], sel[:, :, 1:2]), r=["sel"], w=["halo"])
    P.add("dve", lambda e: e.tensor_copy(halo[:, :, 1:2], sel[:, :, 0:1]), r=["sel", "halo"], w=["halo"])
    P.add("sp", lambda e: e.dma_start(out=x1T3[:, :, T + 2:T + 4], in_=halo), r=["halo"], w=["x1halo"], dma=True)
    P.barrier()
    L = L1(nc, P, A, PS, S, dr)
    L.pairs = pairs
    L.prep_and_pass1()
    L.exchange_states()
    fin = L.pass2_and_out()
    P.add("sp", None, r=fin)
    P.emit()
    return nc, len(P.ops)


def kernel_impl(inputs, S, B, run):
    T = S // 2
    ncores = 2 * B
    pairs = [[2 * i, 2 * i + 1] for i in range(B)]
    nc, nops = build_fused(S, pairs)
    in_maps = []
    for core in range(ncores):
        b, hh = core // 2, core % 2
        m = prep_l0(inputs, b, hh, S)
        m.update(prep_l1(inputs, b, hh))
        in_maps.append(m)
    results = run(nc, in_maps, ncores)
    out = np.empty((B, S, 1024), np.float32)
    for core in range(ncores):
        b, hh = core // 2, core % 2
        o = np.asarray(results[core]["outT"]).T
        if hh == 0:
            out[b, :T] = o
        else:
            out[b, T:] = o[::-1]
    return out


def _run_hw(nc, in_maps, n):
    return run_bass_kernel_spmd(nc, in_maps, core_ids=list(range(n))).results


def kernel(**inputs):
    inputs = {k: np.asarray(v) for k, v in inputs.items()}
    Bn, S, _ = inputs["x"].shape
    return kernel_impl(inputs, S, Bn, _run_hw)
```
